# Optimizing a Trainium2 kernel written in Bass

```python
import jax, jax.numpy as jnp
from jax import lax
import numpy as np

D_MODEL = 1024
BATCH = 2
SEQ = 8192
DEPTH = 2
DEC_BATCH = 32
DEC_SEQ = 16
PAST_LEN = 1024

CHUNK = 64
EPS = 1e-6
Q_BLOCK = 128
MLA_HEADS = 6
MLA_Q_RANK = 256
MLA_KV_RANK = 128
MLA_NOPE = 64
MLA_ROPE = 32
MLA_V = 64
ROPE_BASE = 10000.0
GLA_HEADS = 4
GLA_DK = 64
GLA_DV = 64
GLA_GATE_RANK = 16
GLA_GATE_NORM = 16.0
CA_HEADS = 6
CA_DIM = 64
CA_BAND = 8
REL_CLIP = 128
D_FF = 4 * D_MODEL

MLA_W = MLA_HEADS * MLA_V
GLA_W = GLA_HEADS * GLA_DV
CA_W = CA_HEADS * CA_DIM
IN_SPLITS = (MLA_Q_RANK, MLA_KV_RANK, MLA_ROPE,
             GLA_HEADS * GLA_DK, GLA_HEADS * GLA_DK, GLA_W, GLA_GATE_RANK, GLA_W,
             CA_W, CA_W, CA_W)
D_IN = sum(IN_SPLITS)

kernel_name = "hybrid_streaming_mla_gla_chunkband_step"

f32 = jnp.float32


def _rmsnorm(x, g):
    x32 = x.astype(f32)
    y = x32 * lax.rsqrt(jnp.mean(x32 * x32, axis=-1, keepdims=True) + EPS)
    return (y * g.astype(f32)).astype(x.dtype)


def _rope(x, pos):
    half = x.shape[-1] // 2
    inv = jnp.power(ROPE_BASE, -jnp.arange(half, dtype=f32) / half)
    ang = pos.astype(f32)[:, None] * inv[None, :]
    ang = ang.reshape((1, ang.shape[0]) + (1,) * (x.ndim - 3) + (half,))
    cos, sin = jnp.cos(ang), jnp.sin(ang)
    x1, x2 = x[..., :half].astype(f32), x[..., half:].astype(f32)
    return jnp.concatenate([x1 * cos - x2 * sin, x2 * cos + x1 * sin], axis=-1).astype(x.dtype)


def _project(h, pos, w_in, q_norm, w_qup, kv_norm, w_gate2, gate_bias):
    B, L, _ = h.shape
    z = h @ w_in
    offs = np.cumsum(IN_SPLITS)[:-1].tolist()
    q_lat, ckv, kr, gq, gk, gv, g_lr, g_out, cq, ck, cv = jnp.split(z, offs, axis=-1)
    q = (_rmsnorm(q_lat, q_norm) @ w_qup).reshape(B, L, MLA_HEADS, MLA_NOPE + MLA_ROPE)
    q = jnp.concatenate([q[..., :MLA_NOPE], _rope(q[..., MLA_NOPE:], pos)], axis=-1)
    ckv = _rmsnorm(ckv, kv_norm)
    kr = _rope(kr, pos)
    heads = lambda t, d: t.reshape(B, L, GLA_HEADS, d).transpose(0, 2, 1, 3).astype(f32)
    gq = heads(gq, GLA_DK) * (GLA_DK ** -0.5)
    gk = heads(gk, GLA_DK)
    gv = heads(gv, GLA_DV)
    loga = jax.nn.log_sigmoid((g_lr @ w_gate2 + gate_bias).astype(f32)) / GLA_GATE_NORM
    loga = heads(loga, GLA_DK)
    cq = cq.reshape(B, L, CA_HEADS, CA_DIM)
    ck = ck.reshape(B, L, CA_HEADS, CA_DIM)
    cv = cv.reshape(B, L, CA_HEADS, CA_DIM)
    return q, ckv, kr, gq, gk, gv, loga, g_out, cq, ck, cv


def _mla_kv(ckv, kr, w_kvup):
    B, T, _ = ckv.shape
    kv = (ckv @ w_kvup).reshape(B, T, MLA_HEADS, MLA_NOPE + MLA_V)
    k = jnp.concatenate([kv[..., :MLA_NOPE],
                         jnp.broadcast_to(kr[:, :, None, :], (B, T, MLA_HEADS, MLA_ROPE))], axis=-1)
    return k, kv[..., MLA_NOPE:]


def _mla_prompt(q, k, v):
    B, S, H, E = q.shape
    nb = S // Q_BLOCK
    scale = E ** -0.5
    qb = q.reshape(B, nb, Q_BLOCK, H, E).transpose(1, 0, 2, 3, 4)
    key_chunk = jnp.arange(S) // CHUNK

    def one(args):
        qblk, bi = args
        s = jnp.einsum('bqhe,bkhe->bhqk', qblk, k).astype(f32) * scale
        q_chunk = (bi * Q_BLOCK + jnp.arange(Q_BLOCK)) // CHUNK
        s = jnp.where((key_chunk[None, :] <= q_chunk[:, None])[None, None], s, -jnp.inf)
        p = jax.nn.softmax(s, axis=-1).astype(v.dtype)
        return jnp.einsum('bhqk,bkhd->bqhd', p, v)

    o = lax.map(one, (qb, jnp.arange(nb)))
    return o.transpose(1, 0, 2, 3, 4).reshape(B, S, H * MLA_V)


def _mla_sample(q, k, v):
    B, L, H, E = q.shape
    s = jnp.einsum('bqhe,bkhe->bhqk', q, k).astype(f32) * (E ** -0.5)
    p = jax.nn.softmax(s, axis=-1).astype(v.dtype)
    return jnp.einsum('bhqk,bkhd->bqhd', p, v).reshape(B, L, H * MLA_V)


def _gla_block(S0, q, k, v, loga):
    L = q.shape[2]
    b = jnp.cumsum(loga, axis=2)
    causal = jnp.tril(jnp.ones((L, L), dtype=bool))
    diff = b[:, :, :, None, :] - b[:, :, None, :, :]
    decay = jnp.exp(jnp.where(causal[None, None, :, :, None], diff, -jnp.inf))
    A = jnp.einsum('bhid,bhjd,bhijd->bhij', q, k, decay)
    o = jnp.einsum('bhij,bhjv->bhiv', A, v) + jnp.einsum('bhid,bhdv->bhiv', q * jnp.exp(b), S0)
    b_last = b[:, :, -1:, :]
    S1 = jnp.exp(b_last[:, :, 0, :])[..., None] * S0 \
        + jnp.einsum('bhjd,bhjv->bhdv', k * jnp.exp(b_last - b), v)
    return S1, o


def _gla_prompt(q, k, v, loga):
    B, H, S, _ = q.shape
    n = S // CHUNK
    to_chunks = lambda t: t.reshape(B, H, n, CHUNK, t.shape[-1]).transpose(2, 0, 1, 3, 4)
    S0 = jnp.zeros((B, H, GLA_DK, GLA_DV), f32)
    S_fin, o = lax.scan(lambda s, xs: _gla_block(s, *xs), S0,
                        (to_chunks(q), to_chunks(k), to_chunks(v), to_chunks(loga)))
    return o.transpose(1, 2, 0, 3, 4).reshape(B, H, S, GLA_DV), S_fin


def _rel_bias(rel, table):
    idx = jnp.clip(rel, -REL_CLIP, REL_CLIP) + REL_CLIP
    return jnp.take(table, idx, axis=0).transpose(2, 0, 1).astype(f32)


def _ca_prompt(q, k, v, table):
    B, S, H, E = q.shape
    n = S // CHUNK
    W = (CA_BAND + 1) * CHUNK
    pad = ((0, 0), (CA_BAND * CHUNK, 0), (0, 0), (0, 0))
    idx = jnp.arange(n)[:, None] * CHUNK + jnp.arange(W)[None, :]
    kb = jnp.pad(k, pad)[:, idx]
    vb = jnp.pad(v, pad)[:, idx]
    qc = q.reshape(B, n, CHUNK, H, E)
    s = jnp.einsum('bnqhe,bnkhe->bnhqk', qc, kb).astype(f32) * (E ** -0.5)
    rel = (jnp.arange(W) - CA_BAND * CHUNK)[None, :] - jnp.arange(CHUNK)[:, None]
    s = s + _rel_bias(rel, table)[None, None]
    key_chunk = jnp.arange(n)[:, None] - CA_BAND + (jnp.arange(W) // CHUNK)[None, :]
    s = jnp.where((key_chunk >= 0)[None, :, None, None, :], s, -jnp.inf)
    p = jax.nn.softmax(s, axis=-1).astype(v.dtype)
    return jnp.einsum('bnhqk,bnkhe->bnqhe', p, vb).reshape(B, S, H * E)


def _ca_sample(q, k_new, v_new, cache_k, cache_v, table):
    B, L, H, E = q.shape
    Wp = cache_k.shape[1]
    k = jnp.concatenate([cache_k, k_new], axis=1)
    v = jnp.concatenate([cache_v, v_new], axis=1)
    s = jnp.einsum('bqhe,bkhe->bhqk', q, k).astype(f32) * (E ** -0.5)
    rel = jnp.arange(-Wp, L)[None, :] - jnp.arange(L)[:, None]
    s = s + _rel_bias(rel, table)[None]
    p = jax.nn.softmax(s, axis=-1).astype(v.dtype)
    return jnp.einsum('bhqk,bkhe->bqhe', p, v).reshape(B, L, H * E)


def _merge(o_mla, o_gla, g_out, o_ca, gla_norm, w_out):
    B, L, _ = o_mla.shape
    og = _rmsnorm(o_gla.transpose(0, 2, 1, 3), gla_norm.reshape(GLA_HEADS, GLA_DV)).reshape(B, L, GLA_W)
    og = og.astype(o_mla.dtype) * jax.nn.silu(g_out)
    return jnp.concatenate([o_mla, og, o_ca], axis=-1) @ w_out


def _mlp(h, w_up, w_down):
    return jnp.square(jax.nn.relu(h @ w_up)) @ w_down


def setup_inputs(seed: int = 0) -> dict:
    key = jax.random.key(seed)
    ks = jax.random.split(key, 24)
    nrm = lambda k, shape, scale: jax.random.normal(k, shape, f32) * scale
    gain = lambda k, shape: 1.0 + 0.01 * jax.random.normal(k, shape, f32)
    ca_past = min(CA_BAND * CHUNK, PAST_LEN)
    return {
        "x_prompt": nrm(ks[0], (BATCH, SEQ, D_MODEL), 1.0),
        "x_sample": nrm(ks[1], (DEC_BATCH, DEC_SEQ, D_MODEL), 1.0),
        "cache_mla_ckv": nrm(ks[2], (DEPTH, DEC_BATCH, PAST_LEN, MLA_KV_RANK), 1.0),
        "cache_mla_krope": nrm(ks[3], (DEPTH, DEC_BATCH, PAST_LEN, MLA_ROPE), 1.0),
        "state_gla": nrm(ks[4], (DEPTH, DEC_BATCH, GLA_HEADS, GLA_DK, GLA_DV), 0.5),
        "cache_ca_k": nrm(ks[5], (DEPTH, DEC_BATCH, ca_past, CA_HEADS, CA_DIM), 1.0),
        "cache_ca_v": nrm(ks[6], (DEPTH, DEC_BATCH, ca_past, CA_HEADS, CA_DIM), 1.0),
        "norm1": gain(ks[7], (DEPTH, D_MODEL)),
        "w_in": nrm(ks[8], (DEPTH, D_MODEL, D_IN), D_MODEL ** -0.5),
        "mla_q_norm": gain(ks[9], (DEPTH, MLA_Q_RANK)),
        "mla_w_qup": nrm(ks[10], (DEPTH, MLA_Q_RANK, MLA_HEADS * (MLA_NOPE + MLA_ROPE)), MLA_Q_RANK ** -0.5),
        "mla_kv_norm": gain(ks[11], (DEPTH, MLA_KV_RANK)),
        "mla_w_kvup": nrm(ks[12], (DEPTH, MLA_KV_RANK, MLA_HEADS * (MLA_NOPE + MLA_V)), MLA_KV_RANK ** -0.5),
        "gla_w_gate2": nrm(ks[13], (DEPTH, GLA_GATE_RANK, GLA_HEADS * GLA_DK), GLA_GATE_RANK ** -0.5),
        "gla_gate_bias": nrm(ks[14], (DEPTH, GLA_HEADS * GLA_DK), 0.1),
        "gla_out_norm": gain(ks[15], (DEPTH, GLA_W)),
        "ca_rel_bias": nrm(ks[16], (DEPTH, 2 * REL_CLIP + 1, CA_HEADS), 0.1),
        "w_out": nrm(ks[17], (DEPTH, D_MODEL, D_MODEL), D_MODEL ** -0.5),
        "norm2": gain(ks[18], (DEPTH, D_MODEL)),
        "w_up": nrm(ks[19], (DEPTH, D_MODEL, D_FF), D_MODEL ** -0.5),
        "w_down": nrm(ks[20], (DEPTH, D_FF, D_MODEL), D_FF ** -0.5),
        "final_norm": gain(ks[21], (D_MODEL,)),
    }


def reference(x_prompt, x_sample, cache_mla_ckv, cache_mla_krope, state_gla, cache_ca_k, cache_ca_v,
              norm1, w_in, mla_q_norm, mla_w_qup, mla_kv_norm, mla_w_kvup, gla_w_gate2, gla_gate_bias,
              gla_out_norm, ca_rel_bias, w_out, norm2, w_up, w_down, final_norm):
    n_seq = x_prompt.shape[1]
    n_new = x_sample.shape[1]
    past_len = cache_mla_ckv.shape[2]
    pos_p = jnp.arange(n_seq)
    pos_s = past_len + jnp.arange(n_new)
    band_rows = min(CA_BAND * CHUNK, n_seq)
    xp, xs = x_prompt, x_sample
    p_ckv, p_kr, p_gla, p_ck, p_cv = [], [], [], [], []
    s_ckv, s_kr, s_gla, s_ck, s_cv = [], [], [], [], []
    for l in range(DEPTH):
        proj_w = (w_in[l], mla_q_norm[l], mla_w_qup[l], mla_kv_norm[l], gla_w_gate2[l], gla_gate_bias[l])
        q, ckv, kr, gq, gk, gv, loga, g_out, cq, ck, cv = _project(_rmsnorm(xp, norm1[l]), pos_p, *proj_w)
        k_m, v_m = _mla_kv(ckv, kr, mla_w_kvup[l])
        o_mla = _mla_prompt(q, k_m, v_m)
        o_gla, S_fin = _gla_prompt(gq, gk, gv, loga)
        o_ca = _ca_prompt(cq, ck, cv, ca_rel_bias[l])
        xp = xp + _merge(o_mla, o_gla, g_out, o_ca, gla_out_norm[l], w_out[l])
        xp = xp + _mlp(_rmsnorm(xp, norm2[l]), w_up[l], w_down[l])
        p_ckv.append(ckv)
        p_kr.append(kr)
        p_gla.append(S_fin.astype(xp.dtype))
        p_ck.append(ck[:, n_seq - band_rows:])
        p_cv.append(cv[:, n_seq - band_rows:])
        q, ckv, kr, gq, gk, gv, loga, g_out, cq, ck, cv = _project(_rmsnorm(xs, norm1[l]), pos_s, *proj_w)
        k_m, v_m = _mla_kv(jnp.concatenate([cache_mla_ckv[l], ckv], axis=1),
                           jnp.concatenate([cache_mla_krope[l], kr], axis=1), mla_w_kvup[l])
        o_mla = _mla_sample(q, k_m, v_m)
        S_new, o_gla = _gla_block(state_gla[l].astype(f32), gq, gk, gv, loga)
        o_ca = _ca_sample(cq, ck, cv, cache_ca_k[l], cache_ca_v[l], ca_rel_bias[l])
        xs = xs + _merge(o_mla, o_gla, g_out, o_ca, gla_out_norm[l], w_out[l])
        xs = xs + _mlp(_rmsnorm(xs, norm2[l]), w_up[l], w_down[l])
        s_ckv.append(ckv)
        s_kr.append(kr)
        s_gla.append(S_new.astype(state_gla.dtype))
        s_ck.append(ck)
        s_cv.append(cv)
    y_prompt = _rmsnorm(xp, final_norm)
    y_sample = _rmsnorm(xs, final_norm)
    return (y_prompt, y_sample,
            jnp.stack(p_ckv), jnp.stack(p_kr), jnp.stack(p_gla), jnp.stack(p_ck), jnp.stack(p_cv),
            jnp.stack(s_ckv), jnp.stack(s_kr), jnp.stack(s_gla), jnp.stack(s_ck), jnp.stack(s_cv))
```

```python
import math
import numpy as np
import ml_dtypes
import concourse.bass as bass
import concourse.mybir as mybir
from concourse.ap import AP as RawAP
from concourse.bass_utils import run_bass_kernel_spmd

F32 = mybir.dt.float32
BF16 = mybir.dt.bfloat16
U8 = mybir.dt.uint8
AF = mybir.ActivationFunctionType
ALU = mybir.AluOpType
AX = mybir.AxisListType

D = 1024
DEPTH = 2
SEQ = 8192
EPS = 1e-6
NCORES = 8
SEM_EPOCH = 30000


class Buf:
    __slots__ = ("w", "r", "pr", "locks")

    def __init__(self):
        self.w = {}
        self.r = {}
        self.pr = {}
        self.locks = ()


class BankLock:
    __slots__ = ("last",)

    def __init__(self):
        self.last = None


class Op:
    __slots__ = ("stream", "dom", "fn", "deps", "n", "signal", "val", "semkey", "isdma", "key")


NDMASEM = {"sp": 32, "pool": 16, "act": 8}


class Prog:
    STREAMS = ("pe", "act", "dve", "pool", "sp")

    def __init__(self, nc):
        self.nc = nc
        self.ops = {s: [] for s in self.STREAMS}
        self.domops = {}
        self.pending = {s: {} for s in self.STREAMS}

    def op(self, stream, fn, reads=(), writes=(), dma=False, accum=False):
        o = Op()
        o.stream = stream
        o.isdma = dma
        o.dom = stream + ("_dma" if dma else "")
        o.fn = fn
        o.signal = False
        lst = self.domops.setdefault(o.dom, [])
        o.n = len(lst) + 1
        lst.append(o)
        o.key = (o.dom, o.n) if dma else o.dom
        deps = dict(self.pending[stream])
        self.pending[stream] = {}

        def add(d):
            for k, p in d.items():
                if p is o:
                    continue
                q = deps.get(k)
                if q is None or q.n < p.n:
                    deps[k] = p

        for b in reads:
            add(b.w)
        for b in writes:
            if b.r:
                add(b.r)
                add(b.w)
            elif accum:
                add(b.pr)
            else:
                add(b.w)
                add(b.pr)
        seen = set()
        for b in list(reads) + list(writes):
            for lk in b.locks:
                if id(lk) in seen:
                    continue
                seen.add(id(lk))
                p = lk.last
                if p is not None and p.stream != stream:
                    q = deps.get(p.key)
                    if q is None or q.n < p.n:
                        deps[p.key] = p
                lk.last = o
        if dma:
            N = NDMASEM[stream]
            if o.n > N:
                p = lst[o.n - N - 1]
                deps[p.key] = p
        if stream == "pe" and not dma:
            deps.pop("pe", None)
        o.deps = deps
        for p in deps.values():
            p.signal = True
        for b in reads:
            b.r[o.key] = o
        for b in writes:
            if b.r:
                b.pr = dict(b.r)
                b.pr.pop(o.key, None)
                b.w = {o.key: o}
                b.r = {}
            elif accum:
                b.w[o.key] = o
            else:
                b.w = {o.key: o}
        self.ops[stream].append(o)
        return o

    def barrier(self):
        snap = {}
        for dom, lst in self.domops.items():
            if dom.endswith("_dma"):
                N = NDMASEM[dom[:-4]]
                for p in lst[-N:]:
                    snap[p.key] = p
            elif lst:
                snap[lst[-1].key] = lst[-1]
        for s in self.STREAMS:
            for k, p in snap.items():
                q = self.pending[s].get(k)
                if q is None or q.n < p.n:
                    self.pending[s][k] = p

    def finalize(self):
        self.barrier()
        self.op("sp", None)
        self.semkeys = set()
        for dom, lst in self.domops.items():
            if dom.endswith("_dma"):
                N = NDMASEM[dom[:-4]]
                for o in lst:
                    o.signal = True
                    o.semkey = (dom, (o.n - 1) % N)
                    o.val = ((o.n - 1) // N + 1) * 16
                    self.semkeys.add(o.semkey)
            else:
                cnt = 0
                ep = 0
                for o in lst:
                    if o.signal:
                        if cnt >= SEM_EPOCH:
                            ep += 1
                            cnt = 0
                        cnt += 1
                        o.val = cnt
                        o.semkey = (dom, ep)
                        self.semkeys.add(o.semkey)

    def emit(self):
        nc = self.nc
        import contextlib
        with contextlib.ExitStack() as es:
            sems = {}
            for k in sorted(self.semkeys):
                sems[k] = es.enter_context(nc.semaphore("s_%s_%d" % k))
            block = es.enter_context(nc.Block())

            def run(stream, eng):
                known = {}
                for o in self.ops[stream]:
                    for p in o.deps.values():
                        if known.get(p.semkey, 0) >= p.val:
                            continue
                        known[p.semkey] = p.val
                        eng.wait_ge(sems[p.semkey], p.val)
                    if o.fn is None:
                        continue
                    ins = o.fn(eng)
                    if o.signal:
                        ins.then_inc(sems[o.semkey], 16 if o.isdma else 1)

            @block.tensor
            def _(e):
                run("pe", e)

            @block.scalar
            def _(e):
                run("act", e)

            @block.vector
            def _(e):
                run("dve", e)

            @block.gpsimd
            def _(e):
                run("pool", e)

            @block.sync
            def _(e):
                run("sp", e)


class Arena:
    def __init__(self, ap_u8, size):
        self.ap = ap_u8
        self.size = size
        self.off = 0

    def mark(self):
        return self.off

    def release(self, m):
        self.off = m

    def alloc(self, shape, dt):
        esz = 2 if dt == BF16 else 4
        free = 1
        for s in shape[1:]:
            free *= s
        nb = (free * esz + 63) // 64 * 64
        assert self.off + nb <= self.size, ("arena overflow", self.off, nb, self.size)
        v = self.ap[0:shape[0], self.off:self.off + free * esz].bitcast(dt)
        self.off += nb
        if len(shape) == 3:
            v = v.rearrange("p (a b) -> p a b", a=shape[1])
        elif len(shape) == 4:
            v = v.rearrange("p (a b c) -> p a b c", a=shape[1], b=shape[2])
        return v


class K:
    def __init__(self, T, depth=DEPTH, dbg=False):
        self.T = T
        self.NB = T // 128
        self.depth = depth
        self.dbg = dbg
        nc = bass.Bass("TRN2", target_bir_lowering=False)
        self.nc = nc
        self.P = Prog(nc)
        self.din = {}
        self.dout = {}

    def inp(self, name, shape, dt=F32):
        t = self.nc.dram_tensor(name, list(shape), dt, kind="ExternalInput")
        self.din[name] = t
        return t

    def outp(self, name, shape, dt=F32):
        t = self.nc.dram_tensor(name, list(shape), dt, kind="ExternalOutput")
        self.dout[name] = t
        return t

    def scratch(self, name, shape, dt):
        return self.nc.dram_tensor(name, list(shape), dt)

    def mm(self, out, lhsT, rhs, start, stop, reads, writes):
        return self.P.op("pe", lambda e: e.matmul(out, lhsT, rhs, start=start, stop=stop),
                         reads=reads, writes=writes, accum=True)

    def tr(self, out, in_, ident, reads, writes):
        return self.P.op("pe", lambda e: e.transpose(out, in_, ident), reads=reads, writes=writes, accum=True)

    def act(self, out, in_, func, reads, writes, bias=None, scale=None, accum_out=None, eng="act", accum=False):
        kw = {}
        if bias is not None:
            kw["bias"] = bias
        if scale is not None:
            kw["scale"] = scale
        if accum_out is not None:
            kw["accum_out"] = accum_out
        return self.P.op("act", lambda e: e.activation(out, in_, func, **kw), reads=reads, writes=writes, accum=accum)

    def ts(self, eng, out, in0, s1, s2, op0, op1, reads, writes, accum=False):
        if op1 is None:
            return self.P.op(eng, lambda e: e.tensor_scalar(out, in0, s1, None, op0), reads=reads, writes=writes, accum=accum)
        return self.P.op(eng, lambda e: e.tensor_scalar(out, in0, s1, s2, op0, op1), reads=reads, writes=writes, accum=accum)

    def tt(self, eng, out, in0, in1, op, reads, writes, accum=False):
        return self.P.op(eng, lambda e: e.tensor_tensor(out, in0, in1, op), reads=reads, writes=writes, accum=accum)

    def stt(self, eng, out, in0, scalar, in1, op0, op1, reads, writes, accum=False):
        return self.P.op(eng, lambda e: e.scalar_tensor_tensor(out, in0, scalar, in1, op0, op1),
                         reads=reads, writes=writes, accum=accum)

    def cp(self, eng, out, in_, reads, writes, accum=False):
        if eng == "act":
            return self.P.op("act", lambda e: e.copy(out, in_), reads=reads, writes=writes, accum=accum)
        return self.P.op(eng, lambda e: e.tensor_copy(out, in_), reads=reads, writes=writes, accum=accum)

    def rsqrt(self, out, in_, eps, tmp, reads, writes):
        self.act(tmp.ap if hasattr(tmp, "ap") else tmp, in_, AF.Ln, reads, [self._tmpb(tmp)], bias=eps)
        return self.act(out, tmp.ap if hasattr(tmp, "ap") else tmp, AF.Exp, [self._tmpb(tmp)], writes, scale=-0.5)

    def _tmpb(self, tmp):
        return tmp.b

    def memset(self, eng, ap, val, writes, accum=False):
        return self.P.op(eng, lambda e: e.memset(ap, val), writes=writes, accum=accum)

    def dma(self, q, out, in_, reads, writes, accum=False, slow=False):
        if slow:
            return self.P.op(q, lambda e: e.dma_start(out=out, in_=in_, allow_slow_non_contiguous=True),
                             reads=reads, writes=writes, dma=True, accum=accum)
        return self.P.op(q, lambda e: e.dma_start(out=out, in_=in_), reads=reads, writes=writes, dma=True, accum=accum)


class TL:
    __slots__ = ("ap", "b")

    def __init__(self, ap):
        self.ap = ap
        self.b = Buf()


def TL2(t, a, b):
    v = TL(t.ap[:, a:b])
    v.b = t.b
    return v


def bc_mid(ap2d, n):
    p, f = ap2d.shape
    return ap2d.unsqueeze(1).broadcast_to([p, n, f])


def bc_last(ap2d, n):
    p, a = ap2d.shape
    return ap2d.unsqueeze(2).broadcast_to([p, a, n])


W_IN_SEGS = [(0, 0, 416), (416, 1184, 16), (432, 416, 768), (1200, 1200, 1408)]


def seg_map(d0, d1, segs):
    out = []
    for (ds, ss, ln) in segs:
        a = max(d0, ds)
        b = min(d1, ds + ln)
        if a < b:
            out.append((a, ss + (a - ds), b - a))
    return out


def build_program(T, depth=DEPTH, dbg=False):
    k = K(T, depth, dbg)
    nc = k.nc
    P = k.P
    NB = T // 128
    L = depth
    assert T % 512 == 0

    xp = k.inp("xp", [T, D])
    xs = k.inp("xs", [256, D])
    c_ckv = k.inp("c_ckv", [L, 4, 1024, 128])
    c_kr = k.inp("c_kr", [L, 4, 1024, 32])
    st_gla = k.inp("st_gla", [L, 4, 4, 64, 64])
    c_cak = k.inp("c_cak", [L, 4, 512, 384])
    c_cav = k.inp("c_cav", [L, 4, 512, 384])
    w_in = k.inp("w_in", [L, 1024, 2608])
    w_qup = k.inp("w_qup", [L, 256, 576])
    w_kvup = k.inp("w_kvup", [L, 128, 768])
    w_gate2 = k.inp("w_gate2", [L, 16, 256])
    gate_bias = k.inp("gate_bias", [L, 256])
    w_out = k.inp("w_out", [L, 1024, 1024])
    w_up = k.inp("w_up", [L, 1024, 4096])
    w_down = k.inp("w_down", [L, 4096, 1024])
    norm1T = k.inp("norm1T", [L, 128, 8])
    norm2T = k.inp("norm2T", [L, 128, 8])
    qnormT = k.inp("qnormT", [L, 128, 2])
    kvnorm = k.inp("kvnorm", [L, 128])
    glanorm = k.inp("glanorm", [L, 256])
    fnorm = k.inp("fnorm", [D])
    ext2 = k.inp("ext2", [L, 6, 767])
    c_ident = k.inp("c_ident", [128, 128])
    c_J = k.inp("c_J", [128, 128])
    c_U = k.inp("c_U", [128, 128])
    cs_p = k.inp("cs_p", [T, 64])
    cs_s = k.inp("cs_s", [128, 64])

    y_p = k.outp("y_p", [T, D])
    y_s = k.outp("y_s", [256, D])
    p_ckv = k.outp("p_ckv", [L, T, 128])
    p_kr = k.outp("p_kr", [L, T, 32])
    p_gla = k.outp("p_gla", [L, 4, 64, 64])
    p_cak = k.outp("p_cak", [L, 512, 384])
    p_cav = k.outp("p_cav", [L, 512, 384])
    s_ckv = k.outp("s_ckv", [L, 256, 128])
    s_kr = k.outp("s_kr", [L, 256, 32])
    s_gla = k.outp("s_gla", [L, 4, 4, 64, 64])
    s_cak = k.outp("s_cak", [L, 256, 384])
    s_cav = k.outp("s_cav", [L, 256, 384])

    mk = k.outp if dbg else (lambda n, s, d: k.scratch(n, s, d))
    XB = mk("XB", [T, D], F32)
    QNT = mk("QNT", [3, 128, T], BF16)
    QRT = mk("QRT", [3, 128, T], BF16)
    KNT = mk("KNT", [3, 128, T], BF16)
    KR3T = mk("KR3T", [128, T], BF16)
    VA = mk("VA", [T, 768], BF16)
    QT = mk("QT", [6, 128, T], BF16)
    KT = mk("KT", [6, 128, T], BF16)
    MERGED = mk("MERGED", [T, D], BF16)

    ARENA_BYTES = 206 * 1024
    arena_t = nc.alloc_sbuf_tensor("arena", [128, ARENA_BYTES], U8) if False else None
    import contextlib
    es = contextlib.ExitStack()
    arena_t = es.enter_context(nc.sbuf_tensor("arena", [128, ARENA_BYTES], U8))
    ps_t = es.enter_context(nc.psum_tensor("ps", [128, 4096], F32))
    A = Arena(arena_t, ARENA_BYTES)

    def bank(i):
        return ps_t[:, 512 * i:512 * (i + 1)]

    def bank_bf(i):
        return ps_t[:, 512 * i:512 * (i + 1)].bitcast(BF16)

    def sb(shape, dt):
        return TL(A.alloc(shape, dt))

    bank_locks = [BankLock() for _ in range(8)]

    def TLP(i, bf=False):
        t = TL(bank_bf(i) if bf else bank(i))
        t.b.locks = (bank_locks[i],)
        return t

    ident_f = sb([128, 128], F32)
    ident_b = sb([128, 128], BF16)
    J_b = sb([128, 128], BF16)
    U_f = sb([128, 128], F32)
    cs_st = sb([128, 64], F32)
    x_s = sb([128, 2, D], F32)
    merged_s = sb([128, 2, D], BF16)

    k.dma("sp", ident_f.ap, c_ident[:, :], [], [ident_f.b])
    k.dma("pool", ident_b.ap, c_ident[:, :], [], [ident_b.b])
    k.dma("pool", J_b.ap, c_J[:, :], [], [J_b.b])
    k.dma("sp", U_f.ap, c_U[:, :], [], [U_f.b])
    k.dma("sp", cs_st.ap, cs_s[:, :], [], [cs_st.b])
    k.dma("sp", x_s.ap, xs.ap().rearrange("(t p) c -> p t c", p=128), [], [x_s.b])

    stage = [sb([128, 1024], F32) for _ in range(3)]
    stage_i = [0]

    def load_weight(dst, dstbuf, nchunks, ncols, src_fn, scale_fn, segs=None, eng="pool", stages=None, engs=None):
        stages = stages or stage
        for c in range(nchunks):
            for d0 in range(0, ncols, 1024):
                d1 = min(ncols, d0 + 1024)
                st = stages[stage_i[0] % len(stages)]
                if engs:
                    eng = engs[stage_i[0] % len(engs)]
                stage_i[0] += 1
                pieces = seg_map(d0, d1, segs) if segs else [(d0, d0, d1 - d0)]
                for (a, s0, ln) in pieces:
                    k.dma("sp", st.ap[:, a - d0:a - d0 + ln], src_fn(c, s0, ln), [], [st.b], accum=True)
                if scale_fn is not None:
                    sap, sbuf = scale_fn(c)
                    if eng == "act":
                        k.act(dst[:, c, d0:d1], st.ap[:, 0:d1 - d0], AF.Copy, [st.b, sbuf], [dstbuf], scale=sap,
                              accum=True)
                    else:
                        k.ts(eng, dst[:, c, d0:d1], st.ap[:, 0:d1 - d0], sap, 1.0, ALU.mult, ALU.mult,
                             [st.b, sbuf], [dstbuf], accum=True)
                else:
                    k.cp(eng, dst[:, c, d0:d1], st.ap[:, 0:d1 - d0], [st.b], [dstbuf], accum=True)

    persist_mark = A.mark()
    import os
    STOP = os.environ.get("KSTOP", "")

    class StopBuild(Exception):
        pass

    def chk(name):
        if name == STOP:
            raise StopBuild()

    try:
      chk("consts")
      for l in range(L):
          last_layer = (l == L - 1)
          x_src = xp if l == 0 else XB
          A.release(persist_mark)
          W_in = sb([128, 8, 2608], BF16)
          W_qup = sb([128, 2, 576], BF16)
          W_kn = sb([128, 1, 384], BF16)
          W_v = sb([128, 1, 384], BF16)
          W_g2 = sb([17, 256], BF16)
          n1T = sb([128, 8], F32)
          qnT = sb([128, 2], F32)
          kvg_bc = sb([128, 128], F32)
          gn_bc = sb([128, 256], F32)
          BT = sb([128, 6, 640], BF16)
          BTs = sb([80, 6, 528], BF16)
          cs_blk = [sb([128, 64], F32) for _ in range(2)]
          k.dma("sp", n1T.ap, norm1T[l, :, :], [], [n1T.b])
          k.dma("sp", qnT.ap, qnormT[l, :, :], [], [qnT.b])
          k.dma("sp", kvg_bc.ap, kvnorm[l, :].partition_broadcast(128), [], [kvg_bc.b])
          k.dma("sp", gn_bc.ap, glanorm[l, :].partition_broadcast(128), [], [gn_bc.b])
          k.ts("dve", gn_bc.ap, gn_bc.ap, 8.0, None, ALU.mult, None, [gn_bc.b], [gn_bc.b])
          k.dma("pool", W_g2.ap[0:16, :], w_gate2[l, :, :], [], [W_g2.b], accum=True)
          k.dma("pool", W_g2.ap[16:17, :], gate_bias[l:l + 1, :], [], [W_g2.b], accum=True)
          for h in range(6):
              src = RawAP(ext2, (l * 6 + h) * 767, [[1, 128], [1, 640]])
              k.dma("pool", BT.ap[:, h, :], src, [], [BT.b], accum=True)
              src2 = RawAP(ext2, (l * 6 + h) * 767 + 112, [[1, 16], [1, 528]])
              k.dma("pool", BTs.ap[0:16, h, :], src2, [], [BTs.b], accum=True)
              k.dma("pool", BTs.ap[64:80, h, :], src2, [], [BTs.b], accum=True)
          k.memset("pool", BT.ap[0:64, :, 0:64], -30000.0, [BT.b])
          k.memset("pool", BT.ap[64:128, :, 576:640], -30000.0, [BT.b])
          load_weight(W_in.ap, W_in.b, 8, 2608,
                      lambda c, s0, ln: w_in[l, c * 128:(c + 1) * 128, s0:s0 + ln],
                      lambda c: (n1T.ap[:, c:c + 1], n1T.b), segs=W_IN_SEGS)
          load_weight(W_qup.ap, W_qup.b, 2, 576,
                      lambda c, s0, ln: w_qup[l, c * 128:(c + 1) * 128, s0:s0 + ln],
                      lambda c: (qnT.ap[:, c:c + 1], qnT.b))
          wkv = w_kvup[l, :, :].rearrange("r (h t e) -> r h t e", h=6, t=2)
          k.dma("pool", W_kn.ap[:, 0, :].rearrange("r (h e) -> r h e", h=6), wkv[:, :, 0, :], [], [W_kn.b])
          k.dma("pool", W_v.ap[:, 0, :].rearrange("r (h e) -> r h e", h=6), wkv[:, :, 1, :], [], [W_v.b])

          chk("a1w")
          Xt = [sb([128, D], F32) for _ in range(2)]
          junk = sb([128, 512], BF16)
          junkF = sb([128, D], BF16)
          ss = sb([128, 4], F32)
          rstd = sb([128, 4], F32)
          lnt = sb([128, 8], F32)
          ssF = sb([128, 2], F32)
          rstdF = sb([128, 2], F32)
          lntF = sb([128, 2], F32)
          z0s = [sb([128, 432], F32) for _ in range(2)]
          z1s = [sb([128, 512], F32) for _ in range(2)]
          z2s = [sb([128, 512], F32) for _ in range(2)]
          xn = sb([128, D], BF16)
          hT = sb([128, D], BF16)
          qn = sb([128, 256], BF16)
          ckvn = sb([128, 128], F32)
          ckvb = sb([128, 128], BF16)
          krf = sb([128, 32], F32)
          krt = sb([128, 32], F32)
          kr3 = sb([128, 128], BF16)
          glr = sb([128, 16], BF16)
          TT = sb([128, 384], BF16)
          g_aug = sb([17, 128], BF16)
          Qf = sb([128, 6, 96], F32)
          qtA = sb([128, 6, 32], F32)
          qtB = sb([128, 6, 32], F32)
          Qn_tok = sb([128, 384], BF16)
          Qr_tok = sb([128, 3, 128], BF16)
          QNs = sb([128, 3, 128], BF16)
          QRs = sb([128, 3, 128], BF16)
          KNs = sb([128, 3, 128], BF16)
          KRs = sb([128, 128], BF16)
          Vaug = sb([128, 6, 65], BF16)
          Vpad = sb([128, 6, 128], BF16)
          Qp_tok = sb([128, 6, 128], BF16)
          QTs = sb([128, 6, 512], BF16)
          KTs = sb([128, 6, 512], BF16)
          e1 = sb([128, 256], F32)
          spl = sb([128, 256], F32)
          ebT = sb([128, 2, 128], F32)
          enbT = sb([128, 2, 128], F32)
          enb = sb([128, 256], F32)
          ebl = sb([128, 2, 2], F32)
          gqk = sb([128, 512], BF16)
          qeT = sb([128, 2, 128], BF16)
          keT = sb([128, 2, 128], BF16)
          ke = sb([128, 256], BF16)
          gv = sb([128, 256], BF16)
          e2 = sb([128, 256], F32)
          gate = sb([128, 256], F32)
          ATm = sb([128, 2, 2, 128], BF16)
          S_f = sb([128, 2, 64], F32)
          S_t = sb([128, 2, 64], F32)
          S_b = sb([128, 2, 64], BF16)
          o_sb = sb([128, 4, 64], F32)
          o_sq = sb([128, 4, 64], F32)
          o_ss = sb([128, 4], F32)
          o_r = sb([128, 4], F32)
          merged = sb([128, D], BF16)
          cqb2 = [sb([128, 384], BF16) for _ in range(2)]
          ckb2 = [sb([128, 384], BF16) for _ in range(2)]
          ckf = sb([128, 384], F32)
          cvf = sb([128, 384], F32)
          cqT = sb([128, 3, 128], BF16)
          Kring = [sb([128, 3, 128], BF16) for _ in range(5)]
          Vring = [sb([128, 6, 65], BF16) for _ in range(6)]
          PT = [sb([128, 5, 128], BF16) for _ in range(2)]
          rden = sb([128, 6], F32)
          sc_ckv = sb([128, 8, 128], BF16)
          sc_kr3 = sb([128, 8, 128], BF16)
          sc_ckvT = sb([128, 1024], BF16)
          sc_KN = sb([128, 3, 1024], BF16)
          sc_KR = sb([128, 1024], BF16)
          sc_V = sb([128, 8, 6, 65], BF16)
          sc_cak = sb([128, 4, 384], BF16)
          sc_caKT = sb([128, 3, 512], BF16)
          sc_caV = sb([128, 4, 6, 65], BF16)
          sPT = sb([128, 6, 9, 16], BF16)
          J16 = {0: J_b.ap[0:16, 112:128], 64: J_b.ap[64:80, 48:64]}

          k.memset("pool", g_aug.ap, 1.0, [g_aug.b])
          k.memset("pool", kr3.ap, 0.0, [kr3.b])
          k.memset("pool", Qr_tok.ap, 0.0, [Qr_tok.b])
          k.memset("pool", sc_kr3.ap, 0.0, [sc_kr3.b])
          k.memset("pool", Vaug.ap, 1.0, [Vaug.b])
          k.memset("pool", Vpad.ap, 0.0, [Vpad.b])
          k.memset("pool", Vpad.ap[:, :, 64:65], 1.0, [Vpad.b])
          k.memset("pool", Qp_tok.ap, 0.0, [Qp_tok.b])
          k.memset("pool", KTs.ap, 0.0, [KTs.b])
          for r in range(6):
              k.memset("pool", Vring[r].ap, 1.0, [Vring[r].b])
          k.memset("pool", sc_V.ap, 1.0, [sc_V.b])
          k.memset("pool", sc_caV.ap, 1.0, [sc_caV.b])
          k.memset("pool", merged_s.ap, 0.0, [merged_s.b])
          k.memset("dve", S_f.ap, 0.0, [S_f.b])
          k.memset("dve", S_b.ap, 0.0, [S_b.b])
          P.barrier()
          for h in range(6):
              pe_ = [TLP(4), TLP(5)]
              for w in range(5):
                  pb_ = pe_[0] if w < 4 else pe_[1]
                  k.mm(pb_.ap[:, (w % 4) * 128:(w % 4 + 1) * 128], BT.ap[:, h, w * 128:(w + 1) * 128], J_b.ap, True, True,
                       [BT.b, J_b.b], [pb_.b])
              k.act(BT.ap[:, h, 0:512], pe_[0].ap, AF.Exp, [pe_[0].b], [BT.b])
              k.act(BT.ap[:, h, 512:640], pe_[1].ap[:, 0:128], AF.Exp, [pe_[1].b], [BT.b], accum=True)
          P.barrier()

          pT1 = TLP(0, True)
          pT2 = TLP(1, True)
          pT2f = TLP(1)
          pT2f.b = pT2.b
          pZ = [TLP(2), TLP(3)]
          pM = [TLP(4), TLP(5), TLP(6), TLP(7)]
          pCA = [TL(ps_t[:, 2048:2048 + 640]), TL(ps_t[:, 3072:3072 + 640])]
          pCA[0].b = pM[0].b
          pCA[1].b = pM[2].b

          ZG = [(0, 432), (432, 944), (944, 1456), (1456, 1840), (1840, 2224), (2224, 2608)]

          def front_gen(kind, idx, par, vslot):
              smp = (kind == "s")
              if smp:
                  X = TL(x_s.ap[:, idx, :])
                  X.b = x_s.b
              else:
                  X = Xt[idx % 2]
              k.act(junkF.ap, X.ap, AF.Square, [X.b], [junkF.b, ssF.b], scale=1.0 / 32.0, accum_out=ssF.ap[:, 0:1])
              k.rsqrt(rstdF.ap[:, 0:1], ssF.ap[:, 0:1], EPS, TL2(lntF, 0, 1), [ssF.b], [rstdF.b])
              k.ts("dve", xn.ap, X.ap, rstdF.ap[:, 0:1], None, ALU.mult, None, [X.b, rstdF.b], [xn.b])
              for c in range(8):
                  k.tr(pT1.ap[:, c * 128:(c + 1) * 128], xn.ap[:, c * 128:(c + 1) * 128], ident_b.ap,
                       [xn.b, ident_b.b], [pT1.b])
              k.cp("act", hT.ap, pT1.ap, [pT1.b], [hT.b])
              yield
              need_out = smp or idx >= NB - 4
              for g, (c0, c1) in enumerate(ZG):
                  pz = pZ[g % 2]
                  for c in range(8):
                      k.mm(pz.ap[:, 0:c1 - c0], hT.ap[:, c * 128:(c + 1) * 128], W_in.ap[:, c, c0:c1],
                           c == 0, c == 7, [hT.b, W_in.b], [pz.b])
                  if g == 0:
                      k.cp("act", z0s[par].ap, pz.ap[:, 0:432], [pz.b], [z0s[par].b])
                  elif g == 1:
                      k.cp("dve", z1s[par].ap, pz.ap, [pz.b], [z1s[par].b])
                  elif g == 2:
                      k.cp("act", z2s[par].ap, pz.ap, [pz.b], [z2s[par].b])
                  elif g == 3:
                      k.act(cqb2[par].ap, pz.ap[:, 0:384], AF.Copy, [pz.b], [cqb2[par].b], scale=0.125)
                  elif g == 4:
                      k.cp("dve", ckb2[par].ap, pz.ap[:, 0:384], [pz.b], [ckb2[par].b])
                      if need_out:
                          k.cp("dve", ckf.ap, pz.ap[:, 0:384], [pz.b], [ckf.b])
                          dst = s_cak[l, idx * 128:(idx + 1) * 128, :] if smp else \
                              p_cak[l, (idx - (NB - 4)) * 128:(idx - (NB - 4) + 1) * 128, :]
                          k.dma("sp", dst, ckf.ap, [ckf.b], [])
                  else:
                      Vr_ = Vring[vslot]
                      k.cp("act", Vr_.ap[:, :, 0:64], pz.ap[:, 0:384].rearrange("p (h e) -> p h e", h=6), [pz.b],
                           [Vr_.b], accum=True)
                      if need_out:
                          k.cp("dve", cvf.ap, pz.ap[:, 0:384], [pz.b], [cvf.b])
                          dst = s_cav[l, idx * 128:(idx + 1) * 128, :] if smp else \
                              p_cav[l, (idx - (NB - 4)) * 128:(idx - (NB - 4) + 1) * 128, :]
                          k.dma("sp", dst, cvf.ap, [cvf.b], [])
                  yield

          def a1_block(kind, idx, par, vslot, adv):
              smp = (kind == "s")
              if smp:
                  cs = cs_st.ap
                  csb = cs_st.b
              else:
                  cs = cs_blk[idx % 2].ap
                  csb = cs_blk[idx % 2].b
              z0, z1, z2 = z0s[par], z1s[par], z2s[par]
              cqb, ckb = cqb2[par], ckb2[par]
              k.act(junk.ap[:, 0:256], z0.ap[:, 0:256], AF.Square, [z0.b], [junk.b, ss.b], scale=1.0 / 16.0,
                    accum_out=ss.ap[:, 1:2])
              k.act(junk.ap[:, 256:384], z0.ap[:, 256:384], AF.Square, [z0.b], [junk.b, ss.b],
                    scale=128.0 ** -0.5, accum_out=ss.ap[:, 2:3])
              k.rsqrt(rstd.ap[:, 1:3], ss.ap[:, 1:3], EPS, TL2(lnt, 1, 3), [ss.b], [rstd.b])
              k.ts("dve", qn.ap, z0.ap[:, 0:256], rstd.ap[:, 1:2], None, ALU.mult, None, [z0.b, rstd.b], [qn.b])
              k.stt("dve", ckvn.ap, z0.ap[:, 256:384], rstd.ap[:, 2:3], kvg_bc.ap, ALU.mult, ALU.mult,
                    [z0.b, rstd.b, kvg_bc.b], [ckvn.b])
              k.cp("pool", ckvb.ap, ckvn.ap, [ckvn.b], [ckvb.b])
              k.tt("dve", krf.ap, z0.ap[:, 384:416], cs[:, 0:32], ALU.mult, [z0.b, csb], [krf.b])
              k.tt("dve", krt.ap[:, 0:16], z0.ap[:, 400:416], cs[:, 32:48], ALU.mult, [z0.b, csb], [krt.b], accum=True)
              k.tt("dve", krt.ap[:, 16:32], z0.ap[:, 384:400], cs[:, 48:64], ALU.mult, [z0.b, csb], [krt.b], accum=True)
              k.cp("act", glr.ap, z0.ap[:, 416:432], [z0.b], [glr.b])
              k.tt("dve", krf.ap, krf.ap, krt.ap, ALU.add, [krf.b, krt.b], [krf.b])
              for r in range(2):
                  k.cp("pool", kr3.ap[:, 64 * r:64 * r + 32], krf.ap, [krf.b], [kr3.b], accum=True)
              if smp:
                  k.dma("sp", s_ckv[l, idx * 128:(idx + 1) * 128, :], ckvn.ap, [ckvn.b], [])
                  k.dma("sp", s_kr[l, idx * 128:(idx + 1) * 128, :], krf.ap, [krf.b], [])
              else:
                  k.dma("sp", p_ckv[l, idx * 128:(idx + 1) * 128, :], ckvn.ap, [ckvn.b], [])
                  k.dma("sp", p_kr[l, idx * 128:(idx + 1) * 128, :], krf.ap, [krf.b], [])
              k.tr(pT2.ap[:, 0:128], qn.ap[:, 0:128], ident_b.ap, [qn.b, ident_b.b], [pT2.b])
              k.tr(pT2.ap[:, 128:256], qn.ap[:, 128:256], ident_b.ap, [qn.b], [pT2.b])
              k.tr(pT2.ap[:, 256:384], ckvb.ap, ident_b.ap, [ckvb.b], [pT2.b])
              SK = os.environ.get("KSKIP", "")
              if "kr" not in SK:
                  k.tr(pT2.ap[:, 384:512], kr3.ap, ident_b.ap, [kr3.b], [pT2.b])
              if "glr" not in SK:
                  k.tr(pT2.ap[0:16, 512:640], glr.ap, ident_b.ap, [glr.b], [pT2.b])
              sl = (idx % 4) * 128 if not smp else 0
              if "tt" not in SK:
                  k.cp("act", TT.ap, pT2.ap[:, 0:384], [pT2.b], [TT.b])
              if smp:
                  k.cp("dve", KRs.ap[:, sl:sl + 128], pT2.ap[:, 384:512], [pT2.b], [KRs.b], accum=True)
              else:
                  k.cp("dve", KTs.ap[64:96, :, sl:sl + 128], bc_mid(pT2.ap[64:96, 384:512], 6), [pT2.b], [KTs.b],
                       accum=True)
              if "glr" not in SK:
                  k.cp("dve", g_aug.ap[0:16, :], pT2.ap[0:16, 512:640], [pT2.b], [g_aug.b], accum=True)
              adv()
              for hf in range(2):
                  pq = pM[hf]
                  for c in range(2):
                      k.mm(pq.ap[:, 0:288], TT.ap[:, c * 128:(c + 1) * 128], W_qup.ap[:, c, hf * 288:(hf + 1) * 288],
                           c == 0, c == 1, [TT.b, W_qup.b], [pq.b])
                  k.cp("act", Qf.ap[:, 3 * hf:3 * hf + 3, :], pq.ap[:, 0:288].rearrange("p (h e) -> p h e", h=3),
                       [pq.b], [Qf.b], accum=True)
              if smp:
                  pkn = pM[2]
                  for p in range(3):
                      k.mm(pkn.ap[:, p * 128:(p + 1) * 128], W_kn.ap[:, 0, p * 128:(p + 1) * 128], TT.ap[:, 256:384],
                           True, True, [TT.b, W_kn.b], [pkn.b])
                  pv = pM[3]
                  k.mm(pv.ap[:, 0:384], TT.ap[:, 256:384], W_v.ap[:, 0, :], True, True, [TT.b, W_v.b], [pv.b])
                  k.cp("act", KNs.ap[:, :, sl:sl + 128], pkn.ap[:, 0:384].rearrange("p (a t) -> p a t", a=3),
                       [pkn.b], [KNs.b], accum=True)
                  k.cp("dve", Vaug.ap[:, :, 0:64], pv.ap[:, 0:384].rearrange("p (h e) -> p h e", h=6),
                       [pv.b], [Vaug.b], accum=True)
              else:
                  for r in range(2):
                      pk_ = pM[2 + r]
                      for j in range(3):
                          h = 3 * r + j
                          k.mm(pk_.ap[0:64, j * 128:(j + 1) * 128], W_kn.ap[:, 0, h * 64:(h + 1) * 64],
                               TT.ap[:, 256:384], True, True, [TT.b, W_kn.b], [pk_.b])
                      k.cp("act" if r == 0 else "dve", KTs.ap[0:64, 3 * r:3 * r + 3, sl:sl + 128],
                           pk_.ap[0:64, 0:384].rearrange("p (a t) -> p a t", a=3), [pk_.b], [KTs.b], accum=True)
                  pv = pM[2]
                  k.mm(pv.ap[:, 0:384], TT.ap[:, 256:384], W_v.ap[:, 0, :], True, True, [TT.b, W_v.b], [pv.b])
                  k.cp("act", Vpad.ap[:, :, 0:64], pv.ap[:, 0:384].rearrange("p (h e) -> p h e", h=6),
                       [pv.b], [Vpad.b], accum=True)
                  k.dma("sp", VA[idx * 128:(idx + 1) * 128, :], Vpad.ap.rearrange("p h e -> p (h e)"), [Vpad.b], [])
              adv()
              k.tt("dve", qtA.ap, Qf.ap[:, :, 64:96], bc_mid(cs[:, 0:32], 6), ALU.mult, [Qf.b, csb], [qtA.b])
              k.tt("dve", qtB.ap[:, :, 0:16], Qf.ap[:, :, 80:96], bc_mid(cs[:, 32:48], 6), ALU.mult, [Qf.b, csb],
                   [qtB.b], accum=True)
              k.tt("dve", qtB.ap[:, :, 16:32], Qf.ap[:, :, 64:80], bc_mid(cs[:, 48:64], 6), ALU.mult, [Qf.b, csb],
                   [qtB.b], accum=True)
              if smp:
                  k.tt("dve", Qr_tok.ap.rearrange("p a (hh c) -> p a hh c", hh=2)[:, :, :, 0:32],
                       qtA.ap.rearrange("p (a hh) e -> p a hh e", hh=2), qtB.ap.rearrange("p (a hh) e -> p a hh e", hh=2),
                       ALU.add, [qtA.b, qtB.b], [Qr_tok.b], accum=True)
                  k.cp("pool", Qn_tok.ap.rearrange("p (h e) -> p h e", h=6), Qf.ap[:, :, 0:64], [Qf.b], [Qn_tok.b])
                  for p in range(3):
                      k.tr(pT2.ap[:, p * 128:(p + 1) * 128], Qn_tok.ap[:, p * 128:(p + 1) * 128], ident_b.ap,
                           [Qn_tok.b, ident_b.b], [pT2.b])
                  for g2 in range(3):
                      k.tr(pT2.ap[:, 384 + g2 * 128:512 + g2 * 128], Qr_tok.ap[:, g2, :], ident_b.ap,
                           [Qr_tok.b], [pT2.b])
                  k.cp("act", QNs.ap[:, :, sl:sl + 128], pT2.ap[:, 0:384].rearrange("p (a t) -> p a t", a=3),
                       [pT2.b], [QNs.b], accum=True)
                  k.cp("dve", QRs.ap[:, :, sl:sl + 128], pT2.ap[:, 384:768].rearrange("p (a t) -> p a t", a=3),
                       [pT2.b], [QRs.b], accum=True)
              else:
                  k.tt("dve", Qp_tok.ap[:, :, 64:96], qtA.ap, qtB.ap, ALU.add, [qtA.b, qtB.b], [Qp_tok.b], accum=True)
                  k.cp("pool", Qp_tok.ap[:, :, 0:64], Qf.ap[:, :, 0:64], [Qf.b], [Qp_tok.b], accum=True)
                  for h in range(6):
                      k.tr(pT2.ap[:, h * 128:(h + 1) * 128], Qp_tok.ap[:, h, :], ident_b.ap,
                           [Qp_tok.b, ident_b.b], [pT2.b])
                  k.cp("act", QTs.ap[:, 0:3, sl:sl + 128], pT2.ap[:, 0:384].rearrange("p (a t) -> p a t", a=3),
                       [pT2.b], [QTs.b], accum=True)
                  k.cp("dve", QTs.ap[:, 3:6, sl:sl + 128], pT2.ap[:, 384:768].rearrange("p (a t) -> p a t", a=3),
                       [pT2.b], [QTs.b], accum=True)
                  if idx % 4 == 3:
                      t0 = (idx - 3) * 128
                      k.dma("sp", QT[:, :, t0:t0 + 512].rearrange("a p t -> p a t"), QTs.ap, [QTs.b], [])
                      k.dma("sp", KT[:, :, t0:t0 + 512].rearrange("a p t -> p a t"), KTs.ap, [KTs.b], [])
              adv()
              k.cp("act", gqk.ap, z1.ap, [z1.b], [gqk.b])
              for i in range(4):
                  k.tr(pT2.ap[:, i * 128:(i + 1) * 128], gqk.ap[:, i * 128:(i + 1) * 128], ident_b.ap,
                       [gqk.b, ident_b.b], [pT2.b])
              pg = pM[0]
              k.mm(pg.ap[:, 0:256], g_aug.ap, W_g2.ap, True, True, [g_aug.b, W_g2.b], [pg.b])
              k.act(e1.ap, pg.ap[:, 0:256], AF.Exp, [pg.b], [e1.b], scale=-1.0)
              k.act(spl.ap, e1.ap, AF.Ln, [e1.b], [spl.b], bias=1.0)
              k.mm(pg.ap[:, 256:512], U_f.ap, spl.ap, True, True, [U_f.b, spl.b], [pg.b])
              pc = pM[1]
              for p in range(2):
                  k.mm(pc.ap[:, p * 128:(p + 1) * 128], spl.ap[:, p * 128:(p + 1) * 128], U_f.ap, True, True,
                       [spl.b, U_f.b], [pc.b])
              cbT = pc.ap[:, 0:256].rearrange("p (a t) -> p a t", a=2)
              k.act(ebT.ap, cbT, AF.Exp, [pc.b], [ebT.b], scale=-1.0 / 16.0, bias=math.log(0.125))
              k.act(enbT.ap, cbT, AF.Exp, [pc.b], [enbT.b], scale=1.0 / 16.0)
              k.act(enb.ap, pg.ap[:, 256:512], AF.Exp, [pg.b], [enb.b], scale=1.0 / 16.0)
              lastc = 15 if smp else 63
              k.act(ebl.ap, pc.ap[:, 0:256].rearrange("p (a c t) -> p a c t", a=2, c=2)[:, :, :, lastc],
                    AF.Exp, [pc.b], [ebl.b], scale=-1.0 / 16.0)
              k.tt("dve", qeT.ap, pT2.ap[:, 0:256].rearrange("p (a t) -> p a t", a=2), ebT.ap, ALU.mult,
                   [pT2.b, ebT.b], [qeT.b])
              k.tt("dve", keT.ap, pT2.ap[:, 256:512].rearrange("p (a t) -> p a t", a=2), enbT.ap, ALU.mult,
                   [pT2.b, enbT.b], [keT.b])
              k.tt("dve", ke.ap, z1.ap[:, 256:512], enb.ap, ALU.mult, [z1.b, enb.b], [ke.b])
              k.cp("pool" if False else "dve", gv.ap, z2.ap[:, 0:256], [z2.b], [gv.b])
              k.act(e2.ap, z2.ap[:, 256:512], AF.Exp, [z2.b], [e2.b], scale=-1.0)
              k.ts("dve", e2.ap, e2.ap, 1.0, None, ALU.add, None, [e2.b], [e2.b])
              k.P.op("dve", lambda e: e.reciprocal(e2.ap, e2.ap), reads=[e2.b], writes=[e2.b])
              k.tt("dve", gate.ap, z2.ap[:, 256:512], e2.ap, ALU.mult, [z2.b, e2.b], [gate.b])
              adv()
              for h in range(4):
                  p, hb = h // 2, (h % 2) * 64
                  pa = pM[2 + (h % 2)]
                  k.mm(pa.ap[:, p * 128:(p + 1) * 128], keT.ap[hb:hb + 64, p, :], qeT.ap[hb:hb + 64, p, :], True, True,
                       [keT.b, qeT.b], [pa.b])
              if "atmask" not in os.environ.get("KSKIP", ""):
                  for hh in range(2):
                      k.tt("dve", ATm.ap[:, hh, :, :], pM[2 + hh].ap[:, 0:256].rearrange("p (a t) -> p a t", a=2),
                           bc_mid(U_f.ap, 2), ALU.mult, [pM[2 + hh].b, U_f.b], [ATm.b], accum=(hh == 1))
              po = pM[3]
              for c in range(2):
                  r0 = c * 64
                  nreal = 16 if smp else 64
                  if smp:
                      seq = idx * 2 + c
                      for hh in range(2):
                          k.dma("sp", S_f.ap[hh * 64:(hh + 1) * 64, :, :],
                                st_gla[l, seq, :, :, :].rearrange("(p hh) d v -> hh d p v", hh=2)[hh],
                                [], [S_f.b], accum=(hh == 1))
                      k.cp("dve", S_b.ap, S_f.ap, [S_f.b], [S_b.b])
                  for h in range(4):
                      p, hb = h // 2, (h % 2) * 64
                      k.mm(po.ap[r0:r0 + 64, h * 64:(h + 1) * 64], ATm.ap[:, h % 2, h // 2, r0:r0 + 64], gv.ap[:, h * 64:(h + 1) * 64],
                           True, False, [ATm.b, gv.b], [po.b])
                      k.mm(po.ap[r0:r0 + 64, h * 64:(h + 1) * 64], qeT.ap[hb:hb + 64, p, r0:r0 + 64],
                           S_b.ap[hb:hb + 64, p, :], False, True, [qeT.b, S_b.b], [po.b])
                  for h in range(4):
                      p, hb = h // 2, (h % 2) * 64
                      k.mm(pc.ap[hb:hb + 64, 256 + p * 64:256 + (p + 1) * 64], ke.ap[r0:r0 + nreal, h * 64:(h + 1) * 64],
                           gv.ap[r0:r0 + nreal, h * 64:(h + 1) * 64], True, True, [ke.b, gv.b], [pc.b])
                  k.tt("dve", S_t.ap, pc.ap[:, 256:384].rearrange("p (a v) -> p a v", a=2), S_f.ap, ALU.add,
                       [pc.b, S_f.b], [S_t.b])
                  k.tt("dve", S_b.ap, S_t.ap, bc_last(ebl.ap[:, :, c], 64), ALU.mult, [S_t.b, ebl.b], [S_b.b])
                  k.tt("dve", S_f.ap, S_t.ap, bc_last(ebl.ap[:, :, c], 64), ALU.mult, [S_t.b, ebl.b], [S_f.b])
                  if smp:
                      seq = idx * 2 + c
                      for hh in range(2):
                          k.dma("sp", s_gla[l, seq, :, :, :].rearrange("(p hh) d v -> hh d p v", hh=2)[hh],
                                S_f.ap[hh * 64:(hh + 1) * 64, :, :], [S_f.b], [])
              if (not smp) and idx == NB - 1:
                  for hh in range(2):
                      k.dma("sp", p_gla[l, :, :, :].rearrange("(p hh) d v -> hh d p v", hh=2)[hh],
                            S_f.ap[hh * 64:(hh + 1) * 64, :, :], [S_f.b], [])
              adv()
              k.cp("act", o_sb.ap, po.ap[:, 0:256].rearrange("p (h v) -> p h v", h=4), [po.b], [o_sb.b])
              k.tt("dve", o_sq.ap, o_sb.ap, o_sb.ap, ALU.mult, [o_sb.b], [o_sq.b])
              k.P.op("dve", lambda e: e.tensor_reduce(o_ss.ap, o_sq.ap, AX.X, ALU.add), reads=[o_sq.b], writes=[o_ss.b])
              k.rsqrt(o_r.ap, o_ss.ap, 64.0 * EPS, TL2(lnt, 4, 8), [o_ss.b], [o_r.b])
              k.tt("dve", o_sq.ap, o_sb.ap, bc_last(o_r.ap, 64), ALU.mult, [o_sb.b, o_r.b], [o_sq.b])
              k.tt("dve", o_sb.ap.rearrange("p h v -> p (h v)"), o_sq.ap.rearrange("p h v -> p (h v)"), gn_bc.ap,
                   ALU.mult, [o_sq.b, gn_bc.b], [o_sb.b])
              mg = TL(merged_s.ap[:, idx, :]) if smp else merged
              if smp:
                  mg.b = merged_s.b
              k.tt("dve", mg.ap[:, 384:640], o_sb.ap.rearrange("p h v -> p (h v)"), gate.ap, ALU.mult,
                   [o_sb.b, gate.b], [mg.b], accum=True)
              adv()
              slot = idx % 5 if not smp else 0
              Vr = Vring[vslot]
              Kr = Kring[slot]
              for p in range(3):
                  k.tr(pT2.ap[:, p * 128:(p + 1) * 128], cqb.ap[:, p * 128:(p + 1) * 128], ident_b.ap,
                       [cqb.b, ident_b.b], [pT2.b])
                  k.tr(pT2.ap[:, 384 + p * 128:512 + p * 128], ckb.ap[:, p * 128:(p + 1) * 128], ident_b.ap,
                       [ckb.b], [pT2.b])
              k.cp("act", cqT.ap, pT2.ap[:, 0:384].rearrange("p (a t) -> p a t", a=3), [pT2.b], [cqT.b])
              k.cp("dve", Kr.ap, pT2.ap[:, 384:768].rearrange("p (a t) -> p a t", a=3), [pT2.b], [Kr.b])
              if not smp:
                  b = idx
                  wins = [w for w in range(5) if b - 4 + w >= 0]
                  pov = pT2f
                  w0 = wins[0]

                  def ca_S(h):
                      p, hb = h // 2, (h % 2) * 64
                      pca = pCA[h % 2]
                      for w in wins:
                          kb = Kring[(b - 4 + w) % 5]
                          k.mm(pca.ap[:, w * 128:(w + 1) * 128], kb.ap[hb:hb + 64, p, :], cqT.ap[hb:hb + 64, p, :],
                               True, True, [kb.b, cqT.b], [pca.b, pM[1 + 2 * (h % 2)].b])

                  def ca_E_PV(h):
                      pca = pCA[h % 2]
                      pt = PT[h % 2]
                      k.act(pt.ap[:, w0:5, :], pca.ap[:, w0 * 128:640].rearrange("p (w t) -> p w t", t=128), AF.Exp,
                            [pca.b, pM[1 + 2 * (h % 2)].b], [pt.b])
                      k.tt("dve", pt.ap[:, w0:5, :], pt.ap[:, w0:5, :],
                           BT.ap[:, h, w0 * 128:640].rearrange("p (w t) -> p w t", t=128), ALU.mult,
                           [pt.b, BT.b], [pt.b])
                      for w in wins:
                          vb = Vring[(b - 4 + w) % 6]
                          k.mm(pov.ap[:, h * 65:(h + 1) * 65], pt.ap[:, w, :], vb.ap[:, h, :], w == w0, w == 4,
                               [pt.b, vb.b], [pov.b])

                  ca_S(0)
                  for h in range(6):
                      adv()
                      if h + 1 < 6:
                          ca_S(h + 1)
                      ca_E_PV(h)
                  k.P.op("dve", lambda e: e.reciprocal(rden.ap, pov.ap[:, 0:390].rearrange("p (h e) -> p h e", h=6)[:, :, 64]),
                         reads=[pov.b], writes=[rden.b])
                  k.tt("dve", merged.ap[:, 640:1024].rearrange("p (h e) -> p h e", h=6),
                       pov.ap[:, 0:390].rearrange("p (h e) -> p h e", h=6)[:, :, 0:64], bc_last(rden.ap, 64), ALU.mult,
                       [pov.b, rden.b], [merged.b], accum=True)
                  k.dma("sp", MERGED[b * 128:(b + 1) * 128, 384:1024], merged.ap[:, 384:1024], [merged.b], [])
              else:
                  sample_mixers(idx, Kr, Vr)

          def sample_mixers(t, Kr, Vr):
              for j in range(2):
                  seq = 2 * t + j
                  r0 = 64 * j
                  k.dma("pool", sc_cak.ap, c_cak[l, seq, :, :].rearrange("(w p) c -> p w c", p=128), [], [sc_cak.b])
                  for w in range(4):
                      k.dma("pool", sc_caV.ap[:, w, :, 0:64],
                            c_cav[l, seq, w * 128:(w + 1) * 128, :].rearrange("p (h e) -> p h e", h=6), [], [sc_caV.b],
                            accum=True)
                  for w in range(4):
                      for p in range(3):
                          k.tr(pT2.ap[:, p * 128:(p + 1) * 128], sc_cak.ap[:, w, p * 128:(p + 1) * 128], ident_b.ap,
                               [sc_cak.b, ident_b.b], [pT2.b])
                      k.cp("act", sc_caKT.ap[:, :, w * 128:(w + 1) * 128],
                           pT2.ap[:, 0:384].rearrange("p (a t) -> p a t", a=3), [pT2.b], [sc_caKT.b], accum=True)
                  pssb = [pM[0], pM[3]]
                  pso = pM[1]
                  svb = [pb_.ap[:, 0:240].rearrange("p (h w q) -> p h w q", h=3, w=5) for pb_ in pssb]
                  for h in range(6):
                      p, hb = h // 2, (h % 2) * 64
                      sv = svb[h % 2]
                      pb_ = pssb[h % 2]
                      for w in range(4):
                          k.mm(sv[:, p, w, :], sc_caKT.ap[hb:hb + 64, p, w * 128:(w + 1) * 128],
                               cqT.ap[hb:hb + 64, p, r0:r0 + 16], True, False, [sc_caKT.b, cqT.b], [pb_.b])
                          k.mm(sv[:, p, w, :], BTs.ap[hb:hb + 16, h, w * 128:(w + 1) * 128], J16[hb], False, True,
                               [BTs.b, J_b.b], [pb_.b])
                      k.mm(sv[r0:r0 + 16, p, 4, :], Kr.ap[hb:hb + 64, p, r0:r0 + 16], cqT.ap[hb:hb + 64, p, r0:r0 + 16],
                           True, False, [Kr.b, cqT.b], [pb_.b])
                      k.mm(sv[r0:r0 + 16, p, 4, :], BTs.ap[hb:hb + 16, h, 512:528], J16[hb], False, True,
                           [BTs.b, J_b.b], [pb_.b])
                  for par in range(2):
                      pt = sPT.ap[:, 3 * par:3 * par + 3, 0:5, :]
                      k.act(pt[:, :, 0:4, :], svb[par][:, :, 0:4, :], AF.Exp, [pssb[par].b], [sPT.b], accum=(par > 0))
                      k.act(pt[r0:r0 + 16, :, 4, :], svb[par][r0:r0 + 16, :, 4, :], AF.Exp, [pssb[par].b], [sPT.b],
                            accum=True)
                  ov = pso.ap[:, 0:390].rearrange("p (h e) -> p h e", h=6)
                  for h in range(6):
                      s_ = (h % 2) * 3 + h // 2
                      for w in range(4):
                          k.mm(ov[r0:r0 + 16, h, :], sPT.ap[:, s_, w, :], sc_caV.ap[:, w, h, :], w == 0, False,
                               [sPT.b, sc_caV.b], [pso.b])
                      k.mm(ov[r0:r0 + 16, h, :], sPT.ap[r0:r0 + 16, s_, 4, :], Vr.ap[r0:r0 + 16, h, :], False, True,
                           [sPT.b, Vr.b], [pso.b])
                  k.P.op("dve", lambda e, ov=ov, r0=r0: e.reciprocal(rden.ap[r0:r0 + 16, :], ov[r0:r0 + 16, :, 64]),
                         reads=[pso.b], writes=[rden.b])
                  k.tt("dve", merged_s.ap[r0:r0 + 16, t, 640:1024].rearrange("p (h e) -> p h e", h=6),
                       ov[r0:r0 + 16, :, 0:64], bc_last(rden.ap[r0:r0 + 16, :], 64), ALU.mult,
                       [pso.b, rden.b], [merged_s.b], accum=True)
                  k.dma("pool", sc_ckv.ap, c_ckv[l, seq, :, :].rearrange("(w p) c -> p w c", p=128), [], [sc_ckv.b])
                  for r in range(2):
                      k.dma("pool", sc_kr3.ap[:, :, 64 * r:64 * r + 32],
                            c_kr[l, seq, :, :].rearrange("(w p) c -> p w c", p=128), [], [sc_kr3.b], accum=True)
                  for w in range(8):
                      k.tr(pT1.ap[:, w * 128:(w + 1) * 128], sc_ckv.ap[:, w, :], ident_b.ap, [sc_ckv.b, ident_b.b],
                           [pT1.b])
                  k.cp("act", sc_ckvT.ap, pT1.ap, [pT1.b], [sc_ckvT.b])
                  for w in range(8):
                      k.tr(pT2.ap[:, w * 128:(w + 1) * 128], sc_kr3.ap[:, w, :],
                           ident_b.ap, [sc_kr3.b, ident_b.b], [pT2.b])
                  k.cp("dve", sc_KR.ap, pT2.ap, [pT2.b], [sc_KR.b])
                  for p in range(3):
                      for hf in range(2):
                          pk = pM[2 + hf]
                          k.mm(pk.ap, W_kn.ap[:, 0, p * 128:(p + 1) * 128], sc_ckvT.ap[:, hf * 512:(hf + 1) * 512], True,
                               True, [W_kn.b, sc_ckvT.b], [pk.b])
                          k.cp("act" if hf else "dve", sc_KN.ap[:, p, hf * 512:(hf + 1) * 512], pk.ap, [pk.b], [sc_KN.b],
                               accum=True)
                  for w in range(8):
                      pk = pM[2 + w % 2]
                      k.mm(pk.ap[:, 0:384], sc_ckvT.ap[:, w * 128:(w + 1) * 128], W_v.ap[:, 0, :], True, True,
                           [sc_ckvT.b, W_v.b], [pk.b])
                      k.cp("act" if w % 2 else "dve", sc_V.ap[:, w, :, 0:64],
                           pk.ap[:, 0:384].rearrange("p (h e) -> p h e", h=6), [pk.b], [sc_V.b], accum=True)
                  pss2 = [pM[0], pM[1]]
                  sv2b = [pb_.ap[:, 0:432].rearrange("p (h w q) -> p h w q", h=3, w=9) for pb_ in pss2]
                  for h in range(6):
                      p, hb = h // 2, (h % 2) * 64
                      pb = pss2[h % 2]
                      sv2 = sv2b[h % 2]
                      for w in range(9):
                          if w < 8:
                              out = sv2[:, p, w, :]
                              kn = sc_KN.ap[hb:hb + 64, p, w * 128:(w + 1) * 128]
                              kr_ = sc_KR.ap[hb:hb + 32, w * 128:(w + 1) * 128]
                              rd = [sc_KN.b, sc_KR.b]
                          else:
                              out = sv2[r0:r0 + 16, p, 8, :]
                              kn = KNs.ap[hb:hb + 64, p, r0:r0 + 16]
                              kr_ = KRs.ap[hb:hb + 32, r0:r0 + 16]
                              rd = [KNs.b, KRs.b]
                          k.mm(out, kn, QNs.ap[hb:hb + 64, p, r0:r0 + 16], True, False, rd + [QNs.b], [pb.b])
                          k.mm(out, kr_, QRs.ap[hb:hb + 32, p, r0:r0 + 16], False, True, rd + [QRs.b], [pb.b])
                  sc = 96.0 ** -0.5
                  for par in range(2):
                      pb = pss2[par]
                      sv2 = sv2b[par]
                      k.act(sPT.ap[:, 3 * par:3 * par + 3, 0:8, :], sv2[:, :, 0:8, :], AF.Exp, [pb.b], [sPT.b],
                            scale=sc, accum=(par > 0))
                      k.act(sPT.ap[r0:r0 + 16, 3 * par:3 * par + 3, 8, :], sv2[r0:r0 + 16, :, 8, :], AF.Exp, [pb.b],
                            [sPT.b], scale=sc, accum=True)
                  pso2 = pM[2]
                  ov2 = pso2.ap[:, 0:390].rearrange("p (h e) -> p h e", h=6)
                  for h in range(6):
                      s_ = (h % 2) * 3 + h // 2
                      for w in range(8):
                          k.mm(ov2[r0:r0 + 16, h, :], sPT.ap[:, s_, w, :], sc_V.ap[:, w, h, :], w == 0, False,
                               [sPT.b, sc_V.b], [pso2.b])
                      k.mm(ov2[r0:r0 + 16, h, :], sPT.ap[r0:r0 + 16, s_, 8, :], Vaug.ap[r0:r0 + 16, h, :], False, True,
                           [sPT.b, Vaug.b], [pso2.b])
                  k.P.op("dve", lambda e, ov2=ov2, r0=r0: e.reciprocal(rden.ap[r0:r0 + 16, :], ov2[r0:r0 + 16, :, 64]),
                         reads=[pso2.b], writes=[rden.b])
                  k.tt("dve", merged_s.ap[r0:r0 + 16, t, 0:384].rearrange("p (h e) -> p h e", h=6),
                       ov2[r0:r0 + 16, :, 0:64], bc_last(rden.ap[r0:r0 + 16, :], 64), ALU.mult,
                       [pso2.b, rden.b], [merged_s.b], accum=True)

          seq = [("p", b) for b in range(NB)] + [("s", 0), ("s", 1)]

          def load_X(i):
              kind, idx = seq[i]
              if kind == "p":
                  k.dma("sp", Xt[idx % 2].ap, x_src[idx * 128:(idx + 1) * 128, :], [], [Xt[idx % 2].b])
                  k.dma("sp", cs_blk[idx % 2].ap, cs_p[idx * 128:(idx + 1) * 128, :], [], [cs_blk[idx % 2].b])

          load_X(0)
          for _ in front_gen(seq[0][0], seq[0][1], 0, 0):
              pass
          for i, (kind, idx) in enumerate(seq):
              nxt = None
              if i + 1 < len(seq):
                  load_X(i + 1)
                  nxt = front_gen(seq[i + 1][0], seq[i + 1][1], (i + 1) % 2, (i + 1) % 6)

              def adv(nxt=nxt):
                  if nxt is not None:
                      next(nxt, None)
              a1_block(kind, idx, i % 2, i % 6, adv)
              if nxt is not None:
                  for _ in nxt:
                      pass
          P.barrier()

          A.release(persist_mark)
          NG = T // 512
          KTr = sb([128, 3, T], BF16)
          VAr = sb([128, NB, 384], BF16)
          kvb = [Buf() for _ in range(NG)]
          Qg = [sb([128, 3, 512], BF16) for _ in range(2)]
          oT = [sb([65, 512], F32) for _ in range(2)]
          om = [sb([128, 4, 192], BF16) for _ in range(2)]
          mfull = [sb([128, 4, D], BF16) for _ in range(2)]
          rd2 = sb([128, 4], F32)
          W_o = sb([128, 8, 1024], BF16)
          load_weight(W_o.ap, W_o.b, 8, 1024, lambda c, s0, ln: w_out[l, c * 128:(c + 1) * 128, s0:s0 + ln], None)
          Xq = [sb([128, D], F32) for _ in range(2)]
          mT = sb([128, 8, 128], BF16)
          PTt = [sb([128, 1024], BF16) for _ in range(4)]
          Sreg = []
          for a_ in (0, 2, 4):
              t_ = TL(ps_t[:, 512 * a_:512 * a_ + 1024])
              t_.b.locks = (bank_locks[a_], bank_locks[a_ + 1])
              Sreg.append(t_)
          pO = [TLP(6)]
          pTr = TLP(7)
          pTm = TLP(7, True)
          pWo = TLP(7)
          sc = 96.0 ** -0.5

          def wo_block(mparts, xv, xb, xreads):
            c = 0
            for (ap, buf) in mparts:
                w = ap.shape[1]
                for cc in range(w // 128):
                    k.tr(pTm.ap[:, c * 128:(c + 1) * 128], ap[:, cc * 128:(cc + 1) * 128], ident_b.ap,
                         [buf, ident_b.b], [pTm.b])
                    c += 1
            assert c == 8
            k.cp("dve", mT.ap, pTm.ap.rearrange("p (c t) -> p c t", c=8), [pTm.b], [mT.b])
            for n in range(2):
                for c in range(8):
                    k.mm(pWo.ap, mT.ap[:, c, :], W_o.ap[:, c, n * 512:(n + 1) * 512], c == 0, c == 7,
                         [mT.b, W_o.b], [pWo.b])
                k.tt("dve", xv[:, n * 512:(n + 1) * 512], pWo.ap, xv[:, n * 512:(n + 1) * 512], ALU.add,
                     [pWo.b, xb] + xreads, [xb], accum=True)

          chk("a2w")
          hi = [0]
          xq_i2 = [0]
          ui = [0]
          for hp in range(2):
              def a2_load(g, hp=hp):
                  t0 = g * 512
                  k.dma("sp", KTr.ap[:, :, t0:t0 + 512], KT[3 * hp:3 * hp + 3, :, t0:t0 + 512].rearrange("a p t -> p a t"),
                        [], [kvb[g]])
                  k.dma("sp", VAr.ap[:, 4 * g:4 * g + 4, :],
                        VA[t0:t0 + 512, 384 * hp:384 * hp + 384].rearrange("(w p) c -> p w c", p=128), [], [kvb[g]],
                        accum=True)
                  k.dma("sp", Qg[g % 2].ap, QT[3 * hp:3 * hp + 3, :, t0:t0 + 512].rearrange("a p t -> p a t"), [],
                        [Qg[g % 2].b])
                  if hp == 1:
                      mf = mfull[g % 2]
                      k.dma("sp", mf.ap[:, :, 0:192], MERGED[t0:t0 + 512, 0:192].rearrange("(a p) c -> p a c", p=128),
                            [], [mf.b])
                      k.dma("sp", mf.ap[:, :, 384:1024],
                            MERGED[t0:t0 + 512, 384:1024].rearrange("(a p) c -> p a c", p=128), [], [mf.b], accum=True)

              units = []
              for g in range(NG):
                  for hl in range(3):
                      us = []
                      for kt in range(0, 4 * g, 2):
                          us.append([g, hl, [kt, kt + 1], 0, False, False])
                      for j in range(4):
                          us.append([g, hl, [4 * g + j], 128 * j, False, False])
                      us[0][4] = True
                      us[-1][5] = True
                      units.extend(us)

              def emitS(i, units=units):
                  g, hl, kts, c0, first, last = units[i]
                  sr = Sreg[(ui[0] + i) % 3]
                  qg = Qg[g % 2]
                  for n_, kt in enumerate(kts):
                      k.mm(sr.ap[:, n_ * 512 + c0:(n_ + 1) * 512], KTr.ap[:, hl, kt * 128:(kt + 1) * 128],
                           qg.ap[:, hl, c0:512], True, True, [kvb[kt // 4], qg.b], [sr.b])

              def emitE_PV(i, units=units, hp=hp):
                  g, hl, kts, c0, first, last = units[i]
                  sr = Sreg[(ui[0] + i) % 3]
                  pt = PTt[(ui[0] + i) % 4]
                  n = len(kts)
                  k.act(pt.ap[:, c0:n * 512], sr.ap[:, c0:n * 512], AF.Exp, [sr.b], [pt.b], scale=sc)
                  if kts[0] >= 4 * g:
                      k.memset("pool", pt.ap[64:128, c0:c0 + 64], 0.0, [pt.b])
                  po = pO[0]
                  for n_, kt in enumerate(kts):
                      k.mm(po.ap[:, c0:512], VAr.ap[:, kt, hl * 128:(hl + 1) * 128],
                           pt.ap[:, n_ * 512 + c0:(n_ + 1) * 512],
                           first and n_ == 0, last and n_ == n - 1, [kvb[kt // 4], pt.b], [po.b])
                  if last:
                      ot = oT[hi[0] % 2]
                      hi[0] += 1
                      k.cp("dve", ot.ap, po.ap[0:65, :], [po.b], [ot.b])
                      for q4 in range(4):
                          k.tr(pTr.ap[:, q4 * 65:(q4 + 1) * 65], ot.ap[:, q4 * 128:(q4 + 1) * 128], ident_f.ap[0:65, 0:65],
                               [ot.b, ident_f.b], [pTr.b])
                      tv = pTr.ap[:, 0:260].rearrange("p (a e) -> p a e", a=4)
                      k.P.op("dve", lambda e, tv=tv: e.reciprocal(rd2.ap, tv[:, :, 64]), reads=[pTr.b], writes=[rd2.b])
                      if hp == 0:
                          dst, dbuf = om[g % 2].ap[:, :, hl * 64:(hl + 1) * 64], om[g % 2].b
                      else:
                          dst, dbuf = mfull[g % 2].ap[:, :, 192 + hl * 64:192 + (hl + 1) * 64], mfull[g % 2].b
                      k.tt("dve", dst, tv[:, :, 0:64], bc_last(rd2.ap, 64), ALU.mult, [pTr.b, rd2.b], [dbuf], accum=True)
                      if hl == 2 and hp == 0:
                          k.dma("sp", MERGED[g * 512:(g + 1) * 512, 0:192].rearrange("(a p) c -> p a c", p=128),
                                om[g % 2].ap, [om[g % 2].b], [])
                      if hl == 2 and hp == 1:
                          mf = mfull[g % 2]
                          if dbg:
                              k.dma("sp", MERGED[g * 512:(g + 1) * 512, 192:384].rearrange("(a p) c -> p a c", p=128),
                                    mf.ap[:, :, 192:384], [mf.b], [])
                          for q4 in range(4):
                              xq = Xq[xq_i2[0] % 2]
                              xq_i2[0] += 1
                              r0 = g * 512 + q4 * 128
                              k.dma("sp", xq.ap, x_src[r0:r0 + 128, :], [], [xq.b])
                              wo_block([(mf.ap[:, q4, :], mf.b)], xq.ap, xq.b, [])
                              k.dma("sp", XB[r0:r0 + 128, :], xq.ap, [xq.b], [])

              a2_load(0)
              if NG > 1:
                  a2_load(1)
              emitS(0)
              if len(units) > 1:
                  emitS(1)
              for i in range(len(units)):
                  g, hl, kts, c0, first, last = units[i]
                  if first and hl == 0 and g >= 1 and g + 1 < NG:
                      a2_load(g + 1)
                  if i + 2 < len(units):
                      emitS(i + 2)
                  emitE_PV(i)
              ui[0] += len(units)
              P.barrier()
          chk("a2p")
          for t in range(2):
              wo_block([(merged_s.ap[:, t, :], merged_s.b)], x_s.ap[:, t, :], x_s.b, [])
          P.barrier()
          chk("a2")

          A.release(persist_mark)
          W_u = sb([128, 8, 4096], BF16)
          W_d = sb([128, 32, 1024], BF16)
          n2T = sb([128, 8], F32)
          k.dma("sp", n2T.ap, norm2T[l, :, :], [], [n2T.b])
          Xg = [sb([128, 2, D], F32) for _ in range(2)]
          bst = list(stage) + [TL(Xg[i_].ap[:, s_, :]) for i_ in range(2) for s_ in range(2)]
          load_weight(W_u.ap, W_u.b, 8, 4096, lambda c, s0, ln: w_up[l, c * 128:(c + 1) * 128, s0:s0 + ln],
                      lambda c: (n2T.ap[:, c:c + 1], n2T.b), stages=bst, engs=("pool", "dve", "act"))
          load_weight(W_d.ap, W_d.b, 32, 1024, lambda c, s0, ln: w_down[l, c * 128:(c + 1) * 128, s0:s0 + ln], None,
                      stages=bst, engs=("pool", "dve", "act"))
          P.barrier()
          xn2 = sb([128, D], BF16)
          hT2 = sb([128, 8, 256], BF16)
          rl = [sb([128, 256], BF16) for _ in range(2)]
          uT = sb([128, 32, 256], BF16)
          junk2 = sb([128, D], BF16)
          ssb = sb([128, 4], F32)
          rsb = sb([128, 4], F32)
          lnt2 = sb([128, 4], F32)
          if last_layer:
              yo = sb([128, D], F32)
              fn_bc = sb([128, D], F32)
              k.dma("sp", fn_bc.ap, fnorm.ap().partition_broadcast(128), [], [fn_bc.b])
          pTb = [TLP(0, True), TLP(1, True)]
          pW = [TLP(2), TLP(3)]
          pD = [TLP(4), TLP(5), TLP(6), TLP(7)]
          NGB = T // 256

          def b_load(g):
              k.dma("sp", Xg[g % 2].ap, XB[g * 256:(g + 1) * 256, :].rearrange("(s p) c -> p s c", p=128), [],
                    [Xg[g % 2].b])

          wi = [0]
          di = [0]

          def b_group(Xv, Xb, out_fn):
              for s in range(2):
                  k.act(junk2.ap, Xv[:, s, :], AF.Square, [Xb], [junk2.b, ssb.b], scale=1.0 / 32.0,
                        accum_out=ssb.ap[:, s:s + 1])
                  k.rsqrt(rsb.ap[:, s:s + 1], ssb.ap[:, s:s + 1], EPS, TL2(lnt2, s, s + 1), [ssb.b], [rsb.b])
                  k.ts("dve", xn2.ap, Xv[:, s, :], rsb.ap[:, s:s + 1], None, ALU.mult, None, [Xb, rsb.b], [xn2.b])
                  pt = pTb[s]
                  for c in range(8):
                      k.tr(pt.ap[:, c * 128:(c + 1) * 128], xn2.ap[:, c * 128:(c + 1) * 128], ident_b.ap,
                           [xn2.b, ident_b.b], [pt.b])
                  k.cp("act", hT2.ap[:, :, s * 128:(s + 1) * 128], pt.ap.rearrange("p (c t) -> p c t", c=8), [pt.b],
                       [hT2.b], accum=True)
              for f in range(32):
                  pw = pW[wi[0] % 2]
                  wi[0] += 1
                  for c in range(8):
                      k.mm(pw.ap[:, 0:256], W_u.ap[:, c, f * 128:(f + 1) * 128], hT2.ap[:, c, :], c == 0, c == 7,
                           [W_u.b, hT2.b], [pw.b])
                  r = rl[f % 2]
                  k.act(r.ap, pw.ap[:, 0:256], AF.Relu, [pw.b], [r.b])
                  k.tt("pool" if f % 2 else "dve", uT.ap[:, f, :], r.ap, r.ap, ALU.mult, [r.b], [uT.b], accum=True)
              for s in range(2):
                  for n in range(2):
                      pd = pD[di[0] % 4]
                      di[0] += 1
                      for f in range(32):
                          k.mm(pd.ap, uT.ap[:, f, s * 128:(s + 1) * 128], W_d.ap[:, f, n * 512:(n + 1) * 512], f == 0,
                               f == 31, [uT.b, W_d.b], [pd.b])
                      k.tt("dve", Xv[:, s, n * 512:(n + 1) * 512], pd.ap, Xv[:, s, n * 512:(n + 1) * 512], ALU.add,
                           [pd.b, Xb], [Xb], accum=True)
              for s in range(2):
                  if last_layer:
                      k.act(junk2.ap, Xv[:, s, :], AF.Square, [Xb], [junk2.b, ssb.b], scale=1.0 / 32.0,
                            accum_out=ssb.ap[:, 2 + s:3 + s])
                      k.rsqrt(rsb.ap[:, 2 + s:3 + s], ssb.ap[:, 2 + s:3 + s], EPS, TL2(lnt2, 2 + s, 3 + s), [ssb.b], [rsb.b])
                      k.stt("dve", yo.ap, Xv[:, s, :], rsb.ap[:, 2 + s:3 + s], fn_bc.ap, ALU.mult, ALU.mult,
                            [Xb, rsb.b, fn_bc.b], [yo.b])
                      out_fn(s, yo.ap, yo.b, True)
                  else:
                      out_fn(s, Xv[:, s, :], Xb, False)

          b_load(0)
          for g in range(NGB):
              if g + 1 < NGB:
                  b_load(g + 1)

              def outp(s, ap, buf, final, g=g):
                  dst = y_p if final else XB
                  k.dma("sp", dst[g * 256 + s * 128:g * 256 + (s + 1) * 128, :], ap, [buf], [])
              b_group(Xg[g % 2].ap, Xg[g % 2].b, outp)

          def outs(s, ap, buf, final):
              if final:
                  k.dma("sp", y_s[s * 128:(s + 1) * 128, :], ap, [buf], [])
          b_group(x_s.ap, x_s.b, outs)
          P.barrier()


    except StopBuild:
        pass
    P.finalize()
    P.emit()
    es.close()
    return k


def host_inputs(T, L, core, x_prompt, x_sample, cache_mla_ckv, cache_mla_krope, state_gla, cache_ca_k, cache_ca_v,
                norm1, w_in, mla_q_norm, mla_w_qup, mla_kv_norm, mla_w_kvup, gla_w_gate2, gla_gate_bias,
                gla_out_norm, ca_rel_bias, w_out, norm2, w_up, w_down, final_norm, consts):
    f = lambda a: np.ascontiguousarray(np.asarray(a, dtype=np.float32))
    c = core
    sq = slice(4 * c, 4 * c + 4)
    xs = np.zeros((2, 2, 64, D), np.float32)
    xs[:, :, 0:16, :] = np.asarray(x_sample[sq]).reshape(2, 2, 16, D)
    m = {
        "xp": f(x_prompt[c % x_prompt.shape[0]][:T]),
        "xs": f(xs.reshape(256, D)),
        "c_ckv": f(cache_mla_ckv[:L, sq]),
        "c_kr": f(cache_mla_krope[:L, sq]),
        "st_gla": f(state_gla[:L, sq]),
        "c_cak": f(np.asarray(cache_ca_k[:L, sq]).reshape(L, 4, 512, 384)),
        "c_cav": f(np.asarray(cache_ca_v[:L, sq]).reshape(L, 4, 512, 384)),
        "w_in": f(w_in[:L]), "w_qup": f(mla_w_qup[:L]), "w_kvup": f(mla_w_kvup[:L]),
        "w_gate2": f(gla_w_gate2[:L]), "gate_bias": f(gla_gate_bias[:L]),
        "w_out": f(w_out[:L]), "w_up": f(w_up[:L]), "w_down": f(w_down[:L]),
        "norm1T": f(np.asarray(norm1[:L]).reshape(L, 8, 128).transpose(0, 2, 1)),
        "norm2T": f(np.asarray(norm2[:L]).reshape(L, 8, 128).transpose(0, 2, 1)),
        "qnormT": f(np.asarray(mla_q_norm[:L]).reshape(L, 2, 128).transpose(0, 2, 1)),
        "kvnorm": f(mla_kv_norm[:L]), "glanorm": f(gla_out_norm[:L]), "fnorm": f(final_norm),
    }
    idx = np.clip(np.arange(767) - 639, -128, 128) + 128
    m["ext2"] = f(np.asarray(ca_rel_bias[:L])[:, idx, :].transpose(0, 2, 1))
    m.update(consts)
    return m


def host_consts(T):
    ident = np.eye(128, dtype=np.float32)
    J = np.ascontiguousarray(ident[::-1])
    j = np.arange(128)
    U = ((j[:, None] <= j[None, :]) & (j[:, None] // 64 == j[None, :] // 64)).astype(np.float32)
    inv = np.power(np.float32(10000.0), -np.arange(16, dtype=np.float32) / np.float32(16))

    def table(pos):
        ang = pos.astype(np.float32)[:, None] * inv[None, :].astype(np.float32)
        c, s = np.cos(ang).astype(np.float32), np.sin(ang).astype(np.float32)
        return np.ascontiguousarray(np.concatenate([c, c, -s, s], axis=1).astype(np.float32))
    cs_p = table(np.arange(T))
    cs_s = table(1024 + (np.arange(128) % 64))
    return {"c_ident": ident, "c_J": J, "c_U": U, "cs_p": cs_p, "cs_s": cs_s}


_CACHE = {}


def run(T, L, dbg, inputs):
    key = (T, L, dbg)
    if key not in _CACHE:
        _CACHE[key] = build_program(T, L, dbg)
    kk = _CACHE[key]
    consts = host_consts(T)
    in_maps = [host_inputs(T, L, c, consts=consts, **inputs) for c in range(NCORES)]
    res = run_bass_kernel_spmd(kk.nc, in_maps, core_ids=list(range(NCORES)))
    return res.results


def kernel(**inputs):
    T, L = SEQ, DEPTH
    R = run(T, L, False, inputs)
    B = 2
    y_prompt = np.stack([R[c]["y_p"] for c in range(B)])

    def samp(name, w):
        o = np.stack([R[c][name] for c in range(NCORES)])
        return o

    ys = np.stack([R[c]["y_s"] for c in range(NCORES)]).reshape(NCORES, 2, 2, 64, D)[:, :, :, 0:16, :]
    y_sample = ys.reshape(32, 16, D)
    p_ckv = np.stack([R[c]["p_ckv"] for c in range(B)], axis=1)
    p_kr = np.stack([R[c]["p_kr"] for c in range(B)], axis=1)
    p_gla = np.stack([R[c]["p_gla"] for c in range(B)], axis=1)
    p_cak = np.stack([R[c]["p_cak"] for c in range(B)], axis=1).reshape(L, B, 512, 6, 64)
    p_cav = np.stack([R[c]["p_cav"] for c in range(B)], axis=1).reshape(L, B, 512, 6, 64)

    def stile(name, w):
        o = np.stack([R[c][name] for c in range(NCORES)], axis=1)
        o = o.reshape(L, NCORES, 2, 2, 64, w)[:, :, :, :, 0:16, :]
        return np.ascontiguousarray(o.reshape(L, 32, 16, w))
    s_ckv = stile("s_ckv", 128)
    s_kr = stile("s_kr", 32)
    s_gla = np.stack([R[c]["s_gla"] for c in range(NCORES)], axis=1).reshape(L, 32, 4, 64, 64)
    s_cak = stile("s_cak", 384).reshape(L, 32, 16, 6, 64)
    s_cav = stile("s_cav", 384).reshape(L, 32, 16, 6, 64)
    outs = (y_prompt, y_sample, p_ckv, p_kr, p_gla, p_cak, p_cav, s_ckv, s_kr, s_gla, s_cak, s_cav)
    return tuple(np.ascontiguousarray(o.astype(np.float32)) for o in outs)
```

```python
import math
import numpy as np
import ml_dtypes
import concourse.bass as bass
import concourse.mybir as mybir
from concourse.ap import AP as RawAP
from concourse.bass_utils import run_bass_kernel_spmd

F32 = mybir.dt.float32
BF16 = mybir.dt.bfloat16
U8 = mybir.dt.uint8
AF = mybir.ActivationFunctionType
ALU = mybir.AluOpType
AX = mybir.AxisListType

D = 1024
DEPTH = 2
SEQ = 8192
EPS = 1e-6
NCORES = 8
SEM_EPOCH = 30000


class Buf:
    __slots__ = ("w", "r", "pr", "locks")

    def __init__(self):
        self.w = {}
        self.r = {}
        self.pr = {}
        self.locks = ()


class BankLock:
    __slots__ = ("last",)

    def __init__(self):
        self.last = None


class Op:
    __slots__ = ("stream", "dom", "fn", "deps", "n", "signal", "val", "semkey", "isdma", "key")


NDMASEM = {"sp": 32, "pool": 16, "act": 8}


class Prog:
    STREAMS = ("pe", "act", "dve", "pool", "sp")

    def __init__(self, nc):
        self.nc = nc
        self.ops = {s: [] for s in self.STREAMS}
        self.domops = {}
        self.pending = {s: {} for s in self.STREAMS}

    def op(self, stream, fn, reads=(), writes=(), dma=False, accum=False):
        o = Op()
        o.stream = stream
        o.isdma = dma
        o.dom = stream + ("_dma" if dma else "")
        o.fn = fn
        o.signal = False
        lst = self.domops.setdefault(o.dom, [])
        o.n = len(lst) + 1
        lst.append(o)
        o.key = (o.dom, o.n) if dma else o.dom
        deps = dict(self.pending[stream])
        self.pending[stream] = {}

        def add(d):
            for k, p in d.items():
                if p is o:
                    continue
                q = deps.get(k)
                if q is None or q.n < p.n:
                    deps[k] = p

        for b in reads:
            add(b.w)
        for b in writes:
            if b.r:
                add(b.r)
                add(b.w)
            elif accum:
                add(b.pr)
            else:
                add(b.w)
                add(b.pr)
        seen = set()
        for b in list(reads) + list(writes):
            for lk in b.locks:
                if id(lk) in seen:
                    continue
                seen.add(id(lk))
                p = lk.last
                if p is not None and p.stream != stream:
                    q = deps.get(p.key)
                    if q is None or q.n < p.n:
                        deps[p.key] = p
                lk.last = o
        if dma:
            N = NDMASEM[stream]
            if o.n > N:
                p = lst[o.n - N - 1]
                deps[p.key] = p
        if stream == "pe" and not dma:
            deps.pop("pe", None)
        o.deps = deps
        for p in deps.values():
            p.signal = True
        for b in reads:
            b.r[o.key] = o
        for b in writes:
            if b.r:
                b.pr = dict(b.r)
                b.pr.pop(o.key, None)
                b.w = {o.key: o}
                b.r = {}
            elif accum:
                b.w[o.key] = o
            else:
                b.w = {o.key: o}
        self.ops[stream].append(o)
        return o

    def barrier(self):
        snap = {}
        for dom, lst in self.domops.items():
            if dom.endswith("_dma"):
                N = NDMASEM[dom[:-4]]
                for p in lst[-N:]:
                    snap[p.key] = p
            elif lst:
                snap[lst[-1].key] = lst[-1]
        for s in self.STREAMS:
            for k, p in snap.items():
                q = self.pending[s].get(k)
                if q is None or q.n < p.n:
                    self.pending[s][k] = p

    def finalize(self):
        self.barrier()
        self.op("sp", None)
        self.semkeys = set()
        for dom, lst in self.domops.items():
            if dom.endswith("_dma"):
                N = NDMASEM[dom[:-4]]
                for o in lst:
                    o.signal = True
                    o.semkey = (dom, (o.n - 1) % N)
                    o.val = ((o.n - 1) // N + 1) * 16
                    self.semkeys.add(o.semkey)
            else:
                cnt = 0
                ep = 0
                for o in lst:
                    if o.signal:
                        if cnt >= SEM_EPOCH:
                            ep += 1
                            cnt = 0
                        cnt += 1
                        o.val = cnt
                        o.semkey = (dom, ep)
                        self.semkeys.add(o.semkey)

    def emit(self):
        nc = self.nc
        import contextlib
        with contextlib.ExitStack() as es:
            sems = {}
            for k in sorted(self.semkeys):
                sems[k] = es.enter_context(nc.semaphore("s_%s_%d" % k))
            block = es.enter_context(nc.Block())

            def run(stream, eng):
                known = {}
                for o in self.ops[stream]:
                    for p in o.deps.values():
                        if known.get(p.semkey, 0) >= p.val:
                            continue
                        known[p.semkey] = p.val
                        eng.wait_ge(sems[p.semkey], p.val)
                    if o.fn is None:
                        continue
                    ins = o.fn(eng)
                    if o.signal:
                        ins.then_inc(sems[o.semkey], 16 if o.isdma else 1)

            @block.tensor
            def _(e):
                run("pe", e)

            @block.scalar
            def _(e):
                run("act", e)

            @block.vector
            def _(e):
                run("dve", e)

            @block.gpsimd
            def _(e):
                run("pool", e)

            @block.sync
            def _(e):
                run("sp", e)


class Arena:
    def __init__(self, ap_u8, size):
        self.ap = ap_u8
        self.size = size
        self.off = 0

    def mark(self):
        return self.off

    def release(self, m):
        self.off = m

    def alloc(self, shape, dt):
        esz = 2 if dt == BF16 else 4
        free = 1
        for s in shape[1:]:
            free *= s
        nb = (free * esz + 63) // 64 * 64
        assert self.off + nb <= self.size, ("arena overflow", self.off, nb, self.size)
        v = self.ap[0:shape[0], self.off:self.off + free * esz].bitcast(dt)
        self.off += nb
        if len(shape) == 3:
            v = v.rearrange("p (a b) -> p a b", a=shape[1])
        elif len(shape) == 4:
            v = v.rearrange("p (a b c) -> p a b c", a=shape[1], b=shape[2])
        return v


class K:
    def __init__(self, T, depth=DEPTH, dbg=False):
        self.T = T
        self.NB = T // 128
        self.depth = depth
        self.dbg = dbg
        nc = bass.Bass("TRN2", target_bir_lowering=False)
        self.nc = nc
        self.P = Prog(nc)
        self.din = {}
        self.dout = {}

    def inp(self, name, shape, dt=F32):
        t = self.nc.dram_tensor(name, list(shape), dt, kind="ExternalInput")
        self.din[name] = t
        return t

    def outp(self, name, shape, dt=F32):
        t = self.nc.dram_tensor(name, list(shape), dt, kind="ExternalOutput")
        self.dout[name] = t
        return t

    def scratch(self, name, shape, dt):
        return self.nc.dram_tensor(name, list(shape), dt)

    def mm(self, out, lhsT, rhs, start, stop, reads, writes):
        return self.P.op("pe", lambda e: e.matmul(out, lhsT, rhs, start=start, stop=stop),
                         reads=reads, writes=writes, accum=True)

    def tr(self, out, in_, ident, reads, writes):
        return self.P.op("pe", lambda e: e.transpose(out, in_, ident), reads=reads, writes=writes, accum=True)

    def act(self, out, in_, func, reads, writes, bias=None, scale=None, accum_out=None, eng="act", accum=False):
        kw = {}
        if bias is not None:
            kw["bias"] = bias
        if scale is not None:
            kw["scale"] = scale
        if accum_out is not None:
            kw["accum_out"] = accum_out
        return self.P.op("act", lambda e: e.activation(out, in_, func, **kw), reads=reads, writes=writes, accum=accum)

    def ts(self, eng, out, in0, s1, s2, op0, op1, reads, writes, accum=False):
        if op1 is None:
            return self.P.op(eng, lambda e: e.tensor_scalar(out, in0, s1, None, op0), reads=reads, writes=writes, accum=accum)
        return self.P.op(eng, lambda e: e.tensor_scalar(out, in0, s1, s2, op0, op1), reads=reads, writes=writes, accum=accum)

    def tt(self, eng, out, in0, in1, op, reads, writes, accum=False):
        return self.P.op(eng, lambda e: e.tensor_tensor(out, in0, in1, op), reads=reads, writes=writes, accum=accum)

    def stt(self, eng, out, in0, scalar, in1, op0, op1, reads, writes, accum=False):
        return self.P.op(eng, lambda e: e.scalar_tensor_tensor(out, in0, scalar, in1, op0, op1),
                         reads=reads, writes=writes, accum=accum)

    def cp(self, eng, out, in_, reads, writes, accum=False):
        if eng == "act":
            return self.P.op("act", lambda e: e.copy(out, in_), reads=reads, writes=writes, accum=accum)
        return self.P.op(eng, lambda e: e.tensor_copy(out, in_), reads=reads, writes=writes, accum=accum)

    def rsqrt(self, out, in_, eps, tmp, reads, writes):
        self.act(tmp.ap if hasattr(tmp, "ap") else tmp, in_, AF.Ln, reads, [self._tmpb(tmp)], bias=eps)
        return self.act(out, tmp.ap if hasattr(tmp, "ap") else tmp, AF.Exp, [self._tmpb(tmp)], writes, scale=-0.5)

    def _tmpb(self, tmp):
        return tmp.b

    def memset(self, eng, ap, val, writes, accum=False):
        return self.P.op(eng, lambda e: e.memset(ap, val), writes=writes, accum=accum)

    def dma(self, q, out, in_, reads, writes, accum=False, slow=False):
        if slow:
            return self.P.op(q, lambda e: e.dma_start(out=out, in_=in_, allow_slow_non_contiguous=True),
                             reads=reads, writes=writes, dma=True, accum=accum)
        return self.P.op(q, lambda e: e.dma_start(out=out, in_=in_), reads=reads, writes=writes, dma=True, accum=accum)


class TL:
    __slots__ = ("ap", "b")

    def __init__(self, ap):
        self.ap = ap
        self.b = Buf()


def TL2(t, a, b):
    v = TL(t.ap[:, a:b])
    v.b = t.b
    return v


def bc_mid(ap2d, n):
    p, f = ap2d.shape
    return ap2d.unsqueeze(1).broadcast_to([p, n, f])


def bc_last(ap2d, n):
    p, a = ap2d.shape
    return ap2d.unsqueeze(2).broadcast_to([p, a, n])


W_IN_SEGS = [(0, 0, 416), (416, 1184, 16), (432, 416, 768), (1200, 1200, 1408)]


def seg_map(d0, d1, segs):
    out = []
    for (ds, ss, ln) in segs:
        a = max(d0, ds)
        b = min(d1, ds + ln)
        if a < b:
            out.append((a, ss + (a - ds), b - a))
    return out


def build_program(T, depth=DEPTH, dbg=False):
    k = K(T, depth, dbg)
    nc = k.nc
    P = k.P
    NB = T // 128
    L = depth
    assert T % 512 == 0

    xp = k.inp("xp", [T, D])
    xs = k.inp("xs", [256, D])
    c_ckv = k.inp("c_ckv", [L, 4, 1024, 128])
    c_kr = k.inp("c_kr", [L, 4, 1024, 32])
    st_gla = k.inp("st_gla", [L, 4, 4, 64, 64])
    c_cak = k.inp("c_cak", [L, 4, 512, 384])
    c_cav = k.inp("c_cav", [L, 4, 512, 384])
    w_in = k.inp("w_in", [L, 1024, 2608])
    w_qup = k.inp("w_qup", [L, 256, 576])
    w_kvup = k.inp("w_kvup", [L, 128, 768])
    w_gate2 = k.inp("w_gate2", [L, 16, 256])
    gate_bias = k.inp("gate_bias", [L, 256])
    w_out = k.inp("w_out", [L, 1024, 1024])
    w_up = k.inp("w_up", [L, 1024, 4096])
    w_down = k.inp("w_down", [L, 4096, 1024])
    norm1T = k.inp("norm1T", [L, 128, 8])
    norm2T = k.inp("norm2T", [L, 128, 8])
    qnormT = k.inp("qnormT", [L, 128, 2])
    kvnorm = k.inp("kvnorm", [L, 128])
    glanorm = k.inp("glanorm", [L, 256])
    fnorm = k.inp("fnorm", [D])
    ext2 = k.inp("ext2", [L, 6, 767])
    c_ident = k.inp("c_ident", [128, 128])
    c_J = k.inp("c_J", [128, 128])
    c_U = k.inp("c_U", [128, 128])
    cs_p = k.inp("cs_p", [T, 64])
    cs_s = k.inp("cs_s", [128, 64])

    y_p = k.outp("y_p", [T, D])
    y_s = k.outp("y_s", [256, D])
    p_ckv = k.outp("p_ckv", [L, T, 128])
    p_kr = k.outp("p_kr", [L, T, 32])
    p_gla = k.outp("p_gla", [L, 4, 64, 64])
    p_cak = k.outp("p_cak", [L, 512, 384])
    p_cav = k.outp("p_cav", [L, 512, 384])
    s_ckv = k.outp("s_ckv", [L, 256, 128])
    s_kr = k.outp("s_kr", [L, 256, 32])
    s_gla = k.outp("s_gla", [L, 4, 4, 64, 64])
    s_cak = k.outp("s_cak", [L, 256, 384])
    s_cav = k.outp("s_cav", [L, 256, 384])

    mk = k.outp if dbg else (lambda n, s, d: k.scratch(n, s, d))
    XB = mk("XB", [T, D], F32)
    QNT = mk("QNT", [3, 128, T], BF16)
    QRT = mk("QRT", [3, 128, T], BF16)
    KNT = mk("KNT", [3, 128, T], BF16)
    KR3T = mk("KR3T", [128, T], BF16)
    VA = mk("VA", [T, 768], BF16)
    QT = mk("QT", [6, 128, T], BF16)
    KT = mk("KT", [6, 128, T], BF16)
    MERGED = mk("MERGED", [T, D], BF16)

    ARENA_BYTES = 206 * 1024
    arena_t = nc.alloc_sbuf_tensor("arena", [128, ARENA_BYTES], U8) if False else None
    import contextlib
    es = contextlib.ExitStack()
    arena_t = es.enter_context(nc.sbuf_tensor("arena", [128, ARENA_BYTES], U8))
    ps_t = es.enter_context(nc.psum_tensor("ps", [128, 4096], F32))
    A = Arena(arena_t, ARENA_BYTES)

    def bank(i):
        return ps_t[:, 512 * i:512 * (i + 1)]

    def bank_bf(i):
        return ps_t[:, 512 * i:512 * (i + 1)].bitcast(BF16)

    def sb(shape, dt):
        return TL(A.alloc(shape, dt))

    bank_locks = [BankLock() for _ in range(8)]

    def TLP(i, bf=False):
        t = TL(bank_bf(i) if bf else bank(i))
        t.b.locks = (bank_locks[i],)
        return t

    ident_f = sb([128, 128], F32)
    ident_b = sb([128, 128], BF16)
    J_b = sb([128, 128], BF16)
    U_f = sb([128, 128], F32)
    cs_st = sb([128, 64], F32)
    x_s = sb([128, 2, D], F32)
    merged_s = sb([128, 2, D], BF16)

    k.dma("sp", ident_f.ap, c_ident[:, :], [], [ident_f.b])
    k.dma("pool", ident_b.ap, c_ident[:, :], [], [ident_b.b])
    k.dma("pool", J_b.ap, c_J[:, :], [], [J_b.b])
    k.dma("sp", U_f.ap, c_U[:, :], [], [U_f.b])
    k.dma("sp", cs_st.ap, cs_s[:, :], [], [cs_st.b])
    k.dma("sp", x_s.ap, xs.ap().rearrange("(t p) c -> p t c", p=128), [], [x_s.b])

    stage = [sb([128, 1024], F32) for _ in range(3)]
    stage_i = [0]

    def load_weight(dst, dstbuf, nchunks, ncols, src_fn, scale_fn, segs=None, eng="pool", stages=None, engs=None):
        stages = stages or stage
        for c in range(nchunks):
            for d0 in range(0, ncols, 1024):
                d1 = min(ncols, d0 + 1024)
                st = stages[stage_i[0] % len(stages)]
                if engs:
                    eng = engs[stage_i[0] % len(engs)]
                stage_i[0] += 1
                pieces = seg_map(d0, d1, segs) if segs else [(d0, d0, d1 - d0)]
                for (a, s0, ln) in pieces:
                    k.dma("sp", st.ap[:, a - d0:a - d0 + ln], src_fn(c, s0, ln), [], [st.b], accum=True)
                if scale_fn is not None:
                    sap, sbuf = scale_fn(c)
                    if eng == "act":
                        k.act(dst[:, c, d0:d1], st.ap[:, 0:d1 - d0], AF.Copy, [st.b, sbuf], [dstbuf], scale=sap,
                              accum=True)
                    else:
                        k.ts(eng, dst[:, c, d0:d1], st.ap[:, 0:d1 - d0], sap, 1.0, ALU.mult, ALU.mult,
                             [st.b, sbuf], [dstbuf], accum=True)
                else:
                    k.cp(eng, dst[:, c, d0:d1], st.ap[:, 0:d1 - d0], [st.b], [dstbuf], accum=True)

    persist_mark = A.mark()
    import os
    STOP = os.environ.get("KSTOP", "")

    class StopBuild(Exception):
        pass

    def chk(name):
        if name == STOP:
            raise StopBuild()

    try:
      chk("consts")
      for l in range(L):
          last_layer = (l == L - 1)
          x_src = xp if l == 0 else XB
          A.release(persist_mark)
          W_in = sb([128, 8, 2608], BF16)
          W_qup = sb([128, 2, 576], BF16)
          W_kn = sb([128, 1, 384], BF16)
          W_v = sb([128, 1, 384], BF16)
          W_g2 = sb([17, 256], BF16)
          n1T = sb([128, 8], F32)
          qnT = sb([128, 2], F32)
          kvg_bc = sb([128, 128], F32)
          gn_bc = sb([128, 256], F32)
          BT = sb([128, 6, 640], BF16)
          BTs = sb([80, 6, 528], BF16)
          cs_blk = [sb([128, 64], F32) for _ in range(2)]
          k.dma("sp", n1T.ap, norm1T[l, :, :], [], [n1T.b])
          k.dma("sp", qnT.ap, qnormT[l, :, :], [], [qnT.b])
          k.dma("sp", kvg_bc.ap, kvnorm[l, :].partition_broadcast(128), [], [kvg_bc.b])
          k.dma("sp", gn_bc.ap, glanorm[l, :].partition_broadcast(128), [], [gn_bc.b])
          k.ts("dve", gn_bc.ap, gn_bc.ap, 8.0, None, ALU.mult, None, [gn_bc.b], [gn_bc.b])
          k.dma("pool", W_g2.ap[0:16, :], w_gate2[l, :, :], [], [W_g2.b], accum=True)
          k.dma("pool", W_g2.ap[16:17, :], gate_bias[l:l + 1, :], [], [W_g2.b], accum=True)
          for h in range(6):
              src = RawAP(ext2, (l * 6 + h) * 767, [[1, 128], [1, 640]])
              k.dma("pool", BT.ap[:, h, :], src, [], [BT.b], accum=True)
              src2 = RawAP(ext2, (l * 6 + h) * 767 + 112, [[1, 16], [1, 528]])
              k.dma("pool", BTs.ap[0:16, h, :], src2, [], [BTs.b], accum=True)
              k.dma("pool", BTs.ap[64:80, h, :], src2, [], [BTs.b], accum=True)
          k.memset("pool", BT.ap[0:64, :, 0:64], -30000.0, [BT.b])
          k.memset("pool", BT.ap[64:128, :, 576:640], -30000.0, [BT.b])
          load_weight(W_in.ap, W_in.b, 8, 2608,
                      lambda c, s0, ln: w_in[l, c * 128:(c + 1) * 128, s0:s0 + ln],
                      lambda c: (n1T.ap[:, c:c + 1], n1T.b), segs=W_IN_SEGS, engs=("pool", "dve", "act"))
          load_weight(W_qup.ap, W_qup.b, 2, 576,
                      lambda c, s0, ln: w_qup[l, c * 128:(c + 1) * 128, s0:s0 + ln],
                      lambda c: (qnT.ap[:, c:c + 1], qnT.b))
          wkv = w_kvup[l, :, :].rearrange("r (h t e) -> r h t e", h=6, t=2)
          k.dma("pool", W_kn.ap[:, 0, :].rearrange("r (h e) -> r h e", h=6), wkv[:, :, 0, :], [], [W_kn.b])
          k.dma("pool", W_v.ap[:, 0, :].rearrange("r (h e) -> r h e", h=6), wkv[:, :, 1, :], [], [W_v.b])

          chk("a1w")
          Xt = [sb([128, D], F32) for _ in range(2)]
          junk = sb([128, 512], BF16)
          junkF = sb([128, D], BF16)
          ss = sb([128, 4], F32)
          rstd = sb([128, 4], F32)
          lnt = sb([128, 8], F32)
          ssF = sb([128, 2], F32)
          rstdF = sb([128, 2], F32)
          lntF = sb([128, 2], F32)
          z0s = [sb([128, 432], F32) for _ in range(2)]
          z1s = [sb([128, 512], F32) for _ in range(2)]
          z2s = [sb([128, 512], F32) for _ in range(2)]
          xn = sb([128, D], BF16)
          hT = sb([128, D], BF16)
          qn = sb([128, 256], BF16)
          ckvn = sb([128, 128], F32)
          ckvb = sb([128, 128], BF16)
          krf = sb([128, 32], F32)
          krt = sb([128, 32], F32)
          kr3 = sb([128, 128], BF16)
          glr = sb([128, 16], BF16)
          TT = sb([128, 384], BF16)
          g_aug = sb([17, 128], BF16)
          Qf = sb([128, 6, 96], F32)
          qtA = sb([128, 6, 32], F32)
          qtB = sb([128, 6, 32], F32)
          Qn_tok = sb([128, 384], BF16)
          Qr_tok = sb([128, 3, 128], BF16)
          QNs = sb([128, 3, 128], BF16)
          QRs = sb([128, 3, 128], BF16)
          KNs = sb([128, 3, 128], BF16)
          KRs = sb([128, 128], BF16)
          Vaug = sb([128, 6, 65], BF16)
          Vpad = sb([128, 6, 128], BF16)
          Qp_tok = sb([128, 6, 128], BF16)
          QTs = sb([128, 6, 512], BF16)
          KTs = sb([128, 6, 512], BF16)
          e1 = sb([128, 256], F32)
          spl = sb([128, 256], F32)
          ebT = sb([128, 2, 128], F32)
          enbT = sb([128, 2, 128], F32)
          enb = sb([128, 256], F32)
          ebl = sb([128, 2, 2], F32)
          gqk = sb([128, 512], BF16)
          qeT = sb([128, 2, 128], BF16)
          keT = sb([128, 2, 128], BF16)
          ke = sb([128, 256], BF16)
          gv = sb([128, 256], BF16)
          e2 = sb([128, 256], F32)
          gate = sb([128, 256], F32)
          ATm = sb([128, 2, 2, 128], BF16)
          S_f = sb([128, 2, 64], F32)
          S_t = sb([128, 2, 64], F32)
          S_b = sb([128, 2, 64], BF16)
          o_sb = sb([128, 4, 64], F32)
          o_sq = sb([128, 4, 64], F32)
          o_ss = sb([128, 4], F32)
          o_r = sb([128, 4], F32)
          merged = sb([128, D], BF16)
          cqb2 = [sb([128, 384], BF16) for _ in range(2)]
          ckb2 = [sb([128, 384], BF16) for _ in range(2)]
          ckf = sb([128, 384], F32)
          cvf = sb([128, 384], F32)
          cqT = sb([128, 3, 128], BF16)
          Kring = [sb([128, 3, 128], BF16) for _ in range(5)]
          Vring = [sb([128, 6, 65], BF16) for _ in range(6)]
          PT = [sb([128, 5, 128], BF16) for _ in range(2)]
          rden = sb([128, 6], F32)
          sc_ckv = sb([128, 8, 128], BF16)
          sc_kr3 = sb([128, 8, 128], BF16)
          sc_ckvT = sb([128, 1024], BF16)
          sc_KN = sb([128, 3, 1024], BF16)
          sc_KR = sb([128, 1024], BF16)
          sc_V = sb([128, 8, 6, 65], BF16)
          sc_cak = sb([128, 4, 384], BF16)
          sc_caKT = sb([128, 3, 512], BF16)
          sc_caV = sb([128, 4, 6, 65], BF16)
          sPT = sb([128, 6, 9, 16], BF16)
          J16 = {0: J_b.ap[0:16, 112:128], 64: J_b.ap[64:80, 48:64]}

          k.memset("pool", g_aug.ap, 1.0, [g_aug.b])
          k.memset("pool", kr3.ap, 0.0, [kr3.b])
          k.memset("pool", Qr_tok.ap, 0.0, [Qr_tok.b])
          k.memset("pool", sc_kr3.ap, 0.0, [sc_kr3.b])
          k.memset("pool", Vaug.ap, 1.0, [Vaug.b])
          k.memset("pool", Vpad.ap, 0.0, [Vpad.b])
          k.memset("pool", Vpad.ap[:, :, 64:65], 1.0, [Vpad.b])
          k.memset("pool", Qp_tok.ap, 0.0, [Qp_tok.b])
          k.memset("pool", KTs.ap, 0.0, [KTs.b])
          for r in range(6):
              k.memset("pool", Vring[r].ap, 1.0, [Vring[r].b])
          k.memset("pool", sc_V.ap, 1.0, [sc_V.b])
          k.memset("pool", sc_caV.ap, 1.0, [sc_caV.b])
          k.memset("pool", merged_s.ap, 0.0, [merged_s.b])
          k.memset("dve", S_f.ap, 0.0, [S_f.b])
          k.memset("dve", S_b.ap, 0.0, [S_b.b])
          P.barrier()
          for h in range(6):
              pe_ = [TLP(4), TLP(5)]
              for w in range(5):
                  pb_ = pe_[0] if w < 4 else pe_[1]
                  k.mm(pb_.ap[:, (w % 4) * 128:(w % 4 + 1) * 128], BT.ap[:, h, w * 128:(w + 1) * 128], J_b.ap, True, True,
                       [BT.b, J_b.b], [pb_.b])
              k.act(BT.ap[:, h, 0:512], pe_[0].ap, AF.Exp, [pe_[0].b], [BT.b])
              k.act(BT.ap[:, h, 512:640], pe_[1].ap[:, 0:128], AF.Exp, [pe_[1].b], [BT.b], accum=True)
          P.barrier()

          pT1 = TLP(0, True)
          pT2 = TLP(1, True)
          pT2f = TLP(1)
          pT2f.b = pT2.b
          pZ = [TLP(2), TLP(3)]
          pM = [TLP(4), TLP(5), TLP(6), TLP(7)]
          pCA = [TL(ps_t[:, 2048:2048 + 640]), TL(ps_t[:, 3072:3072 + 640])]
          pCA[0].b = pM[0].b
          pCA[1].b = pM[2].b

          ZG = [(0, 432), (432, 944), (944, 1456), (1456, 1840), (1840, 2224), (2224, 2608)]

          def front_gen(kind, idx, par, vslot):
              smp = (kind == "s")
              if smp:
                  X = TL(x_s.ap[:, idx, :])
                  X.b = x_s.b
              else:
                  X = Xt[idx % 2]
              k.act(junkF.ap, X.ap, AF.Square, [X.b], [junkF.b, ssF.b], scale=1.0 / 32.0, accum_out=ssF.ap[:, 0:1])
              k.rsqrt(rstdF.ap[:, 0:1], ssF.ap[:, 0:1], EPS, TL2(lntF, 0, 1), [ssF.b], [rstdF.b])
              k.ts("dve", xn.ap, X.ap, rstdF.ap[:, 0:1], None, ALU.mult, None, [X.b, rstdF.b], [xn.b])
              for c in range(8):
                  k.tr(pT1.ap[:, c * 128:(c + 1) * 128], xn.ap[:, c * 128:(c + 1) * 128], ident_b.ap,
                       [xn.b, ident_b.b], [pT1.b])
              k.cp("act", hT.ap, pT1.ap, [pT1.b], [hT.b])
              yield
              need_out = smp or idx >= NB - 4
              for g, (c0, c1) in enumerate(ZG):
                  pz = pZ[g % 2]
                  for c in range(8):
                      k.mm(pz.ap[:, 0:c1 - c0], hT.ap[:, c * 128:(c + 1) * 128], W_in.ap[:, c, c0:c1],
                           c == 0, c == 7, [hT.b, W_in.b], [pz.b])
                  if g == 0:
                      k.cp("act", z0s[par].ap, pz.ap[:, 0:432], [pz.b], [z0s[par].b])
                  elif g == 1:
                      k.cp("dve", z1s[par].ap, pz.ap, [pz.b], [z1s[par].b])
                  elif g == 2:
                      k.cp("act", z2s[par].ap, pz.ap, [pz.b], [z2s[par].b])
                  elif g == 3:
                      k.act(cqb2[par].ap, pz.ap[:, 0:384], AF.Copy, [pz.b], [cqb2[par].b], scale=0.125)
                  elif g == 4:
                      k.cp("dve", ckb2[par].ap, pz.ap[:, 0:384], [pz.b], [ckb2[par].b])
                      if need_out:
                          k.cp("dve", ckf.ap, pz.ap[:, 0:384], [pz.b], [ckf.b])
                          dst = s_cak[l, idx * 128:(idx + 1) * 128, :] if smp else \
                              p_cak[l, (idx - (NB - 4)) * 128:(idx - (NB - 4) + 1) * 128, :]
                          k.dma("sp", dst, ckf.ap, [ckf.b], [])
                  else:
                      Vr_ = Vring[vslot]
                      k.cp("act", Vr_.ap[:, :, 0:64], pz.ap[:, 0:384].rearrange("p (h e) -> p h e", h=6), [pz.b],
                           [Vr_.b], accum=True)
                      if need_out:
                          k.cp("dve", cvf.ap, pz.ap[:, 0:384], [pz.b], [cvf.b])
                          dst = s_cav[l, idx * 128:(idx + 1) * 128, :] if smp else \
                              p_cav[l, (idx - (NB - 4)) * 128:(idx - (NB - 4) + 1) * 128, :]
                          k.dma("sp", dst, cvf.ap, [cvf.b], [])
                  yield

          def a1_block(kind, idx, par, vslot, adv):
              smp = (kind == "s")
              if smp:
                  cs = cs_st.ap
                  csb = cs_st.b
              else:
                  cs = cs_blk[idx % 2].ap
                  csb = cs_blk[idx % 2].b
              z0, z1, z2 = z0s[par], z1s[par], z2s[par]
              cqb, ckb = cqb2[par], ckb2[par]
              k.act(junk.ap[:, 0:256], z0.ap[:, 0:256], AF.Square, [z0.b], [junk.b, ss.b], scale=1.0 / 16.0,
                    accum_out=ss.ap[:, 1:2])
              k.act(junk.ap[:, 256:384], z0.ap[:, 256:384], AF.Square, [z0.b], [junk.b, ss.b],
                    scale=128.0 ** -0.5, accum_out=ss.ap[:, 2:3])
              k.rsqrt(rstd.ap[:, 1:3], ss.ap[:, 1:3], EPS, TL2(lnt, 1, 3), [ss.b], [rstd.b])
              k.ts("dve", qn.ap, z0.ap[:, 0:256], rstd.ap[:, 1:2], None, ALU.mult, None, [z0.b, rstd.b], [qn.b])
              k.stt("dve", ckvn.ap, z0.ap[:, 256:384], rstd.ap[:, 2:3], kvg_bc.ap, ALU.mult, ALU.mult,
                    [z0.b, rstd.b, kvg_bc.b], [ckvn.b])
              k.cp("pool", ckvb.ap, ckvn.ap, [ckvn.b], [ckvb.b])
              k.tt("dve", krf.ap, z0.ap[:, 384:416], cs[:, 0:32], ALU.mult, [z0.b, csb], [krf.b])
              k.tt("dve", krt.ap[:, 0:16], z0.ap[:, 400:416], cs[:, 32:48], ALU.mult, [z0.b, csb], [krt.b], accum=True)
              k.tt("dve", krt.ap[:, 16:32], z0.ap[:, 384:400], cs[:, 48:64], ALU.mult, [z0.b, csb], [krt.b], accum=True)
              k.cp("act", glr.ap, z0.ap[:, 416:432], [z0.b], [glr.b])
              k.tt("dve", krf.ap, krf.ap, krt.ap, ALU.add, [krf.b, krt.b], [krf.b])
              for r in range(2):
                  k.cp("pool", kr3.ap[:, 64 * r:64 * r + 32], krf.ap, [krf.b], [kr3.b], accum=True)
              if smp:
                  k.dma("sp", s_ckv[l, idx * 128:(idx + 1) * 128, :], ckvn.ap, [ckvn.b], [])
                  k.dma("sp", s_kr[l, idx * 128:(idx + 1) * 128, :], krf.ap, [krf.b], [])
              else:
                  k.dma("sp", p_ckv[l, idx * 128:(idx + 1) * 128, :], ckvn.ap, [ckvn.b], [])
                  k.dma("sp", p_kr[l, idx * 128:(idx + 1) * 128, :], krf.ap, [krf.b], [])
              k.tr(pT2.ap[:, 0:128], qn.ap[:, 0:128], ident_b.ap, [qn.b, ident_b.b], [pT2.b])
              k.tr(pT2.ap[:, 128:256], qn.ap[:, 128:256], ident_b.ap, [qn.b], [pT2.b])
              k.tr(pT2.ap[:, 256:384], ckvb.ap, ident_b.ap, [ckvb.b], [pT2.b])
              SK = os.environ.get("KSKIP", "")
              if "kr" not in SK:
                  k.tr(pT2.ap[:, 384:512], kr3.ap, ident_b.ap, [kr3.b], [pT2.b])
              if "glr" not in SK:
                  k.tr(pT2.ap[0:16, 512:640], glr.ap, ident_b.ap, [glr.b], [pT2.b])
              sl = (idx % 4) * 128 if not smp else 0
              if "tt" not in SK:
                  k.cp("act", TT.ap, pT2.ap[:, 0:384], [pT2.b], [TT.b])
              if smp:
                  k.cp("dve", KRs.ap[:, sl:sl + 128], pT2.ap[:, 384:512], [pT2.b], [KRs.b], accum=True)
              else:
                  k.cp("dve", KTs.ap[64:96, :, sl:sl + 128], bc_mid(pT2.ap[64:96, 384:512], 6), [pT2.b], [KTs.b],
                       accum=True)
              if "glr" not in SK:
                  k.cp("dve", g_aug.ap[0:16, :], pT2.ap[0:16, 512:640], [pT2.b], [g_aug.b], accum=True)
              adv()
              for hf in range(2):
                  pq = pM[hf]
                  for c in range(2):
                      k.mm(pq.ap[:, 0:288], TT.ap[:, c * 128:(c + 1) * 128], W_qup.ap[:, c, hf * 288:(hf + 1) * 288],
                           c == 0, c == 1, [TT.b, W_qup.b], [pq.b])
                  k.cp("act", Qf.ap[:, 3 * hf:3 * hf + 3, :], pq.ap[:, 0:288].rearrange("p (h e) -> p h e", h=3),
                       [pq.b], [Qf.b], accum=True)
              if smp:
                  pkn = pM[2]
                  for p in range(3):
                      k.mm(pkn.ap[:, p * 128:(p + 1) * 128], W_kn.ap[:, 0, p * 128:(p + 1) * 128], TT.ap[:, 256:384],
                           True, True, [TT.b, W_kn.b], [pkn.b])
                  pv = pM[3]
                  k.mm(pv.ap[:, 0:384], TT.ap[:, 256:384], W_v.ap[:, 0, :], True, True, [TT.b, W_v.b], [pv.b])
                  k.cp("act", KNs.ap[:, :, sl:sl + 128], pkn.ap[:, 0:384].rearrange("p (a t) -> p a t", a=3),
                       [pkn.b], [KNs.b], accum=True)
                  k.cp("dve", Vaug.ap[:, :, 0:64], pv.ap[:, 0:384].rearrange("p (h e) -> p h e", h=6),
                       [pv.b], [Vaug.b], accum=True)
              else:
                  for r in range(2):
                      pk_ = pM[2 + r]
                      for j in range(3):
                          h = 3 * r + j
                          k.mm(pk_.ap[0:64, j * 128:(j + 1) * 128], W_kn.ap[:, 0, h * 64:(h + 1) * 64],
                               TT.ap[:, 256:384], True, True, [TT.b, W_kn.b], [pk_.b])
                      k.cp("act" if r == 0 else "dve", KTs.ap[0:64, 3 * r:3 * r + 3, sl:sl + 128],
                           pk_.ap[0:64, 0:384].rearrange("p (a t) -> p a t", a=3), [pk_.b], [KTs.b], accum=True)
                  pv = pM[2]
                  k.mm(pv.ap[:, 0:384], TT.ap[:, 256:384], W_v.ap[:, 0, :], True, True, [TT.b, W_v.b], [pv.b])
                  k.cp("act", Vpad.ap[:, :, 0:64], pv.ap[:, 0:384].rearrange("p (h e) -> p h e", h=6),
                       [pv.b], [Vpad.b], accum=True)
                  k.dma("sp", VA[idx * 128:(idx + 1) * 128, :], Vpad.ap.rearrange("p h e -> p (h e)"), [Vpad.b], [])
              adv()
              k.tt("dve", qtA.ap, Qf.ap[:, :, 64:96], bc_mid(cs[:, 0:32], 6), ALU.mult, [Qf.b, csb], [qtA.b])
              k.tt("dve", qtB.ap[:, :, 0:16], Qf.ap[:, :, 80:96], bc_mid(cs[:, 32:48], 6), ALU.mult, [Qf.b, csb],
                   [qtB.b], accum=True)
              k.tt("dve", qtB.ap[:, :, 16:32], Qf.ap[:, :, 64:80], bc_mid(cs[:, 48:64], 6), ALU.mult, [Qf.b, csb],
                   [qtB.b], accum=True)
              if smp:
                  k.tt("dve", Qr_tok.ap.rearrange("p a (hh c) -> p a hh c", hh=2)[:, :, :, 0:32],
                       qtA.ap.rearrange("p (a hh) e -> p a hh e", hh=2), qtB.ap.rearrange("p (a hh) e -> p a hh e", hh=2),
                       ALU.add, [qtA.b, qtB.b], [Qr_tok.b], accum=True)
                  k.cp("pool", Qn_tok.ap.rearrange("p (h e) -> p h e", h=6), Qf.ap[:, :, 0:64], [Qf.b], [Qn_tok.b])
                  for p in range(3):
                      k.tr(pT2.ap[:, p * 128:(p + 1) * 128], Qn_tok.ap[:, p * 128:(p + 1) * 128], ident_b.ap,
                           [Qn_tok.b, ident_b.b], [pT2.b])
                  for g2 in range(3):
                      k.tr(pT2.ap[:, 384 + g2 * 128:512 + g2 * 128], Qr_tok.ap[:, g2, :], ident_b.ap,
                           [Qr_tok.b], [pT2.b])
                  k.cp("act", QNs.ap[:, :, sl:sl + 128], pT2.ap[:, 0:384].rearrange("p (a t) -> p a t", a=3),
                       [pT2.b], [QNs.b], accum=True)
                  k.cp("dve", QRs.ap[:, :, sl:sl + 128], pT2.ap[:, 384:768].rearrange("p (a t) -> p a t", a=3),
                       [pT2.b], [QRs.b], accum=True)
              else:
                  k.tt("dve", Qp_tok.ap[:, :, 64:96], qtA.ap, qtB.ap, ALU.add, [qtA.b, qtB.b], [Qp_tok.b], accum=True)
                  k.cp("pool", Qp_tok.ap[:, :, 0:64], Qf.ap[:, :, 0:64], [Qf.b], [Qp_tok.b], accum=True)
                  for h in range(6):
                      k.tr(pT2.ap[:, h * 128:(h + 1) * 128], Qp_tok.ap[:, h, :], ident_b.ap,
                           [Qp_tok.b, ident_b.b], [pT2.b])
                  k.cp("act", QTs.ap[:, 0:3, sl:sl + 128], pT2.ap[:, 0:384].rearrange("p (a t) -> p a t", a=3),
                       [pT2.b], [QTs.b], accum=True)
                  k.cp("dve", QTs.ap[:, 3:6, sl:sl + 128], pT2.ap[:, 384:768].rearrange("p (a t) -> p a t", a=3),
                       [pT2.b], [QTs.b], accum=True)
                  if idx % 4 == 3:
                      t0 = (idx - 3) * 128
                      k.dma("sp", QT[:, :, t0:t0 + 512].rearrange("a p t -> p a t"), QTs.ap, [QTs.b], [])
                      k.dma("sp", KT[:, :, t0:t0 + 512].rearrange("a p t -> p a t"), KTs.ap, [KTs.b], [])
              adv()
              pg = pM[0]
              k.mm(pg.ap[:, 0:256], g_aug.ap, W_g2.ap, True, True, [g_aug.b, W_g2.b], [pg.b])
              k.act(e1.ap, pg.ap[:, 0:256], AF.Exp, [pg.b], [e1.b], scale=-1.0)
              k.act(spl.ap, e1.ap, AF.Ln, [e1.b], [spl.b], bias=1.0)
              k.mm(pg.ap[:, 256:512], U_f.ap, spl.ap, True, True, [U_f.b, spl.b], [pg.b])
              pc = pM[1]
              for p in range(2):
                  k.mm(pc.ap[:, p * 128:(p + 1) * 128], spl.ap[:, p * 128:(p + 1) * 128], U_f.ap, True, True,
                       [spl.b, U_f.b], [pc.b])
              cbT = pc.ap[:, 0:256].rearrange("p (a t) -> p a t", a=2)
              k.act(ebT.ap, cbT, AF.Exp, [pc.b], [ebT.b], scale=-1.0 / 16.0, bias=math.log(0.125))
              k.act(enbT.ap, cbT, AF.Exp, [pc.b], [enbT.b], scale=1.0 / 16.0)
              k.act(enb.ap, pg.ap[:, 256:512], AF.Exp, [pg.b], [enb.b], scale=1.0 / 16.0)
              lastc = 15 if smp else 63
              k.act(ebl.ap, pc.ap[:, 0:256].rearrange("p (a c t) -> p a c t", a=2, c=2)[:, :, :, lastc],
                    AF.Exp, [pc.b], [ebl.b], scale=-1.0 / 16.0)
              k.cp("act", gqk.ap, z1.ap, [z1.b], [gqk.b])
              for i in range(4):
                  k.tr(pT2.ap[:, i * 128:(i + 1) * 128], gqk.ap[:, i * 128:(i + 1) * 128], ident_b.ap,
                       [gqk.b, ident_b.b], [pT2.b])
              k.tt("dve", qeT.ap, pT2.ap[:, 0:256].rearrange("p (a t) -> p a t", a=2), ebT.ap, ALU.mult,
                   [pT2.b, ebT.b], [qeT.b])
              k.tt("dve", keT.ap, pT2.ap[:, 256:512].rearrange("p (a t) -> p a t", a=2), enbT.ap, ALU.mult,
                   [pT2.b, enbT.b], [keT.b])
              k.tt("dve", ke.ap, z1.ap[:, 256:512], enb.ap, ALU.mult, [z1.b, enb.b], [ke.b])
              k.cp("pool" if False else "dve", gv.ap, z2.ap[:, 0:256], [z2.b], [gv.b])
              k.act(e2.ap, z2.ap[:, 256:512], AF.Exp, [z2.b], [e2.b], scale=-1.0)
              k.ts("dve", e2.ap, e2.ap, 1.0, None, ALU.add, None, [e2.b], [e2.b])
              k.P.op("dve", lambda e: e.reciprocal(e2.ap, e2.ap), reads=[e2.b], writes=[e2.b])
              k.tt("dve", gate.ap, z2.ap[:, 256:512], e2.ap, ALU.mult, [z2.b, e2.b], [gate.b])
              adv()
              for h in range(4):
                  p, hb = h // 2, (h % 2) * 64
                  pa = pM[2 + (h % 2)]
                  k.mm(pa.ap[:, p * 128:(p + 1) * 128], keT.ap[hb:hb + 64, p, :], qeT.ap[hb:hb + 64, p, :], True, True,
                       [keT.b, qeT.b], [pa.b])
              if "atmask" not in os.environ.get("KSKIP", ""):
                  for hh in range(2):
                      k.tt("dve", ATm.ap[:, hh, :, :], pM[2 + hh].ap[:, 0:256].rearrange("p (a t) -> p a t", a=2),
                           bc_mid(U_f.ap, 2), ALU.mult, [pM[2 + hh].b, U_f.b], [ATm.b], accum=(hh == 1))
              po = pM[3]
              for c in range(2):
                  r0 = c * 64
                  nreal = 16 if smp else 64
                  if smp:
                      seq = idx * 2 + c
                      for hh in range(2):
                          k.dma("sp", S_f.ap[hh * 64:(hh + 1) * 64, :, :],
                                st_gla[l, seq, :, :, :].rearrange("(p hh) d v -> hh d p v", hh=2)[hh],
                                [], [S_f.b], accum=(hh == 1))
                      k.cp("dve", S_b.ap, S_f.ap, [S_f.b], [S_b.b])
                  for h in range(4):
                      p, hb = h // 2, (h % 2) * 64
                      k.mm(po.ap[r0:r0 + 64, h * 64:(h + 1) * 64], ATm.ap[:, h % 2, h // 2, r0:r0 + 64], gv.ap[:, h * 64:(h + 1) * 64],
                           True, False, [ATm.b, gv.b], [po.b])
                      k.mm(po.ap[r0:r0 + 64, h * 64:(h + 1) * 64], qeT.ap[hb:hb + 64, p, r0:r0 + 64],
                           S_b.ap[hb:hb + 64, p, :], False, True, [qeT.b, S_b.b], [po.b])
                  for h in range(4):
                      p, hb = h // 2, (h % 2) * 64
                      k.mm(pc.ap[hb:hb + 64, 256 + p * 64:256 + (p + 1) * 64], ke.ap[r0:r0 + nreal, h * 64:(h + 1) * 64],
                           gv.ap[r0:r0 + nreal, h * 64:(h + 1) * 64], True, True, [ke.b, gv.b], [pc.b])
                  k.tt("dve", S_t.ap, pc.ap[:, 256:384].rearrange("p (a v) -> p a v", a=2), S_f.ap, ALU.add,
                       [pc.b, S_f.b], [S_t.b])
                  k.tt("dve", S_f.ap, S_t.ap, bc_last(ebl.ap[:, :, c], 64), ALU.mult, [S_t.b, ebl.b], [S_f.b])
                  k.cp("dve", S_b.ap, S_f.ap, [S_f.b], [S_b.b])
                  if smp:
                      seq = idx * 2 + c
                      for hh in range(2):
                          k.dma("sp", s_gla[l, seq, :, :, :].rearrange("(p hh) d v -> hh d p v", hh=2)[hh],
                                S_f.ap[hh * 64:(hh + 1) * 64, :, :], [S_f.b], [])
              if (not smp) and idx == NB - 1:
                  for hh in range(2):
                      k.dma("sp", p_gla[l, :, :, :].rearrange("(p hh) d v -> hh d p v", hh=2)[hh],
                            S_f.ap[hh * 64:(hh + 1) * 64, :, :], [S_f.b], [])
              adv()
              k.cp("act", o_sb.ap, po.ap[:, 0:256].rearrange("p (h v) -> p h v", h=4), [po.b], [o_sb.b])
              k.tt("dve", o_sq.ap, o_sb.ap, o_sb.ap, ALU.mult, [o_sb.b], [o_sq.b])
              k.P.op("dve", lambda e: e.tensor_reduce(o_ss.ap, o_sq.ap, AX.X, ALU.add), reads=[o_sq.b], writes=[o_ss.b])
              k.rsqrt(o_r.ap, o_ss.ap, 64.0 * EPS, TL2(lnt, 4, 8), [o_ss.b], [o_r.b])
              k.tt("dve", o_sq.ap, o_sb.ap, bc_last(o_r.ap, 64), ALU.mult, [o_sb.b, o_r.b], [o_sq.b])
              k.tt("dve", o_sb.ap.rearrange("p h v -> p (h v)"), o_sq.ap.rearrange("p h v -> p (h v)"), gn_bc.ap,
                   ALU.mult, [o_sq.b, gn_bc.b], [o_sb.b])
              mg = TL(merged_s.ap[:, idx, :]) if smp else merged
              if smp:
                  mg.b = merged_s.b
              k.tt("dve", mg.ap[:, 384:640], o_sb.ap.rearrange("p h v -> p (h v)"), gate.ap, ALU.mult,
                   [o_sb.b, gate.b], [mg.b], accum=True)
              adv()
              slot = idx % 5 if not smp else 0
              Vr = Vring[vslot]
              Kr = Kring[slot]
              for p in range(3):
                  k.tr(pT2.ap[:, p * 128:(p + 1) * 128], cqb.ap[:, p * 128:(p + 1) * 128], ident_b.ap,
                       [cqb.b, ident_b.b], [pT2.b])
                  k.tr(pT2.ap[:, 384 + p * 128:512 + p * 128], ckb.ap[:, p * 128:(p + 1) * 128], ident_b.ap,
                       [ckb.b], [pT2.b])
              k.cp("act", cqT.ap, pT2.ap[:, 0:384].rearrange("p (a t) -> p a t", a=3), [pT2.b], [cqT.b])
              k.cp("dve", Kr.ap, pT2.ap[:, 384:768].rearrange("p (a t) -> p a t", a=3), [pT2.b], [Kr.b])
              if not smp:
                  b = idx
                  wins = [w for w in range(5) if b - 4 + w >= 0]
                  pov = pT2f
                  w0 = wins[0]

                  def ca_S(h):
                      p, hb = h // 2, (h % 2) * 64
                      pca = pCA[h % 2]
                      for w in wins:
                          kb = Kring[(b - 4 + w) % 5]
                          k.mm(pca.ap[:, w * 128:(w + 1) * 128], kb.ap[hb:hb + 64, p, :], cqT.ap[hb:hb + 64, p, :],
                               True, True, [kb.b, cqT.b], [pca.b, pM[1 + 2 * (h % 2)].b])

                  def ca_E_PV(h):
                      pca = pCA[h % 2]
                      pt = PT[h % 2]
                      k.act(pt.ap[:, w0:5, :], pca.ap[:, w0 * 128:640].rearrange("p (w t) -> p w t", t=128), AF.Exp,
                            [pca.b, pM[1 + 2 * (h % 2)].b], [pt.b])
                      k.tt("dve", pt.ap[:, w0:5, :], pt.ap[:, w0:5, :],
                           BT.ap[:, h, w0 * 128:640].rearrange("p (w t) -> p w t", t=128), ALU.mult,
                           [pt.b, BT.b], [pt.b])
                      for w in wins:
                          vb = Vring[(b - 4 + w) % 6]
                          k.mm(pov.ap[:, h * 65:(h + 1) * 65], pt.ap[:, w, :], vb.ap[:, h, :], w == w0, w == 4,
                               [pt.b, vb.b], [pov.b])

                  ca_S(0)
                  for h in range(6):
                      adv()
                      if h + 1 < 6:
                          ca_S(h + 1)
                      ca_E_PV(h)
                  k.P.op("dve", lambda e: e.reciprocal(rden.ap, pov.ap[:, 0:390].rearrange("p (h e) -> p h e", h=6)[:, :, 64]),
                         reads=[pov.b], writes=[rden.b])
                  k.tt("dve", merged.ap[:, 640:1024].rearrange("p (h e) -> p h e", h=6),
                       pov.ap[:, 0:390].rearrange("p (h e) -> p h e", h=6)[:, :, 0:64], bc_last(rden.ap, 64), ALU.mult,
                       [pov.b, rden.b], [merged.b], accum=True)
                  k.dma("sp", MERGED[b * 128:(b + 1) * 128, 384:1024], merged.ap[:, 384:1024], [merged.b], [])
              else:
                  sample_mixers(idx, Kr, Vr)

          def sample_mixers(t, Kr, Vr):
              for j in range(2):
                  seq = 2 * t + j
                  r0 = 64 * j
                  k.dma("pool", sc_cak.ap, c_cak[l, seq, :, :].rearrange("(w p) c -> p w c", p=128), [], [sc_cak.b])
                  for w in range(4):
                      k.dma("pool", sc_caV.ap[:, w, :, 0:64],
                            c_cav[l, seq, w * 128:(w + 1) * 128, :].rearrange("p (h e) -> p h e", h=6), [], [sc_caV.b],
                            accum=True)
                  for w in range(4):
                      for p in range(3):
                          k.tr(pT2.ap[:, p * 128:(p + 1) * 128], sc_cak.ap[:, w, p * 128:(p + 1) * 128], ident_b.ap,
                               [sc_cak.b, ident_b.b], [pT2.b])
                      k.cp("act", sc_caKT.ap[:, :, w * 128:(w + 1) * 128],
                           pT2.ap[:, 0:384].rearrange("p (a t) -> p a t", a=3), [pT2.b], [sc_caKT.b], accum=True)
                  pssb = [pM[0], pM[3]]
                  pso = pM[1]
                  svb = [pb_.ap[:, 0:240].rearrange("p (h w q) -> p h w q", h=3, w=5) for pb_ in pssb]
                  for h in range(6):
                      p, hb = h // 2, (h % 2) * 64
                      sv = svb[h % 2]
                      pb_ = pssb[h % 2]
                      for w in range(4):
                          k.mm(sv[:, p, w, :], sc_caKT.ap[hb:hb + 64, p, w * 128:(w + 1) * 128],
                               cqT.ap[hb:hb + 64, p, r0:r0 + 16], True, False, [sc_caKT.b, cqT.b], [pb_.b])
                          k.mm(sv[:, p, w, :], BTs.ap[hb:hb + 16, h, w * 128:(w + 1) * 128], J16[hb], False, True,
                               [BTs.b, J_b.b], [pb_.b])
                      k.mm(sv[r0:r0 + 16, p, 4, :], Kr.ap[hb:hb + 64, p, r0:r0 + 16], cqT.ap[hb:hb + 64, p, r0:r0 + 16],
                           True, False, [Kr.b, cqT.b], [pb_.b])
                      k.mm(sv[r0:r0 + 16, p, 4, :], BTs.ap[hb:hb + 16, h, 512:528], J16[hb], False, True,
                           [BTs.b, J_b.b], [pb_.b])
                  for par in range(2):
                      pt = sPT.ap[:, 3 * par:3 * par + 3, 0:5, :]
                      k.act(pt[:, :, 0:4, :], svb[par][:, :, 0:4, :], AF.Exp, [pssb[par].b], [sPT.b], accum=(par > 0))
                      k.act(pt[r0:r0 + 16, :, 4, :], svb[par][r0:r0 + 16, :, 4, :], AF.Exp, [pssb[par].b], [sPT.b],
                            accum=True)
                  ov = pso.ap[:, 0:390].rearrange("p (h e) -> p h e", h=6)
                  for h in range(6):
                      s_ = (h % 2) * 3 + h // 2
                      for w in range(4):
                          k.mm(ov[r0:r0 + 16, h, :], sPT.ap[:, s_, w, :], sc_caV.ap[:, w, h, :], w == 0, False,
                               [sPT.b, sc_caV.b], [pso.b])
                      k.mm(ov[r0:r0 + 16, h, :], sPT.ap[r0:r0 + 16, s_, 4, :], Vr.ap[r0:r0 + 16, h, :], False, True,
                           [sPT.b, Vr.b], [pso.b])
                  k.P.op("dve", lambda e, ov=ov, r0=r0: e.reciprocal(rden.ap[r0:r0 + 16, :], ov[r0:r0 + 16, :, 64]),
                         reads=[pso.b], writes=[rden.b])
                  k.tt("dve", merged_s.ap[r0:r0 + 16, t, 640:1024].rearrange("p (h e) -> p h e", h=6),
                       ov[r0:r0 + 16, :, 0:64], bc_last(rden.ap[r0:r0 + 16, :], 64), ALU.mult,
                       [pso.b, rden.b], [merged_s.b], accum=True)
                  k.dma("pool", sc_ckv.ap, c_ckv[l, seq, :, :].rearrange("(w p) c -> p w c", p=128), [], [sc_ckv.b])
                  for r in range(2):
                      k.dma("pool", sc_kr3.ap[:, :, 64 * r:64 * r + 32],
                            c_kr[l, seq, :, :].rearrange("(w p) c -> p w c", p=128), [], [sc_kr3.b], accum=True)
                  for w in range(8):
                      k.tr(pT1.ap[:, w * 128:(w + 1) * 128], sc_ckv.ap[:, w, :], ident_b.ap, [sc_ckv.b, ident_b.b],
                           [pT1.b])
                  k.cp("act", sc_ckvT.ap, pT1.ap, [pT1.b], [sc_ckvT.b])
                  for w in range(8):
                      k.tr(pT2.ap[:, w * 128:(w + 1) * 128], sc_kr3.ap[:, w, :],
                           ident_b.ap, [sc_kr3.b, ident_b.b], [pT2.b])
                  k.cp("dve", sc_KR.ap, pT2.ap, [pT2.b], [sc_KR.b])
                  for p in range(3):
                      for hf in range(2):
                          pk = pM[2 + hf]
                          k.mm(pk.ap, W_kn.ap[:, 0, p * 128:(p + 1) * 128], sc_ckvT.ap[:, hf * 512:(hf + 1) * 512], True,
                               True, [W_kn.b, sc_ckvT.b], [pk.b])
                          k.cp("act" if hf else "dve", sc_KN.ap[:, p, hf * 512:(hf + 1) * 512], pk.ap, [pk.b], [sc_KN.b],
                               accum=True)
                  for w in range(8):
                      pk = pM[2 + w % 2]
                      k.mm(pk.ap[:, 0:384], sc_ckvT.ap[:, w * 128:(w + 1) * 128], W_v.ap[:, 0, :], True, True,
                           [sc_ckvT.b, W_v.b], [pk.b])
                      k.cp("act" if w % 2 else "dve", sc_V.ap[:, w, :, 0:64],
                           pk.ap[:, 0:384].rearrange("p (h e) -> p h e", h=6), [pk.b], [sc_V.b], accum=True)
                  pss2 = [pM[0], pM[1]]
                  sv2b = [pb_.ap[:, 0:432].rearrange("p (h w q) -> p h w q", h=3, w=9) for pb_ in pss2]
                  for h in range(6):
                      p, hb = h // 2, (h % 2) * 64
                      pb = pss2[h % 2]
                      sv2 = sv2b[h % 2]
                      for w in range(9):
                          if w < 8:
                              out = sv2[:, p, w, :]
                              kn = sc_KN.ap[hb:hb + 64, p, w * 128:(w + 1) * 128]
                              kr_ = sc_KR.ap[hb:hb + 32, w * 128:(w + 1) * 128]
                              rd = [sc_KN.b, sc_KR.b]
                          else:
                              out = sv2[r0:r0 + 16, p, 8, :]
                              kn = KNs.ap[hb:hb + 64, p, r0:r0 + 16]
                              kr_ = KRs.ap[hb:hb + 32, r0:r0 + 16]
                              rd = [KNs.b, KRs.b]
                          k.mm(out, kn, QNs.ap[hb:hb + 64, p, r0:r0 + 16], True, False, rd + [QNs.b], [pb.b])
                          k.mm(out, kr_, QRs.ap[hb:hb + 32, p, r0:r0 + 16], False, True, rd + [QRs.b], [pb.b])
                  sc = 96.0 ** -0.5
                  for par in range(2):
                      pb = pss2[par]
                      sv2 = sv2b[par]
                      k.act(sPT.ap[:, 3 * par:3 * par + 3, 0:8, :], sv2[:, :, 0:8, :], AF.Exp, [pb.b], [sPT.b],
                            scale=sc, accum=(par > 0))
                      k.act(sPT.ap[r0:r0 + 16, 3 * par:3 * par + 3, 8, :], sv2[r0:r0 + 16, :, 8, :], AF.Exp, [pb.b],
                            [sPT.b], scale=sc, accum=True)
                  pso2 = pM[2]
                  ov2 = pso2.ap[:, 0:390].rearrange("p (h e) -> p h e", h=6)
                  for h in range(6):
                      s_ = (h % 2) * 3 + h // 2
                      for w in range(8):
                          k.mm(ov2[r0:r0 + 16, h, :], sPT.ap[:, s_, w, :], sc_V.ap[:, w, h, :], w == 0, False,
                               [sPT.b, sc_V.b], [pso2.b])
                      k.mm(ov2[r0:r0 + 16, h, :], sPT.ap[r0:r0 + 16, s_, 8, :], Vaug.ap[r0:r0 + 16, h, :], False, True,
                           [sPT.b, Vaug.b], [pso2.b])
                  k.P.op("dve", lambda e, ov2=ov2, r0=r0: e.reciprocal(rden.ap[r0:r0 + 16, :], ov2[r0:r0 + 16, :, 64]),
                         reads=[pso2.b], writes=[rden.b])
                  k.tt("dve", merged_s.ap[r0:r0 + 16, t, 0:384].rearrange("p (h e) -> p h e", h=6),
                       ov2[r0:r0 + 16, :, 0:64], bc_last(rden.ap[r0:r0 + 16, :], 64), ALU.mult,
                       [pso2.b, rden.b], [merged_s.b], accum=True)

          seq = [("p", b) for b in range(NB)] + [("s", 0), ("s", 1)]

          def load_X(i):
              kind, idx = seq[i]
              if kind == "p":
                  k.dma("sp", Xt[idx % 2].ap, x_src[idx * 128:(idx + 1) * 128, :], [], [Xt[idx % 2].b])
                  k.dma("sp", cs_blk[idx % 2].ap, cs_p[idx * 128:(idx + 1) * 128, :], [], [cs_blk[idx % 2].b])

          load_X(0)
          for _ in front_gen(seq[0][0], seq[0][1], 0, 0):
              pass
          for i, (kind, idx) in enumerate(seq):
              nxt = None
              if i + 1 < len(seq):
                  load_X(i + 1)
                  nxt = front_gen(seq[i + 1][0], seq[i + 1][1], (i + 1) % 2, (i + 1) % 6)

              def adv(nxt=nxt):
                  if nxt is not None:
                      next(nxt, None)
              a1_block(kind, idx, i % 2, i % 6, adv)
              if nxt is not None:
                  for _ in nxt:
                      pass
          P.barrier()

          A.release(persist_mark)
          NG = T // 512
          KTr = sb([128, 3, T], BF16)
          VAr = sb([128, NB, 384], BF16)
          kvb = [Buf() for _ in range(NG)]
          Qg = [sb([128, 3, 512], BF16) for _ in range(2)]
          oT = [sb([65, 512], F32) for _ in range(2)]
          om = [sb([128, 4, 192], BF16) for _ in range(2)]
          mfull = [sb([128, 4, D], BF16) for _ in range(2)]
          rd2 = sb([128, 4], F32)
          W_o = sb([128, 8, 1024], BF16)
          load_weight(W_o.ap, W_o.b, 8, 1024, lambda c, s0, ln: w_out[l, c * 128:(c + 1) * 128, s0:s0 + ln], None,
                      engs=("pool", "dve", "act"))
          Xq = [sb([128, D], F32) for _ in range(4)]
          mT = sb([128, 8, 128], BF16)
          PTt = [sb([128, 1024], BF16) for _ in range(4)]
          Sreg = []
          for a_ in (0, 2, 4):
              t_ = TL(ps_t[:, 512 * a_:512 * a_ + 1024])
              t_.b.locks = (bank_locks[a_], bank_locks[a_ + 1])
              Sreg.append(t_)
          pO = [TLP(6)]
          pTr = TLP(7)
          pTm = TLP(7, True)
          pWo = TLP(7)
          sc = 96.0 ** -0.5

          def wo_block(mparts, xv, xb, xreads):
            c = 0
            for (ap, buf) in mparts:
                w = ap.shape[1]
                for cc in range(w // 128):
                    k.tr(pTm.ap[:, c * 128:(c + 1) * 128], ap[:, cc * 128:(cc + 1) * 128], ident_b.ap,
                         [buf, ident_b.b], [pTm.b])
                    c += 1
            assert c == 8
            k.cp("dve", mT.ap, pTm.ap.rearrange("p (c t) -> p c t", c=8), [pTm.b], [mT.b])
            for n in range(2):
                for c in range(8):
                    k.mm(pWo.ap, mT.ap[:, c, :], W_o.ap[:, c, n * 512:(n + 1) * 512], c == 0, c == 7,
                         [mT.b, W_o.b], [pWo.b])
                k.tt("dve", xv[:, n * 512:(n + 1) * 512], pWo.ap, xv[:, n * 512:(n + 1) * 512], ALU.add,
                     [pWo.b, xb] + xreads, [xb], accum=True)

          chk("a2w")
          hi = [0]
          xq_i2 = [0]
          ui = [0]
          for hp in range(2):
              def a2_load(g, hp=hp):
                  t0 = g * 512
                  k.dma("sp", KTr.ap[:, :, t0:t0 + 512], KT[3 * hp:3 * hp + 3, :, t0:t0 + 512].rearrange("a p t -> p a t"),
                        [], [kvb[g]])
                  k.dma("sp", VAr.ap[:, 4 * g:4 * g + 4, :],
                        VA[t0:t0 + 512, 384 * hp:384 * hp + 384].rearrange("(w p) c -> p w c", p=128), [], [kvb[g]],
                        accum=True)
                  k.dma("sp", Qg[g % 2].ap, QT[3 * hp:3 * hp + 3, :, t0:t0 + 512].rearrange("a p t -> p a t"), [],
                        [Qg[g % 2].b])
                  if hp == 1:
                      mf = mfull[g % 2]
                      k.dma("sp", mf.ap[:, :, 0:192], MERGED[t0:t0 + 512, 0:192].rearrange("(a p) c -> p a c", p=128),
                            [], [mf.b])
                      k.dma("sp", mf.ap[:, :, 384:1024],
                            MERGED[t0:t0 + 512, 384:1024].rearrange("(a p) c -> p a c", p=128), [], [mf.b], accum=True)

              units = []
              for g in range(NG):
                  for hl in range(3):
                      us = []
                      for kt in range(0, 4 * g, 2):
                          us.append([g, hl, [kt, kt + 1], 0, False, False])
                      for j in range(4):
                          us.append([g, hl, [4 * g + j], 128 * j, False, False])
                      us[0][4] = True
                      us[-1][5] = True
                      units.extend(us)

              def emitS(i, units=units):
                  g, hl, kts, c0, first, last = units[i]
                  sr = Sreg[(ui[0] + i) % 3]
                  qg = Qg[g % 2]
                  for n_, kt in enumerate(kts):
                      k.mm(sr.ap[:, n_ * 512 + c0:(n_ + 1) * 512], KTr.ap[:, hl, kt * 128:(kt + 1) * 128],
                           qg.ap[:, hl, c0:512], True, True, [kvb[kt // 4], qg.b], [sr.b])

              def emitE_PV(i, units=units, hp=hp):
                  g, hl, kts, c0, first, last = units[i]
                  sr = Sreg[(ui[0] + i) % 3]
                  pt = PTt[(ui[0] + i) % 4]
                  n = len(kts)
                  k.act(pt.ap[:, c0:n * 512], sr.ap[:, c0:n * 512], AF.Exp, [sr.b], [pt.b], scale=sc)
                  if kts[0] >= 4 * g:
                      k.memset("pool", pt.ap[64:128, c0:c0 + 64], 0.0, [pt.b])
                  po = pO[0]
                  for n_, kt in enumerate(kts):
                      k.mm(po.ap[:, c0:512], VAr.ap[:, kt, hl * 128:(hl + 1) * 128],
                           pt.ap[:, n_ * 512 + c0:(n_ + 1) * 512],
                           first and n_ == 0, last and n_ == n - 1, [kvb[kt // 4], pt.b], [po.b])
                  if last:
                      ot = oT[hi[0] % 2]
                      hi[0] += 1
                      k.cp("dve", ot.ap, po.ap[0:65, :], [po.b], [ot.b])
                      for q4 in range(4):
                          k.tr(pTr.ap[:, q4 * 65:(q4 + 1) * 65], ot.ap[:, q4 * 128:(q4 + 1) * 128], ident_f.ap[0:65, 0:65],
                               [ot.b, ident_f.b], [pTr.b])
                      tv = pTr.ap[:, 0:260].rearrange("p (a e) -> p a e", a=4)
                      k.P.op("dve", lambda e, tv=tv: e.reciprocal(rd2.ap, tv[:, :, 64]), reads=[pTr.b], writes=[rd2.b])
                      if hp == 0:
                          dst, dbuf = om[g % 2].ap[:, :, hl * 64:(hl + 1) * 64], om[g % 2].b
                      else:
                          dst, dbuf = mfull[g % 2].ap[:, :, 192 + hl * 64:192 + (hl + 1) * 64], mfull[g % 2].b
                      k.tt("dve", dst, tv[:, :, 0:64], bc_last(rd2.ap, 64), ALU.mult, [pTr.b, rd2.b], [dbuf], accum=True)
                      if hl == 2 and hp == 0:
                          k.dma("sp", MERGED[g * 512:(g + 1) * 512, 0:192].rearrange("(a p) c -> p a c", p=128),
                                om[g % 2].ap, [om[g % 2].b], [])
                      if hl == 2 and hp == 1:
                          mf = mfull[g % 2]
                          if dbg:
                              k.dma("sp", MERGED[g * 512:(g + 1) * 512, 192:384].rearrange("(a p) c -> p a c", p=128),
                                    mf.ap[:, :, 192:384], [mf.b], [])
                          for q4 in range(4):
                              xq = Xq[q4]
                              r0 = g * 512 + q4 * 128
                              wo_block([(mf.ap[:, q4, :], mf.b)], xq.ap, xq.b, [])
                              k.dma("sp", XB[r0:r0 + 128, :], xq.ap, [xq.b], [])

              a2_load(0)
              if NG > 1:
                  a2_load(1)
              emitS(0)
              if len(units) > 1:
                  emitS(1)
              for i in range(len(units)):
                  g, hl, kts, c0, first, last = units[i]
                  if first and hl == 0 and g >= 1 and g + 1 < NG:
                      a2_load(g + 1)
                  if hp == 1 and first and hl == 2:
                      for q4 in range(4):
                          r0_ = g * 512 + q4 * 128
                          k.dma("sp", Xq[q4].ap, x_src[r0_:r0_ + 128, :], [], [Xq[q4].b])
                  if i + 2 < len(units):
                      emitS(i + 2)
                  emitE_PV(i)
              ui[0] += len(units)
              P.barrier()
          chk("a2p")
          for t in range(2):
              wo_block([(merged_s.ap[:, t, :], merged_s.b)], x_s.ap[:, t, :], x_s.b, [])
          P.barrier()
          chk("a2")

          A.release(persist_mark)
          W_u = sb([128, 8, 4096], BF16)
          W_d = sb([128, 32, 1024], BF16)
          n2T = sb([128, 8], F32)
          k.dma("sp", n2T.ap, norm2T[l, :, :], [], [n2T.b])
          Xg = [sb([128, 2, D], F32) for _ in range(2)]
          bst = list(stage) + [TL(Xg[i_].ap[:, s_, :]) for i_ in range(2) for s_ in range(2)]
          load_weight(W_u.ap, W_u.b, 8, 4096, lambda c, s0, ln: w_up[l, c * 128:(c + 1) * 128, s0:s0 + ln],
                      lambda c: (n2T.ap[:, c:c + 1], n2T.b), stages=bst, engs=("pool", "dve", "act"))
          load_weight(W_d.ap, W_d.b, 32, 1024, lambda c, s0, ln: w_down[l, c * 128:(c + 1) * 128, s0:s0 + ln], None,
                      stages=bst, engs=("pool", "dve", "act"))
          P.barrier()
          xn2 = sb([128, D], BF16)
          hT2 = sb([128, 8, 256], BF16)
          rl = [sb([128, 256], BF16) for _ in range(2)]
          uT = sb([128, 32, 256], BF16)
          junk2 = sb([128, D], BF16)
          ssb = sb([128, 4], F32)
          rsb = sb([128, 4], F32)
          lnt2 = sb([128, 4], F32)
          if last_layer:
              yo = sb([128, D], F32)
              fn_bc = sb([128, D], F32)
              k.dma("sp", fn_bc.ap, fnorm.ap().partition_broadcast(128), [], [fn_bc.b])
          pTb = [TLP(0, True), TLP(1, True)]
          pW = [TLP(2), TLP(3)]
          pD = [TLP(4), TLP(5), TLP(6), TLP(7)]
          NGB = T // 256

          def b_load(g):
              k.dma("sp", Xg[g % 2].ap, XB[g * 256:(g + 1) * 256, :].rearrange("(s p) c -> p s c", p=128), [],
                    [Xg[g % 2].b])

          wi = [0]
          di = [0]

          def b_group(Xv, Xb, out_fn):
              for s in range(2):
                  k.act(junk2.ap, Xv[:, s, :], AF.Square, [Xb], [junk2.b, ssb.b], scale=1.0 / 32.0,
                        accum_out=ssb.ap[:, s:s + 1])
                  k.rsqrt(rsb.ap[:, s:s + 1], ssb.ap[:, s:s + 1], EPS, TL2(lnt2, s, s + 1), [ssb.b], [rsb.b])
                  k.ts("dve", xn2.ap, Xv[:, s, :], rsb.ap[:, s:s + 1], None, ALU.mult, None, [Xb, rsb.b], [xn2.b])
                  pt = pTb[s]
                  for c in range(8):
                      k.tr(pt.ap[:, c * 128:(c + 1) * 128], xn2.ap[:, c * 128:(c + 1) * 128], ident_b.ap,
                           [xn2.b, ident_b.b], [pt.b])
                  k.cp("act", hT2.ap[:, :, s * 128:(s + 1) * 128], pt.ap.rearrange("p (c t) -> p c t", c=8), [pt.b],
                       [hT2.b], accum=True)
              for f in range(32):
                  pw = pW[wi[0] % 2]
                  wi[0] += 1
                  for c in range(8):
                      k.mm(pw.ap[:, 0:256], W_u.ap[:, c, f * 128:(f + 1) * 128], hT2.ap[:, c, :], c == 0, c == 7,
                           [W_u.b, hT2.b], [pw.b])
                  r = rl[f % 2]
                  k.act(r.ap, pw.ap[:, 0:256], AF.Relu, [pw.b], [r.b])
                  k.tt("pool" if f % 2 else "dve", uT.ap[:, f, :], r.ap, r.ap, ALU.mult, [r.b], [uT.b], accum=True)
              for s in range(2):
                  for n in range(2):
                      pd = pD[di[0] % 4]
                      di[0] += 1
                      for f in range(32):
                          k.mm(pd.ap, uT.ap[:, f, s * 128:(s + 1) * 128], W_d.ap[:, f, n * 512:(n + 1) * 512], f == 0,
                               f == 31, [uT.b, W_d.b], [pd.b])
                      k.tt("dve", Xv[:, s, n * 512:(n + 1) * 512], pd.ap, Xv[:, s, n * 512:(n + 1) * 512], ALU.add,
                           [pd.b, Xb], [Xb], accum=True)
              for s in range(2):
                  if last_layer:
                      k.act(junk2.ap, Xv[:, s, :], AF.Square, [Xb], [junk2.b, ssb.b], scale=1.0 / 32.0,
                            accum_out=ssb.ap[:, 2 + s:3 + s])
                      k.rsqrt(rsb.ap[:, 2 + s:3 + s], ssb.ap[:, 2 + s:3 + s], EPS, TL2(lnt2, 2 + s, 3 + s), [ssb.b], [rsb.b])
                      k.stt("dve", yo.ap, Xv[:, s, :], rsb.ap[:, 2 + s:3 + s], fn_bc.ap, ALU.mult, ALU.mult,
                            [Xb, rsb.b, fn_bc.b], [yo.b])
                      out_fn(s, yo.ap, yo.b, True)
                  else:
                      out_fn(s, Xv[:, s, :], Xb, False)

          b_load(0)
          for g in range(NGB):
              if g + 1 < NGB:
                  b_load(g + 1)

              def outp(s, ap, buf, final, g=g):
                  dst = y_p if final else XB
                  k.dma("sp", dst[g * 256 + s * 128:g * 256 + (s + 1) * 128, :], ap, [buf], [])
              b_group(Xg[g % 2].ap, Xg[g % 2].b, outp)

          def outs(s, ap, buf, final):
              if final:
                  k.dma("sp", y_s[s * 128:(s + 1) * 128, :], ap, [buf], [])
          b_group(x_s.ap, x_s.b, outs)
          P.barrier()


    except StopBuild:
        pass
    P.finalize()
    P.emit()
    es.close()
    return k


def host_inputs(T, L, core, x_prompt, x_sample, cache_mla_ckv, cache_mla_krope, state_gla, cache_ca_k, cache_ca_v,
                norm1, w_in, mla_q_norm, mla_w_qup, mla_kv_norm, mla_w_kvup, gla_w_gate2, gla_gate_bias,
                gla_out_norm, ca_rel_bias, w_out, norm2, w_up, w_down, final_norm, consts):
    f = lambda a: np.ascontiguousarray(np.asarray(a, dtype=np.float32))
    c = core
    sq = slice(4 * c, 4 * c + 4)
    xs = np.zeros((2, 2, 64, D), np.float32)
    xs[:, :, 0:16, :] = np.asarray(x_sample[sq]).reshape(2, 2, 16, D)
    m = {
        "xp": f(x_prompt[c % x_prompt.shape[0]][:T]),
        "xs": f(xs.reshape(256, D)),
        "c_ckv": f(cache_mla_ckv[:L, sq]),
        "c_kr": f(cache_mla_krope[:L, sq]),
        "st_gla": f(state_gla[:L, sq]),
        "c_cak": f(np.asarray(cache_ca_k[:L, sq]).reshape(L, 4, 512, 384)),
        "c_cav": f(np.asarray(cache_ca_v[:L, sq]).reshape(L, 4, 512, 384)),
        "w_in": f(w_in[:L]), "w_qup": f(mla_w_qup[:L]), "w_kvup": f(mla_w_kvup[:L]),
        "w_gate2": f(gla_w_gate2[:L]), "gate_bias": f(gla_gate_bias[:L]),
        "w_out": f(w_out[:L]), "w_up": f(w_up[:L]), "w_down": f(w_down[:L]),
        "norm1T": f(np.asarray(norm1[:L]).reshape(L, 8, 128).transpose(0, 2, 1)),
        "norm2T": f(np.asarray(norm2[:L]).reshape(L, 8, 128).transpose(0, 2, 1)),
        "qnormT": f(np.asarray(mla_q_norm[:L]).reshape(L, 2, 128).transpose(0, 2, 1)),
        "kvnorm": f(mla_kv_norm[:L]), "glanorm": f(gla_out_norm[:L]), "fnorm": f(final_norm),
    }
    idx = np.clip(np.arange(767) - 639, -128, 128) + 128
    m["ext2"] = f(np.asarray(ca_rel_bias[:L])[:, idx, :].transpose(0, 2, 1))
    m.update(consts)
    return m


def host_consts(T):
    ident = np.eye(128, dtype=np.float32)
    J = np.ascontiguousarray(ident[::-1])
    j = np.arange(128)
    U = ((j[:, None] <= j[None, :]) & (j[:, None] // 64 == j[None, :] // 64)).astype(np.float32)
    inv = np.power(np.float32(10000.0), -np.arange(16, dtype=np.float32) / np.float32(16))

    def table(pos):
        ang = pos.astype(np.float32)[:, None] * inv[None, :].astype(np.float32)
        c, s = np.cos(ang).astype(np.float32), np.sin(ang).astype(np.float32)
        return np.ascontiguousarray(np.concatenate([c, c, -s, s], axis=1).astype(np.float32))
    cs_p = table(np.arange(T))
    cs_s = table(1024 + (np.arange(128) % 64))
    return {"c_ident": ident, "c_J": J, "c_U": U, "cs_p": cs_p, "cs_s": cs_s}


_CACHE = {}


def run(T, L, dbg, inputs):
    key = (T, L, dbg)
    if key not in _CACHE:
        _CACHE[key] = build_program(T, L, dbg)
    kk = _CACHE[key]
    consts = host_consts(T)
    in_maps = [host_inputs(T, L, c, consts=consts, **inputs) for c in range(NCORES)]
    res = run_bass_kernel_spmd(kk.nc, in_maps, core_ids=list(range(NCORES)))
    return res.results


def kernel(**inputs):
    T, L = SEQ, DEPTH
    R = run(T, L, False, inputs)
    B = 2
    y_prompt = np.stack([R[c]["y_p"] for c in range(B)])

    def samp(name, w):
        o = np.stack([R[c][name] for c in range(NCORES)])
        return o

    ys = np.stack([R[c]["y_s"] for c in range(NCORES)]).reshape(NCORES, 2, 2, 64, D)[:, :, :, 0:16, :]
    y_sample = ys.reshape(32, 16, D)
    p_ckv = np.stack([R[c]["p_ckv"] for c in range(B)], axis=1)
    p_kr = np.stack([R[c]["p_kr"] for c in range(B)], axis=1)
    p_gla = np.stack([R[c]["p_gla"] for c in range(B)], axis=1)
    p_cak = np.stack([R[c]["p_cak"] for c in range(B)], axis=1).reshape(L, B, 512, 6, 64)
    p_cav = np.stack([R[c]["p_cav"] for c in range(B)], axis=1).reshape(L, B, 512, 6, 64)

    def stile(name, w):
        o = np.stack([R[c][name] for c in range(NCORES)], axis=1)
        o = o.reshape(L, NCORES, 2, 2, 64, w)[:, :, :, :, 0:16, :]
        return np.ascontiguousarray(o.reshape(L, 32, 16, w))
    s_ckv = stile("s_ckv", 128)
    s_kr = stile("s_kr", 32)
    s_gla = np.stack([R[c]["s_gla"] for c in range(NCORES)], axis=1).reshape(L, 32, 4, 64, 64)
    s_cak = stile("s_cak", 384).reshape(L, 32, 16, 6, 64)
    s_cav = stile("s_cav", 384).reshape(L, 32, 16, 6, 64)
    outs = (y_prompt, y_sample, p_ckv, p_kr, p_gla, p_cak, p_cav, s_ckv, s_kr, s_gla, s_cak, s_cav)
    return tuple(np.ascontiguousarray(o.astype(np.float32)) for o in outs)
```

```python
import math
import numpy as np
import ml_dtypes
import concourse.bass as bass
import concourse.mybir as mybir
from concourse.ap import AP as RawAP
from concourse.bass_utils import run_bass_kernel_spmd

F32 = mybir.dt.float32
BF16 = mybir.dt.bfloat16
U8 = mybir.dt.uint8
AF = mybir.ActivationFunctionType
ALU = mybir.AluOpType
AX = mybir.AxisListType

D = 1024
DEPTH = 2
SEQ = 8192
EPS = 1e-6
NCORES = 8
SEM_EPOCH = 30000


class Buf:
    __slots__ = ("w", "r", "pr", "locks")

    def __init__(self):
        self.w = {}
        self.r = {}
        self.pr = {}
        self.locks = ()


class BankLock:
    __slots__ = ("last",)

    def __init__(self):
        self.last = None


class Op:
    __slots__ = ("stream", "dom", "fn", "deps", "n", "signal", "val", "semkey", "isdma", "key")


NDMASEM = {"sp": 32, "pool": 16, "act": 8}


class Prog:
    STREAMS = ("pe", "act", "dve", "pool", "sp")

    def __init__(self, nc):
        self.nc = nc
        self.ops = {s: [] for s in self.STREAMS}
        self.domops = {}
        self.pending = {s: {} for s in self.STREAMS}

    def op(self, stream, fn, reads=(), writes=(), dma=False, accum=False):
        o = Op()
        o.stream = stream
        o.isdma = dma
        o.dom = stream + ("_dma" if dma else "")
        o.fn = fn
        o.signal = False
        lst = self.domops.setdefault(o.dom, [])
        o.n = len(lst) + 1
        lst.append(o)
        o.key = (o.dom, o.n) if dma else o.dom
        deps = dict(self.pending[stream])
        self.pending[stream] = {}

        def add(d):
            for k, p in d.items():
                if p is o:
                    continue
                q = deps.get(k)
                if q is None or q.n < p.n:
                    deps[k] = p

        for b in reads:
            add(b.w)
        for b in writes:
            if b.r:
                add(b.r)
                add(b.w)
            elif accum:
                add(b.pr)
            else:
                add(b.w)
                add(b.pr)
        seen = set()
        for b in list(reads) + list(writes):
            for lk in b.locks:
                if id(lk) in seen:
                    continue
                seen.add(id(lk))
                p = lk.last
                if p is not None and p.stream != stream:
                    q = deps.get(p.key)
                    if q is None or q.n < p.n:
                        deps[p.key] = p
                lk.last = o
        if dma:
            N = NDMASEM[stream]
            if o.n > N:
                p = lst[o.n - N - 1]
                deps[p.key] = p
        if stream == "pe" and not dma:
            deps.pop("pe", None)
        o.deps = deps
        for p in deps.values():
            p.signal = True
        for b in reads:
            b.r[o.key] = o
        for b in writes:
            if b.r:
                b.pr = dict(b.r)
                b.pr.pop(o.key, None)
                b.w = {o.key: o}
                b.r = {}
            elif accum:
                b.w[o.key] = o
            else:
                b.w = {o.key: o}
        self.ops[stream].append(o)
        return o

    def barrier(self):
        snap = {}
        for dom, lst in self.domops.items():
            if dom.endswith("_dma"):
                N = NDMASEM[dom[:-4]]
                for p in lst[-N:]:
                    snap[p.key] = p
            elif lst:
                snap[lst[-1].key] = lst[-1]
        for s in self.STREAMS:
            for k, p in snap.items():
                q = self.pending[s].get(k)
                if q is None or q.n < p.n:
                    self.pending[s][k] = p

    def finalize(self):
        self.barrier()
        self.op("sp", None)
        self.semkeys = set()
        for dom, lst in self.domops.items():
            if dom.endswith("_dma"):
                N = NDMASEM[dom[:-4]]
                for o in lst:
                    o.signal = True
                    o.semkey = (dom, (o.n - 1) % N)
                    o.val = ((o.n - 1) // N + 1) * 16
                    self.semkeys.add(o.semkey)
            else:
                cnt = 0
                ep = 0
                for o in lst:
                    if o.signal:
                        if cnt >= SEM_EPOCH:
                            ep += 1
                            cnt = 0
                        cnt += 1
                        o.val = cnt
                        o.semkey = (dom, ep)
                        self.semkeys.add(o.semkey)

    def emit(self):
        nc = self.nc
        import contextlib
        with contextlib.ExitStack() as es:
            sems = {}
            for k in sorted(self.semkeys):
                sems[k] = es.enter_context(nc.semaphore("s_%s_%d" % k))
            block = es.enter_context(nc.Block())

            def run(stream, eng):
                known = {}
                for o in self.ops[stream]:
                    for p in o.deps.values():
                        if known.get(p.semkey, 0) >= p.val:
                            continue
                        known[p.semkey] = p.val
                        eng.wait_ge(sems[p.semkey], p.val)
                    if o.fn is None:
                        continue
                    ins = o.fn(eng)
                    if o.signal:
                        ins.then_inc(sems[o.semkey], 16 if o.isdma else 1)

            @block.tensor
            def _(e):
                run("pe", e)

            @block.scalar
            def _(e):
                run("act", e)

            @block.vector
            def _(e):
                run("dve", e)

            @block.gpsimd
            def _(e):
                run("pool", e)

            @block.sync
            def _(e):
                run("sp", e)


class Arena:
    def __init__(self, ap_u8, size):
        self.ap = ap_u8
        self.size = size
        self.off = 0

    def mark(self):
        return self.off

    def release(self, m):
        self.off = m

    def alloc(self, shape, dt):
        esz = 2 if dt == BF16 else 4
        free = 1
        for s in shape[1:]:
            free *= s
        nb = (free * esz + 63) // 64 * 64
        assert self.off + nb <= self.size, ("arena overflow", self.off, nb, self.size)
        v = self.ap[0:shape[0], self.off:self.off + free * esz].bitcast(dt)
        self.off += nb
        if len(shape) == 3:
            v = v.rearrange("p (a b) -> p a b", a=shape[1])
        elif len(shape) == 4:
            v = v.rearrange("p (a b c) -> p a b c", a=shape[1], b=shape[2])
        return v


class K:
    def __init__(self, T, depth=DEPTH, dbg=False):
        self.T = T
        self.NB = T // 128
        self.depth = depth
        self.dbg = dbg
        nc = bass.Bass("TRN2", target_bir_lowering=False)
        self.nc = nc
        self.P = Prog(nc)
        self.din = {}
        self.dout = {}

    def inp(self, name, shape, dt=F32):
        t = self.nc.dram_tensor(name, list(shape), dt, kind="ExternalInput")
        self.din[name] = t
        return t

    def outp(self, name, shape, dt=F32):
        t = self.nc.dram_tensor(name, list(shape), dt, kind="ExternalOutput")
        self.dout[name] = t
        return t

    def scratch(self, name, shape, dt):
        return self.nc.dram_tensor(name, list(shape), dt)

    def mm(self, out, lhsT, rhs, start, stop, reads, writes):
        return self.P.op("pe", lambda e: e.matmul(out, lhsT, rhs, start=start, stop=stop),
                         reads=reads, writes=writes, accum=True)

    def tr(self, out, in_, ident, reads, writes):
        return self.P.op("pe", lambda e: e.transpose(out, in_, ident), reads=reads, writes=writes, accum=True)

    def act(self, out, in_, func, reads, writes, bias=None, scale=None, accum_out=None, eng="act", accum=False):
        kw = {}
        if bias is not None:
            kw["bias"] = bias
        if scale is not None:
            kw["scale"] = scale
        if accum_out is not None:
            kw["accum_out"] = accum_out
        return self.P.op("act", lambda e: e.activation(out, in_, func, **kw), reads=reads, writes=writes, accum=accum)

    def ts(self, eng, out, in0, s1, s2, op0, op1, reads, writes, accum=False):
        if op1 is None:
            return self.P.op(eng, lambda e: e.tensor_scalar(out, in0, s1, None, op0), reads=reads, writes=writes, accum=accum)
        return self.P.op(eng, lambda e: e.tensor_scalar(out, in0, s1, s2, op0, op1), reads=reads, writes=writes, accum=accum)

    def tt(self, eng, out, in0, in1, op, reads, writes, accum=False):
        return self.P.op(eng, lambda e: e.tensor_tensor(out, in0, in1, op), reads=reads, writes=writes, accum=accum)

    def stt(self, eng, out, in0, scalar, in1, op0, op1, reads, writes, accum=False):
        return self.P.op(eng, lambda e: e.scalar_tensor_tensor(out, in0, scalar, in1, op0, op1),
                         reads=reads, writes=writes, accum=accum)

    def cp(self, eng, out, in_, reads, writes, accum=False):
        if eng == "act":
            return self.P.op("act", lambda e: e.copy(out, in_), reads=reads, writes=writes, accum=accum)
        return self.P.op(eng, lambda e: e.tensor_copy(out, in_), reads=reads, writes=writes, accum=accum)

    def rsqrt(self, out, in_, eps, tmp, reads, writes):
        self.act(tmp.ap if hasattr(tmp, "ap") else tmp, in_, AF.Ln, reads, [self._tmpb(tmp)], bias=eps)
        return self.act(out, tmp.ap if hasattr(tmp, "ap") else tmp, AF.Exp, [self._tmpb(tmp)], writes, scale=-0.5)

    def _tmpb(self, tmp):
        return tmp.b

    def memset(self, eng, ap, val, writes, accum=False):
        return self.P.op(eng, lambda e: e.memset(ap, val), writes=writes, accum=accum)

    def dma(self, q, out, in_, reads, writes, accum=False, slow=False):
        if slow:
            return self.P.op(q, lambda e: e.dma_start(out=out, in_=in_, allow_slow_non_contiguous=True),
                             reads=reads, writes=writes, dma=True, accum=accum)
        return self.P.op(q, lambda e: e.dma_start(out=out, in_=in_), reads=reads, writes=writes, dma=True, accum=accum)


class TL:
    __slots__ = ("ap", "b")

    def __init__(self, ap):
        self.ap = ap
        self.b = Buf()


def TL2(t, a, b):
    v = TL(t.ap[:, a:b])
    v.b = t.b
    return v


def bc_mid(ap2d, n):
    p, f = ap2d.shape
    return ap2d.unsqueeze(1).broadcast_to([p, n, f])


def bc_last(ap2d, n):
    p, a = ap2d.shape
    return ap2d.unsqueeze(2).broadcast_to([p, a, n])


W_IN_SEGS = [(0, 0, 416), (416, 1184, 16), (432, 416, 768), (1200, 1200, 1408)]


def seg_map(d0, d1, segs):
    out = []
    for (ds, ss, ln) in segs:
        a = max(d0, ds)
        b = min(d1, ds + ln)
        if a < b:
            out.append((a, ss + (a - ds), b - a))
    return out


def build_program(T, depth=DEPTH, dbg=False):
    k = K(T, depth, dbg)
    nc = k.nc
    P = k.P
    NB = T // 128
    L = depth
    assert T % 512 == 0

    xp = k.inp("xp", [T, D])
    xs = k.inp("xs", [256, D])
    c_ckv = k.inp("c_ckv", [L, 4, 1024, 128])
    c_kr = k.inp("c_kr", [L, 4, 1024, 32])
    st_gla = k.inp("st_gla", [L, 4, 4, 64, 64])
    c_cak = k.inp("c_cak", [L, 4, 512, 384])
    c_cav = k.inp("c_cav", [L, 4, 512, 384])
    w_in = k.inp("w_in", [L, 1024, 2608])
    w_qup = k.inp("w_qup", [L, 256, 576])
    w_kvup = k.inp("w_kvup", [L, 128, 768])
    w_gate2 = k.inp("w_gate2", [L, 16, 256])
    gate_bias = k.inp("gate_bias", [L, 256])
    w_out = k.inp("w_out", [L, 1024, 1024])
    w_up = k.inp("w_up", [L, 1024, 4096])
    w_down = k.inp("w_down", [L, 4096, 1024])
    norm1T = k.inp("norm1T", [L, 128, 8])
    norm2T = k.inp("norm2T", [L, 128, 8])
    qnormT = k.inp("qnormT", [L, 128, 2])
    kvnorm = k.inp("kvnorm", [L, 128])
    glanorm = k.inp("glanorm", [L, 256])
    fnorm = k.inp("fnorm", [D])
    ext2 = k.inp("ext2", [L, 6, 767])
    c_ident = k.inp("c_ident", [128, 128])
    c_J = k.inp("c_J", [128, 128])
    c_U = k.inp("c_U", [128, 128])
    cs_p = k.inp("cs_p", [T, 64])
    cs_s = k.inp("cs_s", [128, 64])

    y_p = k.outp("y_p", [T, D])
    y_s = k.outp("y_s", [256, D])
    p_ckv = k.outp("p_ckv", [L, T, 128])
    p_kr = k.outp("p_kr", [L, T, 32])
    p_gla = k.outp("p_gla", [L, 4, 64, 64])
    p_cak = k.outp("p_cak", [L, 512, 384])
    p_cav = k.outp("p_cav", [L, 512, 384])
    s_ckv = k.outp("s_ckv", [L, 256, 128])
    s_kr = k.outp("s_kr", [L, 256, 32])
    s_gla = k.outp("s_gla", [L, 4, 4, 64, 64])
    s_cak = k.outp("s_cak", [L, 256, 384])
    s_cav = k.outp("s_cav", [L, 256, 384])

    mk = k.outp if dbg else (lambda n, s, d: k.scratch(n, s, d))
    XB = mk("XB", [T, D], F32)
    QNT = mk("QNT", [3, 128, T], BF16)
    QRT = mk("QRT", [3, 128, T], BF16)
    KNT = mk("KNT", [3, 128, T], BF16)
    KR3T = mk("KR3T", [128, T], BF16)
    VA = mk("VA", [T, 768], BF16)
    QT = mk("QT", [6, 128, T], BF16)
    KT = mk("KT", [6, 128, T], BF16)
    MERGED = mk("MERGED", [T, D], BF16)

    ARENA_BYTES = 206 * 1024
    arena_t = nc.alloc_sbuf_tensor("arena", [128, ARENA_BYTES], U8) if False else None
    import contextlib
    es = contextlib.ExitStack()
    arena_t = es.enter_context(nc.sbuf_tensor("arena", [128, ARENA_BYTES], U8))
    ps_t = es.enter_context(nc.psum_tensor("ps", [128, 4096], F32))
    A = Arena(arena_t, ARENA_BYTES)

    def bank(i):
        return ps_t[:, 512 * i:512 * (i + 1)]

    def bank_bf(i):
        return ps_t[:, 512 * i:512 * (i + 1)].bitcast(BF16)

    def sb(shape, dt):
        return TL(A.alloc(shape, dt))

    bank_locks = [BankLock() for _ in range(8)]

    def TLP(i, bf=False):
        t = TL(bank_bf(i) if bf else bank(i))
        t.b.locks = (bank_locks[i],)
        return t

    ident_f = sb([128, 128], F32)
    ident_b = sb([128, 128], BF16)
    J_b = sb([128, 128], BF16)
    U_f = sb([128, 128], F32)
    cs_st = sb([128, 64], F32)
    x_s = sb([128, 2, D], F32)
    merged_s = sb([128, 2, D], BF16)

    k.dma("sp", ident_f.ap, c_ident[:, :], [], [ident_f.b])
    k.dma("pool", ident_b.ap, c_ident[:, :], [], [ident_b.b])
    k.dma("pool", J_b.ap, c_J[:, :], [], [J_b.b])
    k.dma("sp", U_f.ap, c_U[:, :], [], [U_f.b])
    k.dma("sp", cs_st.ap, cs_s[:, :], [], [cs_st.b])
    k.dma("sp", x_s.ap, xs.ap().rearrange("(t p) c -> p t c", p=128), [], [x_s.b])

    stage = [sb([128, 1024], F32) for _ in range(3)]
    stage_i = [0]

    def load_weight(dst, dstbuf, nchunks, ncols, src_fn, scale_fn, segs=None, eng="pool", stages=None, engs=None):
        stages = stages or stage
        for c in range(nchunks):
            for d0 in range(0, ncols, 1024):
                d1 = min(ncols, d0 + 1024)
                st = stages[stage_i[0] % len(stages)]
                if engs:
                    eng = engs[stage_i[0] % len(engs)]
                stage_i[0] += 1
                pieces = seg_map(d0, d1, segs) if segs else [(d0, d0, d1 - d0)]
                for (a, s0, ln) in pieces:
                    k.dma("sp", st.ap[:, a - d0:a - d0 + ln], src_fn(c, s0, ln), [], [st.b], accum=True)
                if scale_fn is not None:
                    sap, sbuf = scale_fn(c)
                    if eng == "act":
                        k.act(dst[:, c, d0:d1], st.ap[:, 0:d1 - d0], AF.Copy, [st.b, sbuf], [dstbuf], scale=sap,
                              accum=True)
                    else:
                        k.ts(eng, dst[:, c, d0:d1], st.ap[:, 0:d1 - d0], sap, 1.0, ALU.mult, ALU.mult,
                             [st.b, sbuf], [dstbuf], accum=True)
                else:
                    k.cp(eng, dst[:, c, d0:d1], st.ap[:, 0:d1 - d0], [st.b], [dstbuf], accum=True)

    persist_mark = A.mark()
    import os
    STOP = os.environ.get("KSTOP", "")

    class StopBuild(Exception):
        pass

    def chk(name):
        if name == STOP:
            raise StopBuild()

    try:
      chk("consts")
      for l in range(L):
          last_layer = (l == L - 1)
          x_src = xp if l == 0 else XB
          A.release(persist_mark)
          W_in = sb([128, 8, 2608], BF16)
          W_qup = sb([128, 2, 576], BF16)
          W_kn = sb([128, 1, 384], BF16)
          W_v = sb([128, 1, 384], BF16)
          W_g2 = sb([17, 256], BF16)
          n1T = sb([128, 8], F32)
          qnT = sb([128, 2], F32)
          kvg_bc = sb([128, 128], F32)
          gn_bc = sb([128, 256], F32)
          BT = sb([128, 6, 640], BF16)
          BTs = sb([80, 6, 528], BF16)
          cs_blk = [sb([128, 64], F32) for _ in range(2)]
          k.dma("sp", n1T.ap, norm1T[l, :, :], [], [n1T.b])
          k.dma("sp", qnT.ap, qnormT[l, :, :], [], [qnT.b])
          k.dma("sp", kvg_bc.ap, kvnorm[l, :].partition_broadcast(128), [], [kvg_bc.b])
          k.dma("sp", gn_bc.ap, glanorm[l, :].partition_broadcast(128), [], [gn_bc.b])
          k.ts("dve", gn_bc.ap, gn_bc.ap, 8.0, None, ALU.mult, None, [gn_bc.b], [gn_bc.b])
          k.dma("pool", W_g2.ap[0:16, :], w_gate2[l, :, :], [], [W_g2.b], accum=True)
          k.dma("pool", W_g2.ap[16:17, :], gate_bias[l:l + 1, :], [], [W_g2.b], accum=True)
          for h in range(6):
              src = RawAP(ext2, (l * 6 + h) * 767, [[1, 128], [1, 640]])
              k.dma("pool", BT.ap[:, h, :], src, [], [BT.b], accum=True)
              src2 = RawAP(ext2, (l * 6 + h) * 767 + 112, [[1, 16], [1, 528]])
              k.dma("pool", BTs.ap[0:16, h, :], src2, [], [BTs.b], accum=True)
              k.dma("pool", BTs.ap[64:80, h, :], src2, [], [BTs.b], accum=True)
          k.memset("pool", BT.ap[0:64, :, 0:64], -30000.0, [BT.b])
          k.memset("pool", BT.ap[64:128, :, 576:640], -30000.0, [BT.b])
          load_weight(W_in.ap, W_in.b, 8, 2608,
                      lambda c, s0, ln: w_in[l, c * 128:(c + 1) * 128, s0:s0 + ln],
                      lambda c: (n1T.ap[:, c:c + 1], n1T.b), segs=W_IN_SEGS, engs=("pool", "dve", "act"))
          load_weight(W_qup.ap, W_qup.b, 2, 576,
                      lambda c, s0, ln: w_qup[l, c * 128:(c + 1) * 128, s0:s0 + ln],
                      lambda c: (qnT.ap[:, c:c + 1], qnT.b))
          wkv = w_kvup[l, :, :].rearrange("r (h t e) -> r h t e", h=6, t=2)
          k.dma("pool", W_kn.ap[:, 0, :].rearrange("r (h e) -> r h e", h=6), wkv[:, :, 0, :], [], [W_kn.b])
          k.dma("pool", W_v.ap[:, 0, :].rearrange("r (h e) -> r h e", h=6), wkv[:, :, 1, :], [], [W_v.b])

          chk("a1w")
          Xt = [sb([128, D], F32) for _ in range(2)]
          junk = sb([128, 512], BF16)
          junkF = sb([128, D], BF16)
          ss = sb([128, 4], F32)
          rstd = sb([128, 4], F32)
          lnt = sb([128, 8], F32)
          ssF = sb([128, 2], F32)
          rstdF = sb([128, 2], F32)
          lntF = sb([128, 2], F32)
          z0s = [sb([128, 432], F32) for _ in range(2)]
          z1s = [sb([128, 512], F32) for _ in range(2)]
          z2s = [sb([128, 512], F32) for _ in range(2)]
          xn = sb([128, D], BF16)
          hT = sb([128, D], BF16)
          qn = sb([128, 256], BF16)
          ckvn = sb([128, 128], F32)
          ckvb = sb([128, 128], BF16)
          krf = sb([128, 32], F32)
          krt = sb([128, 32], F32)
          kr3 = sb([128, 128], BF16)
          glr = sb([128, 16], BF16)
          TT = sb([128, 384], BF16)
          g_aug = sb([17, 128], BF16)
          Qf = sb([128, 6, 96], F32)
          qtA = sb([128, 6, 32], F32)
          qtB = sb([128, 6, 32], F32)
          Qn_tok = sb([128, 384], BF16)
          Qr_tok = sb([128, 3, 128], BF16)
          QNs = sb([128, 3, 128], BF16)
          QRs = sb([128, 3, 128], BF16)
          KNs = sb([128, 3, 128], BF16)
          KRs = sb([128, 128], BF16)
          Vaug = sb([128, 6, 65], BF16)
          Vpad = sb([128, 6, 128], BF16)
          Qp_tok = sb([128, 6, 128], BF16)
          QTs = sb([128, 6, 512], BF16)
          KTs = sb([128, 6, 512], BF16)
          e1 = sb([128, 256], F32)
          spl = sb([128, 256], F32)
          ebT = sb([128, 2, 128], F32)
          enbT = sb([128, 2, 128], F32)
          enb = sb([128, 256], F32)
          ebl = sb([128, 2, 2], F32)
          gqk = sb([128, 512], BF16)
          qeT = sb([128, 2, 128], BF16)
          keT = sb([128, 2, 128], BF16)
          ke = sb([128, 256], BF16)
          gv = sb([128, 256], BF16)
          e2 = sb([128, 256], F32)
          gate = sb([128, 256], F32)
          ATm = sb([128, 2, 2, 128], BF16)
          S_f = sb([128, 2, 64], F32)
          S_t = sb([128, 2, 64], F32)
          S_b = sb([128, 2, 64], BF16)
          o_sb = sb([128, 4, 64], F32)
          o_sq = sb([128, 4, 64], F32)
          o_ss = sb([128, 4], F32)
          o_r = sb([128, 4], F32)
          merged = sb([128, D], BF16)
          cqb2 = [sb([128, 384], BF16) for _ in range(2)]
          ckb2 = [sb([128, 384], BF16) for _ in range(2)]
          ckf = sb([128, 384], F32)
          cvf = sb([128, 384], F32)
          cqT = sb([128, 3, 128], BF16)
          Kring = [sb([128, 3, 128], BF16) for _ in range(5)]
          Vring = [sb([128, 6, 65], BF16) for _ in range(6)]
          PT = [sb([128, 5, 128], BF16) for _ in range(2)]
          rden = sb([128, 6], F32)
          sc_ckv = sb([128, 8, 128], BF16)
          sc_kr3 = sb([128, 8, 128], BF16)
          sc_ckvT = sb([128, 1024], BF16)
          sc_KN = sb([128, 3, 1024], BF16)
          sc_KR = sb([128, 1024], BF16)
          sc_V = sb([128, 8, 6, 65], BF16)
          sc_cak = sb([128, 4, 384], BF16)
          sc_caKT = sb([128, 3, 512], BF16)
          sc_caV = sb([128, 4, 6, 65], BF16)
          sPT = sb([128, 6, 9, 16], BF16)
          J16 = {0: J_b.ap[0:16, 112:128], 64: J_b.ap[64:80, 48:64]}

          k.memset("pool", g_aug.ap, 1.0, [g_aug.b])
          k.memset("pool", kr3.ap, 0.0, [kr3.b])
          k.memset("pool", Qr_tok.ap, 0.0, [Qr_tok.b])
          k.memset("pool", sc_kr3.ap, 0.0, [sc_kr3.b])
          k.memset("pool", Vaug.ap, 1.0, [Vaug.b])
          k.memset("pool", Vpad.ap, 0.0, [Vpad.b])
          k.memset("pool", Vpad.ap[:, :, 64:65], 1.0, [Vpad.b])
          k.memset("pool", Qp_tok.ap, 0.0, [Qp_tok.b])
          k.memset("pool", KTs.ap, 0.0, [KTs.b])
          for r in range(6):
              k.memset("pool", Vring[r].ap, 1.0, [Vring[r].b])
          k.memset("pool", sc_V.ap, 1.0, [sc_V.b])
          k.memset("pool", sc_caV.ap, 1.0, [sc_caV.b])
          k.memset("pool", merged_s.ap, 0.0, [merged_s.b])
          k.memset("dve", S_f.ap, 0.0, [S_f.b])
          k.memset("dve", S_b.ap, 0.0, [S_b.b])
          P.barrier()
          for h in range(6):
              pe_ = [TLP(4), TLP(5)]
              for w in range(5):
                  pb_ = pe_[0] if w < 4 else pe_[1]
                  k.mm(pb_.ap[:, (w % 4) * 128:(w % 4 + 1) * 128], BT.ap[:, h, w * 128:(w + 1) * 128], J_b.ap, True, True,
                       [BT.b, J_b.b], [pb_.b])
              k.act(BT.ap[:, h, 0:512], pe_[0].ap, AF.Exp, [pe_[0].b], [BT.b])
              k.act(BT.ap[:, h, 512:640], pe_[1].ap[:, 0:128], AF.Exp, [pe_[1].b], [BT.b], accum=True)
          P.barrier()

          pT1 = TLP(0, True)
          pT2 = TLP(1, True)
          pT2f = TLP(1)
          pT2f.b = pT2.b
          pZ = [TLP(2), TLP(3)]
          pM = [TLP(4), TLP(5), TLP(6), TLP(7)]
          pCA = [TL(ps_t[:, 2048:2048 + 640]), TL(ps_t[:, 3072:3072 + 640])]
          pCA[0].b = pM[0].b
          pCA[1].b = pM[2].b

          ZG = [(0, 432), (432, 944), (944, 1456), (1456, 1840), (1840, 2224), (2224, 2608)]

          def front_gen(kind, idx, par, vslot):
              smp = (kind == "s")
              if smp:
                  X = TL(x_s.ap[:, idx, :])
                  X.b = x_s.b
              else:
                  X = Xt[idx % 2]
              k.act(junkF.ap, X.ap, AF.Square, [X.b], [junkF.b, ssF.b], scale=1.0 / 32.0, accum_out=ssF.ap[:, 0:1])
              k.rsqrt(rstdF.ap[:, 0:1], ssF.ap[:, 0:1], EPS, TL2(lntF, 0, 1), [ssF.b], [rstdF.b])
              k.ts("dve", xn.ap, X.ap, rstdF.ap[:, 0:1], None, ALU.mult, None, [X.b, rstdF.b], [xn.b])
              for c in range(8):
                  k.tr(pT1.ap[:, c * 128:(c + 1) * 128], xn.ap[:, c * 128:(c + 1) * 128], ident_b.ap,
                       [xn.b, ident_b.b], [pT1.b])
              k.cp("act", hT.ap, pT1.ap, [pT1.b], [hT.b])
              yield
              need_out = smp or idx >= NB - 4
              for g, (c0, c1) in enumerate(ZG):
                  pz = pZ[g % 2]
                  for c in range(8):
                      k.mm(pz.ap[:, 0:c1 - c0], hT.ap[:, c * 128:(c + 1) * 128], W_in.ap[:, c, c0:c1],
                           c == 0, c == 7, [hT.b, W_in.b], [pz.b])
                  if g == 0:
                      k.cp("act", z0s[par].ap, pz.ap[:, 0:432], [pz.b], [z0s[par].b])
                  elif g == 1:
                      k.cp("dve", z1s[par].ap, pz.ap, [pz.b], [z1s[par].b])
                  elif g == 2:
                      k.cp("act", z2s[par].ap, pz.ap, [pz.b], [z2s[par].b])
                  elif g == 3:
                      k.act(cqb2[par].ap, pz.ap[:, 0:384], AF.Copy, [pz.b], [cqb2[par].b], scale=0.125)
                  elif g == 4:
                      k.cp("dve", ckb2[par].ap, pz.ap[:, 0:384], [pz.b], [ckb2[par].b])
                      if need_out:
                          k.cp("dve", ckf.ap, pz.ap[:, 0:384], [pz.b], [ckf.b])
                          dst = s_cak[l, idx * 128:(idx + 1) * 128, :] if smp else \
                              p_cak[l, (idx - (NB - 4)) * 128:(idx - (NB - 4) + 1) * 128, :]
                          k.dma("sp", dst, ckf.ap, [ckf.b], [])
                  else:
                      Vr_ = Vring[vslot]
                      k.cp("act", Vr_.ap[:, :, 0:64], pz.ap[:, 0:384].rearrange("p (h e) -> p h e", h=6), [pz.b],
                           [Vr_.b], accum=True)
                      if need_out:
                          k.cp("dve", cvf.ap, pz.ap[:, 0:384], [pz.b], [cvf.b])
                          dst = s_cav[l, idx * 128:(idx + 1) * 128, :] if smp else \
                              p_cav[l, (idx - (NB - 4)) * 128:(idx - (NB - 4) + 1) * 128, :]
                          k.dma("sp", dst, cvf.ap, [cvf.b], [])
                  yield

          def a1_block(kind, idx, par, vslot, adv):
              smp = (kind == "s")
              if smp:
                  cs = cs_st.ap
                  csb = cs_st.b
              else:
                  cs = cs_blk[idx % 2].ap
                  csb = cs_blk[idx % 2].b
              z0, z1, z2 = z0s[par], z1s[par], z2s[par]
              cqb, ckb = cqb2[par], ckb2[par]
              k.act(junk.ap[:, 0:256], z0.ap[:, 0:256], AF.Square, [z0.b], [junk.b, ss.b], scale=1.0 / 16.0,
                    accum_out=ss.ap[:, 1:2])
              k.act(junk.ap[:, 256:384], z0.ap[:, 256:384], AF.Square, [z0.b], [junk.b, ss.b],
                    scale=128.0 ** -0.5, accum_out=ss.ap[:, 2:3])
              k.rsqrt(rstd.ap[:, 1:3], ss.ap[:, 1:3], EPS, TL2(lnt, 1, 3), [ss.b], [rstd.b])
              k.ts("dve", qn.ap, z0.ap[:, 0:256], rstd.ap[:, 1:2], None, ALU.mult, None, [z0.b, rstd.b], [qn.b])
              k.stt("dve", ckvn.ap, z0.ap[:, 256:384], rstd.ap[:, 2:3], kvg_bc.ap, ALU.mult, ALU.mult,
                    [z0.b, rstd.b, kvg_bc.b], [ckvn.b])
              k.cp("pool", ckvb.ap, ckvn.ap, [ckvn.b], [ckvb.b])
              k.tt("dve", krf.ap, z0.ap[:, 384:416], cs[:, 0:32], ALU.mult, [z0.b, csb], [krf.b])
              k.tt("dve", krt.ap[:, 0:16], z0.ap[:, 400:416], cs[:, 32:48], ALU.mult, [z0.b, csb], [krt.b], accum=True)
              k.tt("dve", krt.ap[:, 16:32], z0.ap[:, 384:400], cs[:, 48:64], ALU.mult, [z0.b, csb], [krt.b], accum=True)
              k.cp("act", glr.ap, z0.ap[:, 416:432], [z0.b], [glr.b])
              k.tt("dve", krf.ap, krf.ap, krt.ap, ALU.add, [krf.b, krt.b], [krf.b])
              for r in range(2):
                  k.cp("pool", kr3.ap[:, 64 * r:64 * r + 32], krf.ap, [krf.b], [kr3.b], accum=True)
              if smp:
                  k.dma("sp", s_ckv[l, idx * 128:(idx + 1) * 128, :], ckvn.ap, [ckvn.b], [])
                  k.dma("sp", s_kr[l, idx * 128:(idx + 1) * 128, :], krf.ap, [krf.b], [])
              else:
                  k.dma("sp", p_ckv[l, idx * 128:(idx + 1) * 128, :], ckvn.ap, [ckvn.b], [])
                  k.dma("sp", p_kr[l, idx * 128:(idx + 1) * 128, :], krf.ap, [krf.b], [])
              k.tr(pT2.ap[:, 0:128], qn.ap[:, 0:128], ident_b.ap, [qn.b, ident_b.b], [pT2.b])
              k.tr(pT2.ap[:, 128:256], qn.ap[:, 128:256], ident_b.ap, [qn.b], [pT2.b])
              k.tr(pT2.ap[:, 256:384], ckvb.ap, ident_b.ap, [ckvb.b], [pT2.b])
              SK = os.environ.get("KSKIP", "")
              if "kr" not in SK:
                  k.tr(pT2.ap[:, 384:512], kr3.ap, ident_b.ap, [kr3.b], [pT2.b])
              if "glr" not in SK:
                  k.tr(pT2.ap[0:16, 512:640], glr.ap, ident_b.ap, [glr.b], [pT2.b])
              sl = (idx % 4) * 128 if not smp else 0
              if "tt" not in SK:
                  k.cp("act", TT.ap, pT2.ap[:, 0:384], [pT2.b], [TT.b])
              if smp:
                  k.cp("dve", KRs.ap[:, sl:sl + 128], pT2.ap[:, 384:512], [pT2.b], [KRs.b], accum=True)
              else:
                  k.cp("dve", KTs.ap[64:96, :, sl:sl + 128], bc_mid(pT2.ap[64:96, 384:512], 6), [pT2.b], [KTs.b],
                       accum=True)
              if "glr" not in SK:
                  k.cp("dve", g_aug.ap[0:16, :], pT2.ap[0:16, 512:640], [pT2.b], [g_aug.b], accum=True)
              adv()
              for hf in range(2):
                  pq = pM[hf]
                  for c in range(2):
                      k.mm(pq.ap[:, 0:288], TT.ap[:, c * 128:(c + 1) * 128], W_qup.ap[:, c, hf * 288:(hf + 1) * 288],
                           c == 0, c == 1, [TT.b, W_qup.b], [pq.b])
                  k.cp("act", Qf.ap[:, 3 * hf:3 * hf + 3, :], pq.ap[:, 0:288].rearrange("p (h e) -> p h e", h=3),
                       [pq.b], [Qf.b], accum=True)
              if smp:
                  pkn = pM[2]
                  for p in range(3):
                      k.mm(pkn.ap[:, p * 128:(p + 1) * 128], W_kn.ap[:, 0, p * 128:(p + 1) * 128], TT.ap[:, 256:384],
                           True, True, [TT.b, W_kn.b], [pkn.b])
                  pv = pM[3]
                  k.mm(pv.ap[:, 0:384], TT.ap[:, 256:384], W_v.ap[:, 0, :], True, True, [TT.b, W_v.b], [pv.b])
                  k.cp("act", KNs.ap[:, :, sl:sl + 128], pkn.ap[:, 0:384].rearrange("p (a t) -> p a t", a=3),
                       [pkn.b], [KNs.b], accum=True)
                  k.cp("dve", Vaug.ap[:, :, 0:64], pv.ap[:, 0:384].rearrange("p (h e) -> p h e", h=6),
                       [pv.b], [Vaug.b], accum=True)
              else:
                  for r in range(2):
                      pk_ = pM[2 + r]
                      for j in range(3):
                          h = 3 * r + j
                          k.mm(pk_.ap[0:64, j * 128:(j + 1) * 128], W_kn.ap[:, 0, h * 64:(h + 1) * 64],
                               TT.ap[:, 256:384], True, True, [TT.b, W_kn.b], [pk_.b])
                      k.cp("act" if r == 0 else "dve", KTs.ap[0:64, 3 * r:3 * r + 3, sl:sl + 128],
                           pk_.ap[0:64, 0:384].rearrange("p (a t) -> p a t", a=3), [pk_.b], [KTs.b], accum=True)
                  pv = pM[2]
                  k.mm(pv.ap[:, 0:384], TT.ap[:, 256:384], W_v.ap[:, 0, :], True, True, [TT.b, W_v.b], [pv.b])
                  k.cp("act", Vpad.ap[:, :, 0:64], pv.ap[:, 0:384].rearrange("p (h e) -> p h e", h=6),
                       [pv.b], [Vpad.b], accum=True)
                  k.dma("sp", VA[idx * 128:(idx + 1) * 128, :], Vpad.ap.rearrange("p h e -> p (h e)"), [Vpad.b], [])
              adv()
              k.tt("dve", qtA.ap, Qf.ap[:, :, 64:96], bc_mid(cs[:, 0:32], 6), ALU.mult, [Qf.b, csb], [qtA.b])
              k.tt("dve", qtB.ap[:, :, 0:16], Qf.ap[:, :, 80:96], bc_mid(cs[:, 32:48], 6), ALU.mult, [Qf.b, csb],
                   [qtB.b], accum=True)
              k.tt("dve", qtB.ap[:, :, 16:32], Qf.ap[:, :, 64:80], bc_mid(cs[:, 48:64], 6), ALU.mult, [Qf.b, csb],
                   [qtB.b], accum=True)
              if smp:
                  k.tt("dve", Qr_tok.ap.rearrange("p a (hh c) -> p a hh c", hh=2)[:, :, :, 0:32],
                       qtA.ap.rearrange("p (a hh) e -> p a hh e", hh=2), qtB.ap.rearrange("p (a hh) e -> p a hh e", hh=2),
                       ALU.add, [qtA.b, qtB.b], [Qr_tok.b], accum=True)
                  k.cp("pool", Qn_tok.ap.rearrange("p (h e) -> p h e", h=6), Qf.ap[:, :, 0:64], [Qf.b], [Qn_tok.b])
                  for p in range(3):
                      k.tr(pT2.ap[:, p * 128:(p + 1) * 128], Qn_tok.ap[:, p * 128:(p + 1) * 128], ident_b.ap,
                           [Qn_tok.b, ident_b.b], [pT2.b])
                  for g2 in range(3):
                      k.tr(pT2.ap[:, 384 + g2 * 128:512 + g2 * 128], Qr_tok.ap[:, g2, :], ident_b.ap,
                           [Qr_tok.b], [pT2.b])
                  k.cp("act", QNs.ap[:, :, sl:sl + 128], pT2.ap[:, 0:384].rearrange("p (a t) -> p a t", a=3),
                       [pT2.b], [QNs.b], accum=True)
                  k.cp("dve", QRs.ap[:, :, sl:sl + 128], pT2.ap[:, 384:768].rearrange("p (a t) -> p a t", a=3),
                       [pT2.b], [QRs.b], accum=True)
              else:
                  k.tt("dve", Qp_tok.ap[:, :, 64:96], qtA.ap, qtB.ap, ALU.add, [qtA.b, qtB.b], [Qp_tok.b], accum=True)
                  k.cp("pool", Qp_tok.ap[:, :, 0:64], Qf.ap[:, :, 0:64], [Qf.b], [Qp_tok.b], accum=True)
                  for h in range(6):
                      k.tr(pT2.ap[:, h * 128:(h + 1) * 128], Qp_tok.ap[:, h, :], ident_b.ap,
                           [Qp_tok.b, ident_b.b], [pT2.b])
                  k.cp("act", QTs.ap[:, 0:3, sl:sl + 128], pT2.ap[:, 0:384].rearrange("p (a t) -> p a t", a=3),
                       [pT2.b], [QTs.b], accum=True)
                  k.cp("dve", QTs.ap[:, 3:6, sl:sl + 128], pT2.ap[:, 384:768].rearrange("p (a t) -> p a t", a=3),
                       [pT2.b], [QTs.b], accum=True)
                  if idx % 4 == 3:
                      t0 = (idx - 3) * 128
                      k.dma("sp", QT[:, :, t0:t0 + 512].rearrange("a p t -> p a t"), QTs.ap, [QTs.b], [])
                      k.dma("sp", KT[:, :, t0:t0 + 512].rearrange("a p t -> p a t"), KTs.ap, [KTs.b], [])
              adv()
              pg = pM[0]
              k.mm(pg.ap[:, 0:256], g_aug.ap, W_g2.ap, True, True, [g_aug.b, W_g2.b], [pg.b])
              k.act(e1.ap, pg.ap[:, 0:256], AF.Exp, [pg.b], [e1.b], scale=-1.0)
              k.act(spl.ap, e1.ap, AF.Ln, [e1.b], [spl.b], bias=1.0)
              k.mm(pg.ap[:, 256:512], U_f.ap, spl.ap, True, True, [U_f.b, spl.b], [pg.b])
              pc = pM[1]
              for p in range(2):
                  k.mm(pc.ap[:, p * 128:(p + 1) * 128], spl.ap[:, p * 128:(p + 1) * 128], U_f.ap, True, True,
                       [spl.b, U_f.b], [pc.b])
              cbT = pc.ap[:, 0:256].rearrange("p (a t) -> p a t", a=2)
              k.act(ebT.ap, cbT, AF.Exp, [pc.b], [ebT.b], scale=-1.0 / 16.0, bias=math.log(0.125))
              k.act(enbT.ap, cbT, AF.Exp, [pc.b], [enbT.b], scale=1.0 / 16.0)
              k.act(enb.ap, pg.ap[:, 256:512], AF.Exp, [pg.b], [enb.b], scale=1.0 / 16.0)
              lastc = 15 if smp else 63
              k.act(ebl.ap, pc.ap[:, 0:256].rearrange("p (a c t) -> p a c t", a=2, c=2)[:, :, :, lastc],
                    AF.Exp, [pc.b], [ebl.b], scale=-1.0 / 16.0)
              k.cp("act", gqk.ap, z1.ap, [z1.b], [gqk.b])
              for i in range(4):
                  k.tr(pT2.ap[:, i * 128:(i + 1) * 128], gqk.ap[:, i * 128:(i + 1) * 128], ident_b.ap,
                       [gqk.b, ident_b.b], [pT2.b])
              k.tt("dve", qeT.ap, pT2.ap[:, 0:256].rearrange("p (a t) -> p a t", a=2), ebT.ap, ALU.mult,
                   [pT2.b, ebT.b], [qeT.b])
              k.tt("dve", keT.ap, pT2.ap[:, 256:512].rearrange("p (a t) -> p a t", a=2), enbT.ap, ALU.mult,
                   [pT2.b, enbT.b], [keT.b])
              k.tt("dve", ke.ap, z1.ap[:, 256:512], enb.ap, ALU.mult, [z1.b, enb.b], [ke.b])
              k.cp("pool" if False else "dve", gv.ap, z2.ap[:, 0:256], [z2.b], [gv.b])
              k.act(e2.ap, z2.ap[:, 256:512], AF.Exp, [z2.b], [e2.b], scale=-1.0)
              k.ts("dve", e2.ap, e2.ap, 1.0, None, ALU.add, None, [e2.b], [e2.b])
              k.P.op("dve", lambda e: e.reciprocal(e2.ap, e2.ap), reads=[e2.b], writes=[e2.b])
              k.tt("dve", gate.ap, z2.ap[:, 256:512], e2.ap, ALU.mult, [z2.b, e2.b], [gate.b])
              adv()
              for h in range(4):
                  p, hb = h // 2, (h % 2) * 64
                  pa = pM[2 + (h % 2)]
                  k.mm(pa.ap[:, p * 128:(p + 1) * 128], keT.ap[hb:hb + 64, p, :], qeT.ap[hb:hb + 64, p, :], True, True,
                       [keT.b, qeT.b], [pa.b])
              if "atmask" not in os.environ.get("KSKIP", ""):
                  for hh in range(2):
                      k.tt("dve", ATm.ap[:, hh, :, :], pM[2 + hh].ap[:, 0:256].rearrange("p (a t) -> p a t", a=2),
                           bc_mid(U_f.ap, 2), ALU.mult, [pM[2 + hh].b, U_f.b], [ATm.b], accum=(hh == 1))
              po = pM[3]
              for c in range(2):
                  r0 = c * 64
                  nreal = 16 if smp else 64
                  if smp:
                      seq = idx * 2 + c
                      for hh in range(2):
                          k.dma("sp", S_f.ap[hh * 64:(hh + 1) * 64, :, :],
                                st_gla[l, seq, :, :, :].rearrange("(p hh) d v -> hh d p v", hh=2)[hh],
                                [], [S_f.b], accum=(hh == 1))
                      k.cp("dve", S_b.ap, S_f.ap, [S_f.b], [S_b.b])
                  for h in range(4):
                      p, hb = h // 2, (h % 2) * 64
                      k.mm(po.ap[r0:r0 + 64, h * 64:(h + 1) * 64], ATm.ap[:, h % 2, h // 2, r0:r0 + 64], gv.ap[:, h * 64:(h + 1) * 64],
                           True, False, [ATm.b, gv.b], [po.b])
                      k.mm(po.ap[r0:r0 + 64, h * 64:(h + 1) * 64], qeT.ap[hb:hb + 64, p, r0:r0 + 64],
                           S_b.ap[hb:hb + 64, p, :], False, True, [qeT.b, S_b.b], [po.b])
                  for h in range(4):
                      p, hb = h // 2, (h % 2) * 64
                      k.mm(pc.ap[hb:hb + 64, 256 + p * 64:256 + (p + 1) * 64], ke.ap[r0:r0 + nreal, h * 64:(h + 1) * 64],
                           gv.ap[r0:r0 + nreal, h * 64:(h + 1) * 64], True, True, [ke.b, gv.b], [pc.b])
                  k.tt("dve", S_t.ap, pc.ap[:, 256:384].rearrange("p (a v) -> p a v", a=2), S_f.ap, ALU.add,
                       [pc.b, S_f.b], [S_t.b])
                  k.tt("dve", S_f.ap, S_t.ap, bc_last(ebl.ap[:, :, c], 64), ALU.mult, [S_t.b, ebl.b], [S_f.b])
                  k.cp("dve", S_b.ap, S_f.ap, [S_f.b], [S_b.b])
                  if smp:
                      seq = idx * 2 + c
                      for hh in range(2):
                          k.dma("sp", s_gla[l, seq, :, :, :].rearrange("(p hh) d v -> hh d p v", hh=2)[hh],
                                S_f.ap[hh * 64:(hh + 1) * 64, :, :], [S_f.b], [])
              if (not smp) and idx == NB - 1:
                  for hh in range(2):
                      k.dma("sp", p_gla[l, :, :, :].rearrange("(p hh) d v -> hh d p v", hh=2)[hh],
                            S_f.ap[hh * 64:(hh + 1) * 64, :, :], [S_f.b], [])
              adv()
              k.cp("act", o_sb.ap, po.ap[:, 0:256].rearrange("p (h v) -> p h v", h=4), [po.b], [o_sb.b])
              k.tt("dve", o_sq.ap, o_sb.ap, o_sb.ap, ALU.mult, [o_sb.b], [o_sq.b])
              k.P.op("dve", lambda e: e.tensor_reduce(o_ss.ap, o_sq.ap, AX.X, ALU.add), reads=[o_sq.b], writes=[o_ss.b])
              k.rsqrt(o_r.ap, o_ss.ap, 64.0 * EPS, TL2(lnt, 4, 8), [o_ss.b], [o_r.b])
              k.tt("dve", o_sq.ap, o_sb.ap, bc_last(o_r.ap, 64), ALU.mult, [o_sb.b, o_r.b], [o_sq.b])
              k.tt("dve", o_sb.ap.rearrange("p h v -> p (h v)"), o_sq.ap.rearrange("p h v -> p (h v)"), gn_bc.ap,
                   ALU.mult, [o_sq.b, gn_bc.b], [o_sb.b])
              mg = TL(merged_s.ap[:, idx, :]) if smp else merged
              if smp:
                  mg.b = merged_s.b
              k.tt("dve", mg.ap[:, 384:640], o_sb.ap.rearrange("p h v -> p (h v)"), gate.ap, ALU.mult,
                   [o_sb.b, gate.b], [mg.b], accum=True)
              adv()
              slot = idx % 5 if not smp else 0
              Vr = Vring[vslot]
              Kr = Kring[slot]
              for p in range(3):
                  k.tr(pT2.ap[:, p * 128:(p + 1) * 128], cqb.ap[:, p * 128:(p + 1) * 128], ident_b.ap,
                       [cqb.b, ident_b.b], [pT2.b])
                  k.tr(pT2.ap[:, 384 + p * 128:512 + p * 128], ckb.ap[:, p * 128:(p + 1) * 128], ident_b.ap,
                       [ckb.b], [pT2.b])
              k.cp("act", cqT.ap, pT2.ap[:, 0:384].rearrange("p (a t) -> p a t", a=3), [pT2.b], [cqT.b])
              k.cp("dve", Kr.ap, pT2.ap[:, 384:768].rearrange("p (a t) -> p a t", a=3), [pT2.b], [Kr.b])
              if not smp:
                  b = idx
                  wins = [w for w in range(5) if b - 4 + w >= 0]
                  pov = pT2f
                  w0 = wins[0]

                  def ca_S(h):
                      p, hb = h // 2, (h % 2) * 64
                      pca = pCA[h % 2]
                      for w in wins:
                          kb = Kring[(b - 4 + w) % 5]
                          k.mm(pca.ap[:, w * 128:(w + 1) * 128], kb.ap[hb:hb + 64, p, :], cqT.ap[hb:hb + 64, p, :],
                               True, True, [kb.b, cqT.b], [pca.b, pM[1 + 2 * (h % 2)].b])

                  def ca_E_PV(h):
                      pca = pCA[h % 2]
                      pt = PT[h % 2]
                      k.act(pt.ap[:, w0:5, :], pca.ap[:, w0 * 128:640].rearrange("p (w t) -> p w t", t=128), AF.Exp,
                            [pca.b, pM[1 + 2 * (h % 2)].b], [pt.b])
                      k.tt("dve", pt.ap[:, w0:5, :], pt.ap[:, w0:5, :],
                           BT.ap[:, h, w0 * 128:640].rearrange("p (w t) -> p w t", t=128), ALU.mult,
                           [pt.b, BT.b], [pt.b])
                      for w in wins:
                          vb = Vring[(b - 4 + w) % 6]
                          k.mm(pov.ap[:, h * 65:(h + 1) * 65], pt.ap[:, w, :], vb.ap[:, h, :], w == w0, w == 4,
                               [pt.b, vb.b], [pov.b])

                  ca_S(0)
                  for h in range(6):
                      adv()
                      if h + 1 < 6:
                          ca_S(h + 1)
                      ca_E_PV(h)
                  k.P.op("dve", lambda e: e.reciprocal(rden.ap, pov.ap[:, 0:390].rearrange("p (h e) -> p h e", h=6)[:, :, 64]),
                         reads=[pov.b], writes=[rden.b])
                  k.tt("dve", merged.ap[:, 640:1024].rearrange("p (h e) -> p h e", h=6),
                       pov.ap[:, 0:390].rearrange("p (h e) -> p h e", h=6)[:, :, 0:64], bc_last(rden.ap, 64), ALU.mult,
                       [pov.b, rden.b], [merged.b], accum=True)
                  k.dma("sp", MERGED[b * 128:(b + 1) * 128, 384:1024], merged.ap[:, 384:1024], [merged.b], [])
              else:
                  sample_mixers(idx, Kr, Vr)

          def sample_mixers(t, Kr, Vr):
              for j in range(2):
                  seq = 2 * t + j
                  r0 = 64 * j
                  k.dma("pool", sc_cak.ap, c_cak[l, seq, :, :].rearrange("(w p) c -> p w c", p=128), [], [sc_cak.b])
                  for w in range(4):
                      k.dma("pool", sc_caV.ap[:, w, :, 0:64],
                            c_cav[l, seq, w * 128:(w + 1) * 128, :].rearrange("p (h e) -> p h e", h=6), [], [sc_caV.b],
                            accum=True)
                  for w in range(4):
                      for p in range(3):
                          k.tr(pT2.ap[:, p * 128:(p + 1) * 128], sc_cak.ap[:, w, p * 128:(p + 1) * 128], ident_b.ap,
                               [sc_cak.b, ident_b.b], [pT2.b])
                      k.cp("act", sc_caKT.ap[:, :, w * 128:(w + 1) * 128],
                           pT2.ap[:, 0:384].rearrange("p (a t) -> p a t", a=3), [pT2.b], [sc_caKT.b], accum=True)
                  pssb = [pM[0], pM[3]]
                  pso = pM[1]
                  svb = [pb_.ap[:, 0:240].rearrange("p (h w q) -> p h w q", h=3, w=5) for pb_ in pssb]
                  for h in range(6):
                      p, hb = h // 2, (h % 2) * 64
                      sv = svb[h % 2]
                      pb_ = pssb[h % 2]
                      for w in range(4):
                          k.mm(sv[:, p, w, :], sc_caKT.ap[hb:hb + 64, p, w * 128:(w + 1) * 128],
                               cqT.ap[hb:hb + 64, p, r0:r0 + 16], True, False, [sc_caKT.b, cqT.b], [pb_.b])
                          k.mm(sv[:, p, w, :], BTs.ap[hb:hb + 16, h, w * 128:(w + 1) * 128], J16[hb], False, True,
                               [BTs.b, J_b.b], [pb_.b])
                      k.mm(sv[r0:r0 + 16, p, 4, :], Kr.ap[hb:hb + 64, p, r0:r0 + 16], cqT.ap[hb:hb + 64, p, r0:r0 + 16],
                           True, False, [Kr.b, cqT.b], [pb_.b])
                      k.mm(sv[r0:r0 + 16, p, 4, :], BTs.ap[hb:hb + 16, h, 512:528], J16[hb], False, True,
                           [BTs.b, J_b.b], [pb_.b])
                  for par in range(2):
                      pt = sPT.ap[:, 3 * par:3 * par + 3, 0:5, :]
                      k.act(pt[:, :, 0:4, :], svb[par][:, :, 0:4, :], AF.Exp, [pssb[par].b], [sPT.b], accum=(par > 0))
                      k.act(pt[r0:r0 + 16, :, 4, :], svb[par][r0:r0 + 16, :, 4, :], AF.Exp, [pssb[par].b], [sPT.b],
                            accum=True)
                  ov = pso.ap[:, 0:390].rearrange("p (h e) -> p h e", h=6)
                  for h in range(6):
                      s_ = (h % 2) * 3 + h // 2
                      for w in range(4):
                          k.mm(ov[r0:r0 + 16, h, :], sPT.ap[:, s_, w, :], sc_caV.ap[:, w, h, :], w == 0, False,
                               [sPT.b, sc_caV.b], [pso.b])
                      k.mm(ov[r0:r0 + 16, h, :], sPT.ap[r0:r0 + 16, s_, 4, :], Vr.ap[r0:r0 + 16, h, :], False, True,
                           [sPT.b, Vr.b], [pso.b])
                  k.P.op("dve", lambda e, ov=ov, r0=r0: e.reciprocal(rden.ap[r0:r0 + 16, :], ov[r0:r0 + 16, :, 64]),
                         reads=[pso.b], writes=[rden.b])
                  k.tt("dve", merged_s.ap[r0:r0 + 16, t, 640:1024].rearrange("p (h e) -> p h e", h=6),
                       ov[r0:r0 + 16, :, 0:64], bc_last(rden.ap[r0:r0 + 16, :], 64), ALU.mult,
                       [pso.b, rden.b], [merged_s.b], accum=True)
                  k.dma("pool", sc_ckv.ap, c_ckv[l, seq, :, :].rearrange("(w p) c -> p w c", p=128), [], [sc_ckv.b])
                  for r in range(2):
                      k.dma("pool", sc_kr3.ap[:, :, 64 * r:64 * r + 32],
                            c_kr[l, seq, :, :].rearrange("(w p) c -> p w c", p=128), [], [sc_kr3.b], accum=True)
                  for w in range(8):
                      k.tr(pT1.ap[:, w * 128:(w + 1) * 128], sc_ckv.ap[:, w, :], ident_b.ap, [sc_ckv.b, ident_b.b],
                           [pT1.b])
                  k.cp("act", sc_ckvT.ap, pT1.ap, [pT1.b], [sc_ckvT.b])
                  for w in range(8):
                      k.tr(pT2.ap[:, w * 128:(w + 1) * 128], sc_kr3.ap[:, w, :],
                           ident_b.ap, [sc_kr3.b, ident_b.b], [pT2.b])
                  k.cp("dve", sc_KR.ap, pT2.ap, [pT2.b], [sc_KR.b])
                  for p in range(3):
                      for hf in range(2):
                          pk = pM[2 + hf]
                          k.mm(pk.ap, W_kn.ap[:, 0, p * 128:(p + 1) * 128], sc_ckvT.ap[:, hf * 512:(hf + 1) * 512], True,
                               True, [W_kn.b, sc_ckvT.b], [pk.b])
                          k.cp("act" if hf else "dve", sc_KN.ap[:, p, hf * 512:(hf + 1) * 512], pk.ap, [pk.b], [sc_KN.b],
                               accum=True)
                  for w in range(8):
                      pk = pM[2 + w % 2]
                      k.mm(pk.ap[:, 0:384], sc_ckvT.ap[:, w * 128:(w + 1) * 128], W_v.ap[:, 0, :], True, True,
                           [sc_ckvT.b, W_v.b], [pk.b])
                      k.cp("act" if w % 2 else "dve", sc_V.ap[:, w, :, 0:64],
                           pk.ap[:, 0:384].rearrange("p (h e) -> p h e", h=6), [pk.b], [sc_V.b], accum=True)
                  pss2 = [pM[0], pM[1]]
                  sv2b = [pb_.ap[:, 0:432].rearrange("p (h w q) -> p h w q", h=3, w=9) for pb_ in pss2]
                  for h in range(6):
                      p, hb = h // 2, (h % 2) * 64
                      pb = pss2[h % 2]
                      sv2 = sv2b[h % 2]
                      for w in range(9):
                          if w < 8:
                              out = sv2[:, p, w, :]
                              kn = sc_KN.ap[hb:hb + 64, p, w * 128:(w + 1) * 128]
                              kr_ = sc_KR.ap[hb:hb + 32, w * 128:(w + 1) * 128]
                              rd = [sc_KN.b, sc_KR.b]
                          else:
                              out = sv2[r0:r0 + 16, p, 8, :]
                              kn = KNs.ap[hb:hb + 64, p, r0:r0 + 16]
                              kr_ = KRs.ap[hb:hb + 32, r0:r0 + 16]
                              rd = [KNs.b, KRs.b]
                          k.mm(out, kn, QNs.ap[hb:hb + 64, p, r0:r0 + 16], True, False, rd + [QNs.b], [pb.b])
                          k.mm(out, kr_, QRs.ap[hb:hb + 32, p, r0:r0 + 16], False, True, rd + [QRs.b], [pb.b])
                  sc = 96.0 ** -0.5
                  for par in range(2):
                      pb = pss2[par]
                      sv2 = sv2b[par]
                      k.act(sPT.ap[:, 3 * par:3 * par + 3, 0:8, :], sv2[:, :, 0:8, :], AF.Exp, [pb.b], [sPT.b],
                            scale=sc, accum=(par > 0))
                      k.act(sPT.ap[r0:r0 + 16, 3 * par:3 * par + 3, 8, :], sv2[r0:r0 + 16, :, 8, :], AF.Exp, [pb.b],
                            [sPT.b], scale=sc, accum=True)
                  pso2 = pM[2]
                  ov2 = pso2.ap[:, 0:390].rearrange("p (h e) -> p h e", h=6)
                  for h in range(6):
                      s_ = (h % 2) * 3 + h // 2
                      for w in range(8):
                          k.mm(ov2[r0:r0 + 16, h, :], sPT.ap[:, s_, w, :], sc_V.ap[:, w, h, :], w == 0, False,
                               [sPT.b, sc_V.b], [pso2.b])
                      k.mm(ov2[r0:r0 + 16, h, :], sPT.ap[r0:r0 + 16, s_, 8, :], Vaug.ap[r0:r0 + 16, h, :], False, True,
                           [sPT.b, Vaug.b], [pso2.b])
                  k.P.op("dve", lambda e, ov2=ov2, r0=r0: e.reciprocal(rden.ap[r0:r0 + 16, :], ov2[r0:r0 + 16, :, 64]),
                         reads=[pso2.b], writes=[rden.b])
                  k.tt("dve", merged_s.ap[r0:r0 + 16, t, 0:384].rearrange("p (h e) -> p h e", h=6),
                       ov2[r0:r0 + 16, :, 0:64], bc_last(rden.ap[r0:r0 + 16, :], 64), ALU.mult,
                       [pso2.b, rden.b], [merged_s.b], accum=True)

          seq = [("p", b) for b in range(NB)] + [("s", 0), ("s", 1)]

          def load_X(i):
              kind, idx = seq[i]
              if kind == "p":
                  k.dma("sp", Xt[idx % 2].ap, x_src[idx * 128:(idx + 1) * 128, :], [], [Xt[idx % 2].b])
                  k.dma("sp", cs_blk[idx % 2].ap, cs_p[idx * 128:(idx + 1) * 128, :], [], [cs_blk[idx % 2].b])

          load_X(0)
          for _ in front_gen(seq[0][0], seq[0][1], 0, 0):
              pass
          for i, (kind, idx) in enumerate(seq):
              nxt = None
              if i + 1 < len(seq):
                  load_X(i + 1)
                  nxt = front_gen(seq[i + 1][0], seq[i + 1][1], (i + 1) % 2, (i + 1) % 6)

              def adv(nxt=nxt):
                  if nxt is not None:
                      next(nxt, None)
              a1_block(kind, idx, i % 2, i % 6, adv)
              if nxt is not None:
                  for _ in nxt:
                      pass
          P.barrier()

          A.release(persist_mark)
          NG = T // 512
          KTr = sb([128, 3, T], BF16)
          VAr = sb([128, NB, 384], BF16)
          kvb = [Buf() for _ in range(NG)]
          Qg = [sb([128, 3, 512], BF16) for _ in range(2)]
          oT = [sb([65, 512], F32) for _ in range(2)]
          om = [sb([128, 4, 192], BF16) for _ in range(2)]
          mfull = [sb([128, 4, D], BF16) for _ in range(2)]
          rd2 = sb([128, 4], F32)
          W_o = sb([128, 8, 1024], BF16)
          load_weight(W_o.ap, W_o.b, 8, 1024, lambda c, s0, ln: w_out[l, c * 128:(c + 1) * 128, s0:s0 + ln], None,
                      engs=("pool", "dve", "act"))
          Xq = [sb([128, D], F32) for _ in range(4)]
          mT = sb([128, 8, 128], BF16)
          PTt = [sb([128, 1024], BF16) for _ in range(4)]
          Sreg = []
          for a_ in (0, 2, 4):
              t_ = TL(ps_t[:, 512 * a_:512 * a_ + 1024])
              t_.b.locks = (bank_locks[a_], bank_locks[a_ + 1])
              Sreg.append(t_)
          pO = [TLP(6)]
          pTr = TLP(7)
          pTm = TLP(7, True)
          pWo = TLP(7)
          sc = 96.0 ** -0.5

          def wo_block(mparts, xv, xb, xreads):
            c = 0
            for (ap, buf) in mparts:
                w = ap.shape[1]
                for cc in range(w // 128):
                    k.tr(pTm.ap[:, c * 128:(c + 1) * 128], ap[:, cc * 128:(cc + 1) * 128], ident_b.ap,
                         [buf, ident_b.b], [pTm.b])
                    c += 1
            assert c == 8
            k.cp("dve", mT.ap, pTm.ap.rearrange("p (c t) -> p c t", c=8), [pTm.b], [mT.b])
            for n in range(2):
                for c in range(8):
                    k.mm(pWo.ap, mT.ap[:, c, :], W_o.ap[:, c, n * 512:(n + 1) * 512], c == 0, c == 7,
                         [mT.b, W_o.b], [pWo.b])
                k.tt("dve", xv[:, n * 512:(n + 1) * 512], pWo.ap, xv[:, n * 512:(n + 1) * 512], ALU.add,
                     [pWo.b, xb] + xreads, [xb], accum=True)

          chk("a2w")
          hi = [0]
          xq_i2 = [0]
          ui = [0]
          for hp in range(2):
              def a2_load(g, hp=hp):
                  t0 = g * 512
                  k.dma("sp", KTr.ap[:, :, t0:t0 + 512], KT[3 * hp:3 * hp + 3, :, t0:t0 + 512].rearrange("a p t -> p a t"),
                        [], [kvb[g]])
                  k.dma("sp", VAr.ap[:, 4 * g:4 * g + 4, :],
                        VA[t0:t0 + 512, 384 * hp:384 * hp + 384].rearrange("(w p) c -> p w c", p=128), [], [kvb[g]],
                        accum=True)
                  k.dma("sp", Qg[g % 2].ap, QT[3 * hp:3 * hp + 3, :, t0:t0 + 512].rearrange("a p t -> p a t"), [],
                        [Qg[g % 2].b])
                  if hp == 1:
                      mf = mfull[g % 2]
                      k.dma("sp", mf.ap[:, :, 0:192], MERGED[t0:t0 + 512, 0:192].rearrange("(a p) c -> p a c", p=128),
                            [], [mf.b])
                      k.dma("sp", mf.ap[:, :, 384:1024],
                            MERGED[t0:t0 + 512, 384:1024].rearrange("(a p) c -> p a c", p=128), [], [mf.b], accum=True)

              units = []
              for g in range(NG):
                  for hl in range(3):
                      us = []
                      for kt in range(0, 4 * g, 2):
                          us.append([g, hl, [kt, kt + 1], 0, False, False])
                      for j in range(4):
                          us.append([g, hl, [4 * g + j], 128 * j, False, False])
                      us[0][4] = True
                      us[-1][5] = True
                      units.extend(us)

              def emitS(i, units=units):
                  g, hl, kts, c0, first, last = units[i]
                  sr = Sreg[(ui[0] + i) % 3]
                  qg = Qg[g % 2]
                  for n_, kt in enumerate(kts):
                      k.mm(sr.ap[:, n_ * 512 + c0:(n_ + 1) * 512], KTr.ap[:, hl, kt * 128:(kt + 1) * 128],
                           qg.ap[:, hl, c0:512], True, True, [kvb[kt // 4], qg.b], [sr.b])

              def emitE_PV(i, units=units, hp=hp):
                  g, hl, kts, c0, first, last = units[i]
                  sr = Sreg[(ui[0] + i) % 3]
                  pt = PTt[(ui[0] + i) % 4]
                  n = len(kts)
                  k.act(pt.ap[:, c0:n * 512], sr.ap[:, c0:n * 512], AF.Exp, [sr.b], [pt.b], scale=sc)
                  if kts[0] >= 4 * g:
                      k.memset("pool", pt.ap[64:128, c0:c0 + 64], 0.0, [pt.b])
                  po = pO[0]
                  for n_, kt in enumerate(kts):
                      k.mm(po.ap[:, c0:512], VAr.ap[:, kt, hl * 128:(hl + 1) * 128],
                           pt.ap[:, n_ * 512 + c0:(n_ + 1) * 512],
                           first and n_ == 0, last and n_ == n - 1, [kvb[kt // 4], pt.b], [po.b])
                  if last:
                      ot = oT[hi[0] % 2]
                      hi[0] += 1
                      k.cp("dve", ot.ap, po.ap[0:65, :], [po.b], [ot.b])
                      for q4 in range(4):
                          k.tr(pTr.ap[:, q4 * 65:(q4 + 1) * 65], ot.ap[:, q4 * 128:(q4 + 1) * 128], ident_f.ap[0:65, 0:65],
                               [ot.b, ident_f.b], [pTr.b])
                      tv = pTr.ap[:, 0:260].rearrange("p (a e) -> p a e", a=4)
                      k.P.op("dve", lambda e, tv=tv: e.reciprocal(rd2.ap, tv[:, :, 64]), reads=[pTr.b], writes=[rd2.b])
                      if hp == 0:
                          dst, dbuf = om[g % 2].ap[:, :, hl * 64:(hl + 1) * 64], om[g % 2].b
                      else:
                          dst, dbuf = mfull[g % 2].ap[:, :, 192 + hl * 64:192 + (hl + 1) * 64], mfull[g % 2].b
                      k.tt("dve", dst, tv[:, :, 0:64], bc_last(rd2.ap, 64), ALU.mult, [pTr.b, rd2.b], [dbuf], accum=True)
                      if hl == 2 and hp == 0:
                          k.dma("sp", MERGED[g * 512:(g + 1) * 512, 0:192].rearrange("(a p) c -> p a c", p=128),
                                om[g % 2].ap, [om[g % 2].b], [])
                      if hl == 2 and hp == 1:
                          mf = mfull[g % 2]
                          if dbg:
                              k.dma("sp", MERGED[g * 512:(g + 1) * 512, 192:384].rearrange("(a p) c -> p a c", p=128),
                                    mf.ap[:, :, 192:384], [mf.b], [])
                          for q4 in range(4):
                              xq = Xq[q4]
                              r0 = g * 512 + q4 * 128
                              wo_block([(mf.ap[:, q4, :], mf.b)], xq.ap, xq.b, [])
                              k.dma("sp", XB[r0:r0 + 128, :], xq.ap, [xq.b], [])

              a2_load(0)
              if NG > 1:
                  a2_load(1)
              emitS(0)
              if len(units) > 1:
                  emitS(1)
              for i in range(len(units)):
                  g, hl, kts, c0, first, last = units[i]
                  if first and hl == 0 and g >= 1 and g + 1 < NG:
                      a2_load(g + 1)
                  if hp == 1 and first and hl == 2:
                      for q4 in range(4):
                          r0_ = g * 512 + q4 * 128
                          k.dma("sp", Xq[q4].ap, x_src[r0_:r0_ + 128, :], [], [Xq[q4].b])
                  if i + 2 < len(units):
                      emitS(i + 2)
                  emitE_PV(i)
              ui[0] += len(units)
              P.barrier()
          chk("a2p")
          for t in range(2):
              wo_block([(merged_s.ap[:, t, :], merged_s.b)], x_s.ap[:, t, :], x_s.b, [])
          P.barrier()
          chk("a2")

          A.release(persist_mark)
          W_u = sb([128, 8, 4096], BF16)
          W_d = sb([128, 32, 1024], BF16)
          n2T = sb([128, 8], F32)
          k.dma("sp", n2T.ap, norm2T[l, :, :], [], [n2T.b])
          Xg = [sb([128, 2, D], F32) for _ in range(2)]
          bst = list(stage) + [TL(Xg[i_].ap[:, s_, :]) for i_ in range(2) for s_ in range(2)]
          load_weight(W_u.ap, W_u.b, 8, 4096, lambda c, s0, ln: w_up[l, c * 128:(c + 1) * 128, s0:s0 + ln],
                      lambda c: (n2T.ap[:, c:c + 1], n2T.b), stages=bst, engs=("pool", "dve", "act"))
          load_weight(W_d.ap, W_d.b, 32, 1024, lambda c, s0, ln: w_down[l, c * 128:(c + 1) * 128, s0:s0 + ln], None,
                      stages=bst, engs=("pool", "dve", "act"))
          P.barrier()
          xn2 = sb([128, D], BF16)
          hT2 = sb([128, 8, 256], BF16)
          rl = [sb([128, 256], BF16) for _ in range(2)]
          uT = sb([128, 32, 256], BF16)
          junk2 = sb([128, D], BF16)
          ssb = sb([128, 4], F32)
          rsb = sb([128, 4], F32)
          lnt2 = sb([128, 4], F32)
          if last_layer:
              yo = sb([128, D], F32)
              fn_bc = sb([128, D], F32)
              k.dma("sp", fn_bc.ap, fnorm.ap().partition_broadcast(128), [], [fn_bc.b])
          pTb = [TLP(0, True), TLP(1, True)]
          pW = [TLP(2), TLP(3)]
          pD = [TLP(4), TLP(5), TLP(6), TLP(7)]
          NGB = T // 256

          def b_load(g):
              k.dma("sp", Xg[g % 2].ap, XB[g * 256:(g + 1) * 256, :].rearrange("(s p) c -> p s c", p=128), [],
                    [Xg[g % 2].b])

          wi = [0]
          di = [0]

          def b_group(Xv, Xb, out_fn):
              for s in range(2):
                  k.act(junk2.ap, Xv[:, s, :], AF.Square, [Xb], [junk2.b, ssb.b], scale=1.0 / 32.0,
                        accum_out=ssb.ap[:, s:s + 1])
                  k.rsqrt(rsb.ap[:, s:s + 1], ssb.ap[:, s:s + 1], EPS, TL2(lnt2, s, s + 1), [ssb.b], [rsb.b])
                  k.ts("dve", xn2.ap, Xv[:, s, :], rsb.ap[:, s:s + 1], None, ALU.mult, None, [Xb, rsb.b], [xn2.b])
                  pt = pTb[s]
                  for c in range(8):
                      k.tr(pt.ap[:, c * 128:(c + 1) * 128], xn2.ap[:, c * 128:(c + 1) * 128], ident_b.ap,
                           [xn2.b, ident_b.b], [pt.b])
                  k.cp("act", hT2.ap[:, :, s * 128:(s + 1) * 128], pt.ap.rearrange("p (c t) -> p c t", c=8), [pt.b],
                       [hT2.b], accum=True)
              for f in range(32):
                  pw = pW[wi[0] % 2]
                  wi[0] += 1
                  for c in range(8):
                      k.mm(pw.ap[:, 0:256], W_u.ap[:, c, f * 128:(f + 1) * 128], hT2.ap[:, c, :], c == 0, c == 7,
                           [W_u.b, hT2.b], [pw.b])
                  r = rl[f % 2]
                  k.act(r.ap, pw.ap[:, 0:256], AF.Relu, [pw.b], [r.b])
                  k.tt("dve", uT.ap[:, f, :], r.ap, r.ap, ALU.mult, [r.b], [uT.b], accum=True)
              for s in range(2):
                  for n in range(2):
                      pd = pD[di[0] % 4]
                      di[0] += 1
                      for f in range(32):
                          k.mm(pd.ap, uT.ap[:, f, s * 128:(s + 1) * 128], W_d.ap[:, f, n * 512:(n + 1) * 512], f == 0,
                               f == 31, [uT.b, W_d.b], [pd.b])
                      k.tt("dve", Xv[:, s, n * 512:(n + 1) * 512], pd.ap, Xv[:, s, n * 512:(n + 1) * 512], ALU.add,
                           [pd.b, Xb], [Xb], accum=True)
              for s in range(2):
                  if last_layer:
                      k.act(junk2.ap, Xv[:, s, :], AF.Square, [Xb], [junk2.b, ssb.b], scale=1.0 / 32.0,
                            accum_out=ssb.ap[:, 2 + s:3 + s])
                      k.rsqrt(rsb.ap[:, 2 + s:3 + s], ssb.ap[:, 2 + s:3 + s], EPS, TL2(lnt2, 2 + s, 3 + s), [ssb.b], [rsb.b])
                      k.stt("dve", yo.ap, Xv[:, s, :], rsb.ap[:, 2 + s:3 + s], fn_bc.ap, ALU.mult, ALU.mult,
                            [Xb, rsb.b, fn_bc.b], [yo.b])
                      out_fn(s, yo.ap, yo.b, True)
                  else:
                      out_fn(s, Xv[:, s, :], Xb, False)

          b_load(0)
          for g in range(NGB):
              if g + 1 < NGB:
                  b_load(g + 1)

              def outp(s, ap, buf, final, g=g):
                  dst = y_p if final else XB
                  k.dma("sp", dst[g * 256 + s * 128:g * 256 + (s + 1) * 128, :], ap, [buf], [])
              b_group(Xg[g % 2].ap, Xg[g % 2].b, outp)

          def outs(s, ap, buf, final):
              if final:
                  k.dma("sp", y_s[s * 128:(s + 1) * 128, :], ap, [buf], [])
          b_group(x_s.ap, x_s.b, outs)
          P.barrier()


    except StopBuild:
        pass
    P.finalize()
    P.emit()
    es.close()
    return k


def host_inputs(T, L, core, x_prompt, x_sample, cache_mla_ckv, cache_mla_krope, state_gla, cache_ca_k, cache_ca_v,
                norm1, w_in, mla_q_norm, mla_w_qup, mla_kv_norm, mla_w_kvup, gla_w_gate2, gla_gate_bias,
                gla_out_norm, ca_rel_bias, w_out, norm2, w_up, w_down, final_norm, consts):
    f = lambda a: np.ascontiguousarray(np.asarray(a, dtype=np.float32))
    c = core
    sq = slice(4 * c, 4 * c + 4)
    xs = np.zeros((2, 2, 64, D), np.float32)
    xs[:, :, 0:16, :] = np.asarray(x_sample[sq]).reshape(2, 2, 16, D)
    m = {
        "xp": f(x_prompt[c % x_prompt.shape[0]][:T]),
        "xs": f(xs.reshape(256, D)),
        "c_ckv": f(cache_mla_ckv[:L, sq]),
        "c_kr": f(cache_mla_krope[:L, sq]),
        "st_gla": f(state_gla[:L, sq]),
        "c_cak": f(np.asarray(cache_ca_k[:L, sq]).reshape(L, 4, 512, 384)),
        "c_cav": f(np.asarray(cache_ca_v[:L, sq]).reshape(L, 4, 512, 384)),
        "w_in": f(w_in[:L]), "w_qup": f(mla_w_qup[:L]), "w_kvup": f(mla_w_kvup[:L]),
        "w_gate2": f(gla_w_gate2[:L]), "gate_bias": f(gla_gate_bias[:L]),
        "w_out": f(w_out[:L]), "w_up": f(w_up[:L]), "w_down": f(w_down[:L]),
        "norm1T": f(np.asarray(norm1[:L]).reshape(L, 8, 128).transpose(0, 2, 1)),
        "norm2T": f(np.asarray(norm2[:L]).reshape(L, 8, 128).transpose(0, 2, 1)),
        "qnormT": f(np.asarray(mla_q_norm[:L]).reshape(L, 2, 128).transpose(0, 2, 1)),
        "kvnorm": f(mla_kv_norm[:L]), "glanorm": f(gla_out_norm[:L]), "fnorm": f(final_norm),
    }
    idx = np.clip(np.arange(767) - 639, -128, 128) + 128
    m["ext2"] = f(np.asarray(ca_rel_bias[:L])[:, idx, :].transpose(0, 2, 1))
    m.update(consts)
    return m


def host_consts(T):
    ident = np.eye(128, dtype=np.float32)
    J = np.ascontiguousarray(ident[::-1])
    j = np.arange(128)
    U = ((j[:, None] <= j[None, :]) & (j[:, None] // 64 == j[None, :] // 64)).astype(np.float32)
    inv = np.power(np.float32(10000.0), -np.arange(16, dtype=np.float32) / np.float32(16))

    def table(pos):
        ang = pos.astype(np.float32)[:, None] * inv[None, :].astype(np.float32)
        c, s = np.cos(ang).astype(np.float32), np.sin(ang).astype(np.float32)
        return np.ascontiguousarray(np.concatenate([c, c, -s, s], axis=1).astype(np.float32))
    cs_p = table(np.arange(T))
    cs_s = table(1024 + (np.arange(128) % 64))
    return {"c_ident": ident, "c_J": J, "c_U": U, "cs_p": cs_p, "cs_s": cs_s}


_CACHE = {}


def run(T, L, dbg, inputs):
    key = (T, L, dbg)
    if key not in _CACHE:
        _CACHE[key] = build_program(T, L, dbg)
    kk = _CACHE[key]
    consts = host_consts(T)
    in_maps = [host_inputs(T, L, c, consts=consts, **inputs) for c in range(NCORES)]
    res = run_bass_kernel_spmd(kk.nc, in_maps, core_ids=list(range(NCORES)))
    return res.results


def kernel(**inputs):
    T, L = SEQ, DEPTH
    R = run(T, L, False, inputs)
    B = 2
    y_prompt = np.stack([R[c]["y_p"] for c in range(B)])

    def samp(name, w):
        o = np.stack([R[c][name] for c in range(NCORES)])
        return o

    ys = np.stack([R[c]["y_s"] for c in range(NCORES)]).reshape(NCORES, 2, 2, 64, D)[:, :, :, 0:16, :]
    y_sample = ys.reshape(32, 16, D)
    p_ckv = np.stack([R[c]["p_ckv"] for c in range(B)], axis=1)
    p_kr = np.stack([R[c]["p_kr"] for c in range(B)], axis=1)
    p_gla = np.stack([R[c]["p_gla"] for c in range(B)], axis=1)
    p_cak = np.stack([R[c]["p_cak"] for c in range(B)], axis=1).reshape(L, B, 512, 6, 64)
    p_cav = np.stack([R[c]["p_cav"] for c in range(B)], axis=1).reshape(L, B, 512, 6, 64)

    def stile(name, w):
        o = np.stack([R[c][name] for c in range(NCORES)], axis=1)
        o = o.reshape(L, NCORES, 2, 2, 64, w)[:, :, :, :, 0:16, :]
        return np.ascontiguousarray(o.reshape(L, 32, 16, w))
    s_ckv = stile("s_ckv", 128)
    s_kr = stile("s_kr", 32)
    s_gla = np.stack([R[c]["s_gla"] for c in range(NCORES)], axis=1).reshape(L, 32, 4, 64, 64)
    s_cak = stile("s_cak", 384).reshape(L, 32, 16, 6, 64)
    s_cav = stile("s_cav", 384).reshape(L, 32, 16, 6, 64)
    outs = (y_prompt, y_sample, p_ckv, p_kr, p_gla, p_cak, p_cav, s_ckv, s_kr, s_gla, s_cak, s_cav)
    return tuple(np.ascontiguousarray(o.astype(np.float32)) for o in outs)
```
